# Optimizing a Trainium2 kernel written in Bass

```python
import math, functools
import jax, jax.numpy as jnp
from jax import lax
import numpy as np

D_MODEL = 1024
BATCH = 8
SEQ = 2048
DEPTH = 1
DEC_BATCH = 32
DEC_SEQ = 4
PAST_LEN = 16384
PAGE_SIZE = 128

SSD_D_INNER = 2 * D_MODEL
SSD_HEAD_DIM = 64
SSD_N_HEADS = SSD_D_INNER // SSD_HEAD_DIM
SSD_N_GROUPS = 8
SSD_D_STATE = 128
SSD_CONV_W = 4
SSD_CONV_DIM = SSD_D_INNER + 2 * SSD_N_GROUPS * SSD_D_STATE
SSD_CHUNK = 128
ATT_N_HEADS = 16
ATT_N_KV_HEADS = 4
ATT_HEAD_DIM = 64
IDX_N_HEADS = 8
IDX_HEAD_DIM = 64
TOPK_MAX = 256
Q_BLOCK = 128
MEM_LEN = 256
MEM_N_HEADS = 4
MEM_HEAD_DIM = D_MODEL // MEM_N_HEADS
FFN_HIDDEN = 2816
EPS = 1e-6

IN_SPLITS = (SSD_D_INNER, SSD_CONV_DIM, SSD_N_HEADS,
             ATT_N_HEADS * ATT_HEAD_DIM, ATT_N_KV_HEADS * ATT_HEAD_DIM, ATT_N_KV_HEADS * ATT_HEAD_DIM,
             IDX_N_HEADS * IDX_HEAD_DIM, IDX_HEAD_DIM, IDX_N_HEADS,
             D_MODEL, D_MODEL)
IN_DIM = sum(IN_SPLITS)
IN_OFFSETS = tuple(int(o) for o in np.cumsum(IN_SPLITS)[:-1])

kernel_name = 'hybrid_ssd_dsa_macaron_step'


def rmsnorm(x, g):
    xf = x.astype(jnp.float32)
    y = xf * lax.rsqrt(jnp.mean(xf * xf, axis=-1, keepdims=True) + EPS)
    return (y * g.astype(jnp.float32)).astype(x.dtype)


def swiglu(x, wg, wu, wd):
    return (jax.nn.silu(x @ wg) * (x @ wu)) @ wd


def gather_rows(a, idx):
    return jax.vmap(lambda ar, ix: ar[ix])(a, idx)


def causal_dwconv(xbc, buf, w, b):
    l = xbc.shape[1]
    xp = jnp.concatenate([buf.astype(xbc.dtype), xbc], axis=1)
    y = b
    for j in range(SSD_CONV_W):
        y = y + w[j] * xp[:, j:j + l]
    return jax.nn.silu(y), xp[:, l:]


def ssd_scan(x, dt, a, bm, cm, h0):
    b, l, h, p = x.shape
    g, n = bm.shape[2], bm.shape[3]
    r = h // g
    q = math.gcd(l, SSD_CHUNK)
    c = l // q
    xdt = (x * dt[..., None]).reshape(b, c, q, g, r, p)
    a_cs = jnp.cumsum((dt * a).reshape(b, c, q, g, r), axis=2)
    bc = bm.reshape(b, c, q, g, n)
    cc = cm.reshape(b, c, q, g, n)
    causal = jnp.tril(jnp.ones((q, q), dtype=bool))[:, :, None, None]
    seg = a_cs[:, :, :, None] - a_cs[:, :, None, :]
    decay = jnp.exp(jnp.where(causal, seg, -jnp.inf))
    cb = jnp.einsum('bctgn,bcsgn->bctsg', cc, bc)
    y_diag = jnp.einsum('bctsgr,bcsgrp->bctgrp', cb[..., None] * decay, xdt)
    to_end = jnp.exp(a_cs[:, :, -1:] - a_cs)
    chunk_states = jnp.einsum('bcsgn,bcsgrp->bcgrpn', bc, xdt * to_end[..., None])
    chunk_decay = jnp.exp(a_cs[:, :, -1])

    def step(state, inp):
        s_c, d_c = inp
        return state * d_c[..., None, None] + s_c, state

    h_fin, h_in = lax.scan(step, h0.reshape(b, g, r, p, n),
                           (jnp.moveaxis(chunk_states, 1, 0), jnp.moveaxis(chunk_decay, 1, 0)))
    h_in = jnp.moveaxis(h_in, 0, 1)
    y_off = jnp.einsum('bctgn,bcgrpn->bctgrp', cc, h_in) * jnp.exp(a_cs)[..., None]
    y = (y_diag + y_off).reshape(b, l, h, p)
    return y, h_fin.reshape(b, h, p, n)


def ssd_branch(z, xbc, dt_raw, conv_buf, h0, conv_w, conv_b, dt_bias, a_log, d_skip, norm_g):
    f32 = jnp.float32
    b, l, _ = z.shape
    xbc_act, new_buf = causal_dwconv(xbc, conv_buf, conv_w, conv_b)
    xs, bm, cm = jnp.split(xbc_act, [SSD_D_INNER, SSD_D_INNER + SSD_N_GROUPS * SSD_D_STATE], axis=-1)
    xs = xs.reshape(b, l, SSD_N_HEADS, SSD_HEAD_DIM).astype(f32)
    bm = bm.reshape(b, l, SSD_N_GROUPS, SSD_D_STATE).astype(f32)
    cm = cm.reshape(b, l, SSD_N_GROUPS, SSD_D_STATE).astype(f32)
    dt = jax.nn.softplus((dt_raw + dt_bias).astype(f32))
    a = -jnp.exp(a_log.astype(f32))
    y, h_new = ssd_scan(xs, dt, a, bm, cm, h0.astype(f32))
    y = y + d_skip.astype(f32)[:, None] * xs
    y = y.reshape(b, l, SSD_D_INNER) * jax.nn.silu(z.astype(f32))
    yg = y.reshape(b, l, SSD_N_GROUPS, -1)
    yg = yg * lax.rsqrt(jnp.mean(yg * yg, axis=-1, keepdims=True) + EPS)
    y = yg.reshape(b, l, SSD_D_INNER) * norm_g.astype(f32)
    return y.astype(z.dtype), new_buf, h_new.astype(h0.dtype)


def indexer_select(qi, wi, ki, q_pos, n_keys, topk):
    dots = jnp.einsum('bthd,bsd->bths', qi, ki).astype(jnp.float32) * (IDX_HEAD_DIM ** -0.5)
    score = jnp.einsum('bth,bths->bts', wi.astype(jnp.float32) * (IDX_N_HEADS ** -0.5), jax.nn.relu(dots))
    allowed = jnp.arange(n_keys)[None, :] <= q_pos[:, None]
    score = jnp.where(allowed[None], score, -jnp.inf)
    _, sel = lax.top_k(score, topk)
    valid = sel <= q_pos[None, :, None]
    return sel, valid


def sparse_attend(q, k_sel, v_sel, valid):
    b, t = q.shape[:2]
    qg = q.reshape(b, t, ATT_N_KV_HEADS, ATT_N_HEADS // ATT_N_KV_HEADS, ATT_HEAD_DIM)
    s = jnp.einsum('btgrd,btkgd->btgrk', qg, k_sel).astype(jnp.float32) * (ATT_HEAD_DIM ** -0.5)
    s = jnp.where(valid[:, :, None, None, :], s, -jnp.inf)
    pr = jax.nn.softmax(s, axis=-1).astype(v_sel.dtype)
    o = jnp.einsum('btgrk,btkgd->btgrd', pr, v_sel)
    return o.reshape(b, t, ATT_N_HEADS * ATT_HEAD_DIM)


def dsa_prompt(q, k, v, qi, ki, wi):
    b, T = q.shape[:2]
    topk = min(TOPK_MAX, T // 4)

    def block(t0):
        sl = lambda arr: lax.dynamic_slice_in_dim(arr, t0, Q_BLOCK, axis=1)
        q_pos = t0 + jnp.arange(Q_BLOCK)
        sel, valid = indexer_select(sl(qi), sl(wi), ki, q_pos, T, topk)
        return sparse_attend(sl(q), gather_rows(k, sel), gather_rows(v, sel), valid)

    out = lax.map(block, jnp.arange(T // Q_BLOCK) * Q_BLOCK)
    return jnp.moveaxis(out, 0, 1).reshape(b, T, ATT_N_HEADS * ATT_HEAD_DIM)


def dsa_sample(q, k, v, qi, ki, wi, pool_k, pool_v, pool_kidx, page_table):
    b, l = q.shape[:2]
    n_pages = page_table.shape[1]
    past = n_pages * PAGE_SIZE
    n_keys = past + l
    topk = min(TOPK_MAX, n_keys // 4)
    ki_past = pool_kidx[page_table].reshape(b, past, IDX_HEAD_DIM)
    ki_all = jnp.concatenate([ki_past.astype(ki.dtype), ki], axis=1)
    q_pos = past + jnp.arange(l)
    sel, valid = indexer_select(qi, wi, ki_all, q_pos, n_keys, topk)
    is_past = sel < past
    ps = jnp.where(is_past, sel, 0)
    phys = jax.vmap(lambda pt, s: pt[s])(page_table, ps // PAGE_SIZE)
    off = ps % PAGE_SIZE
    ns = jnp.clip(sel - past, 0, l - 1)
    m = is_past[..., None, None]
    k_sel = jnp.where(m, pool_k[phys, off].astype(k.dtype), gather_rows(k, ns))
    v_sel = jnp.where(m, pool_v[phys, off].astype(v.dtype), gather_rows(v, ns))
    return sparse_attend(q, k_sel, v_sel, valid)


def memory_kv(mem, g, wk, wv):
    b, n, _ = mem.shape
    m = rmsnorm(mem, g)
    return ((m @ wk).reshape(b, n, MEM_N_HEADS, MEM_HEAD_DIM),
            (m @ wv).reshape(b, n, MEM_N_HEADS, MEM_HEAD_DIM))


def mem_attend(u, wq, mk, mv):
    b, l, _ = u.shape
    q = (u @ wq).reshape(b, l, MEM_N_HEADS, MEM_HEAD_DIM)
    s = jnp.einsum('blhd,bmhd->bhlm', q, mk.astype(q.dtype)).astype(jnp.float32) * (MEM_HEAD_DIM ** -0.5)
    pr = jax.nn.softmax(s, axis=-1).astype(q.dtype)
    return jnp.einsum('bhlm,bmhd->blhd', pr, mv.astype(q.dtype)).reshape(b, l, MEM_N_HEADS * MEM_HEAD_DIM)


def layer_forward(h, lp, conv_buf, ssm_h0, attn_fn, mem_k, mem_v):
    u = rmsnorm(h, lp['ffn1_pre_g'])
    h = h + 0.5 * rmsnorm(swiglu(u, lp['ffn1_wg'], lp['ffn1_wu'], lp['ffn1_wd']), lp['ffn1_post_g'])
    u = rmsnorm(h, lp['mix_pre_g'])
    b, l, _ = u.shape
    z, xbc, dt_raw, q, k, v, qi, ki, wi, g_ssd, g_att = jnp.split(u @ lp['w_in'], IN_OFFSETS, axis=-1)
    y_ssd, new_conv, new_ssm = ssd_branch(z, xbc, dt_raw, conv_buf, ssm_h0, lp['conv_w'], lp['conv_b'],
                                          lp['dt_bias'], lp['a_log'], lp['d_skip'], lp['ssd_norm_g'])
    q = q.reshape(b, l, ATT_N_HEADS, ATT_HEAD_DIM)
    k = k.reshape(b, l, ATT_N_KV_HEADS, ATT_HEAD_DIM)
    v = v.reshape(b, l, ATT_N_KV_HEADS, ATT_HEAD_DIM)
    qi = qi.reshape(b, l, IDX_N_HEADS, IDX_HEAD_DIM)
    y_att = attn_fn(q, k, v, qi, ki, wi)
    merged = (jax.nn.sigmoid(g_ssd) * (y_ssd @ lp['w_br_ssd'])
              + jax.nn.sigmoid(g_att) * (y_att @ lp['w_br_att']))
    h = h + rmsnorm(merged @ lp['w_out'], lp['mix_post_g'])
    u = rmsnorm(h, lp['mem_pre_g'])
    h = h + rmsnorm(mem_attend(u, lp['w_mq'], mem_k, mem_v) @ lp['w_mo'], lp['mem_post_g'])
    u = rmsnorm(h, lp['ffn2_pre_g'])
    h = h + 0.5 * rmsnorm(swiglu(u, lp['ffn2_wg'], lp['ffn2_wu'], lp['ffn2_wd']), lp['ffn2_post_g'])
    return h, (k, v, ki, new_ssm, new_conv)


def setup_inputs(seed: int = 0) -> dict:
    key = jax.random.key(seed)
    ks = iter(jax.random.split(key, 64))

    def nrm(shape, scale=1.0):
        return jax.random.normal(next(ks), shape, jnp.float32) * scale

    def gain(n):
        return 1.0 + nrm((DEPTH, n), 0.02)

    n_pages = PAST_LEN // PAGE_SIZE
    n_pool = (DEC_BATCH * n_pages * 5) // 4
    D = D_MODEL
    inp = {}
    inp['x_prompt'] = nrm((BATCH, SEQ, D))
    inp['x_sample'] = nrm((DEC_BATCH, DEC_SEQ, D))
    inp['mem_prompt'] = nrm((BATCH, MEM_LEN, D))
    inp['cache_k'] = nrm((DEPTH, n_pool, PAGE_SIZE, ATT_N_KV_HEADS, ATT_HEAD_DIM))
    inp['cache_v'] = nrm((DEPTH, n_pool, PAGE_SIZE, ATT_N_KV_HEADS, ATT_HEAD_DIM))
    inp['cache_kidx'] = nrm((DEPTH, n_pool, PAGE_SIZE, IDX_HEAD_DIM))
    inp['state_ssm'] = nrm((DEPTH, DEC_BATCH, SSD_N_HEADS, SSD_HEAD_DIM, SSD_D_STATE), 0.3)
    inp['state_conv'] = nrm((DEPTH, DEC_BATCH, SSD_CONV_W - 1, SSD_CONV_DIM))
    inp['cache_mem_k'] = nrm((DEPTH, DEC_BATCH, MEM_LEN, MEM_N_HEADS, MEM_HEAD_DIM))
    inp['cache_mem_v'] = nrm((DEPTH, DEC_BATCH, MEM_LEN, MEM_N_HEADS, MEM_HEAD_DIM))
    perm = jax.random.permutation(next(ks), n_pool)[: DEC_BATCH * n_pages]
    inp['page_table'] = perm.reshape(DEC_BATCH, n_pages).astype(jnp.int32)
    inp['ffn1_pre_g'] = gain(D)
    inp['ffn1_wg'] = nrm((DEPTH, D, FFN_HIDDEN), D ** -0.5)
    inp['ffn1_wu'] = nrm((DEPTH, D, FFN_HIDDEN), D ** -0.5)
    inp['ffn1_wd'] = nrm((DEPTH, FFN_HIDDEN, D), FFN_HIDDEN ** -0.5)
    inp['ffn1_post_g'] = gain(D)
    inp['mix_pre_g'] = gain(D)
    inp['w_in'] = nrm((DEPTH, D, IN_DIM), D ** -0.5)
    inp['conv_w'] = nrm((DEPTH, SSD_CONV_W, SSD_CONV_DIM), 0.5)
    inp['conv_b'] = nrm((DEPTH, SSD_CONV_DIM), 0.01)
    u = jax.random.uniform(next(ks), (DEPTH, SSD_N_HEADS))
    dt0 = jnp.exp(u * (math.log(0.1) - math.log(0.001)) + math.log(0.001))
    inp['dt_bias'] = dt0 + jnp.log(-jnp.expm1(-dt0))
    inp['a_log'] = jnp.log(jax.random.uniform(next(ks), (DEPTH, SSD_N_HEADS), minval=1.0, maxval=16.0))
    inp['d_skip'] = 1.0 + nrm((DEPTH, SSD_N_HEADS), 0.1)
    inp['ssd_norm_g'] = gain(SSD_D_INNER)
    inp['w_br_ssd'] = nrm((DEPTH, SSD_D_INNER, D), SSD_D_INNER ** -0.5)
    inp['w_br_att'] = nrm((DEPTH, ATT_N_HEADS * ATT_HEAD_DIM, D), (ATT_N_HEADS * ATT_HEAD_DIM) ** -0.5)
    inp['w_out'] = nrm((DEPTH, D, D), D ** -0.5)
    inp['mix_post_g'] = gain(D)
    inp['mem_pre_g'] = gain(D)
    inp['mem_kv_g'] = gain(D)
    inp['w_mq'] = nrm((DEPTH, D, MEM_N_HEADS * MEM_HEAD_DIM), D ** -0.5)
    inp['w_mk'] = nrm((DEPTH, D, MEM_N_HEADS * MEM_HEAD_DIM), D ** -0.5)
    inp['w_mv'] = nrm((DEPTH, D, MEM_N_HEADS * MEM_HEAD_DIM), D ** -0.5)
    inp['w_mo'] = nrm((DEPTH, MEM_N_HEADS * MEM_HEAD_DIM, D), (MEM_N_HEADS * MEM_HEAD_DIM) ** -0.5)
    inp['mem_post_g'] = gain(D)
    inp['ffn2_pre_g'] = gain(D)
    inp['ffn2_wg'] = nrm((DEPTH, D, FFN_HIDDEN), D ** -0.5)
    inp['ffn2_wu'] = nrm((DEPTH, D, FFN_HIDDEN), D ** -0.5)
    inp['ffn2_wd'] = nrm((DEPTH, FFN_HIDDEN, D), FFN_HIDDEN ** -0.5)
    inp['ffn2_post_g'] = gain(D)
    return inp


def reference(x_prompt, x_sample, mem_prompt, cache_k, cache_v, cache_kidx, state_ssm, state_conv,
              cache_mem_k, cache_mem_v, page_table,
              ffn1_pre_g, ffn1_wg, ffn1_wu, ffn1_wd, ffn1_post_g,
              mix_pre_g, w_in, conv_w, conv_b, dt_bias, a_log, d_skip, ssd_norm_g,
              w_br_ssd, w_br_att, w_out, mix_post_g,
              mem_pre_g, mem_kv_g, w_mq, w_mk, w_mv, w_mo, mem_post_g,
              ffn2_pre_g, ffn2_wg, ffn2_wu, ffn2_wd, ffn2_post_g):
    y_p, y_s = x_prompt, x_sample
    b_p = x_prompt.shape[0]
    acc = [[] for _ in range(12)]
    for i in range(DEPTH):
        lp = dict(ffn1_pre_g=ffn1_pre_g[i], ffn1_wg=ffn1_wg[i], ffn1_wu=ffn1_wu[i], ffn1_wd=ffn1_wd[i],
                  ffn1_post_g=ffn1_post_g[i], mix_pre_g=mix_pre_g[i], w_in=w_in[i], conv_w=conv_w[i],
                  conv_b=conv_b[i], dt_bias=dt_bias[i], a_log=a_log[i], d_skip=d_skip[i],
                  ssd_norm_g=ssd_norm_g[i], w_br_ssd=w_br_ssd[i], w_br_att=w_br_att[i], w_out=w_out[i],
                  mix_post_g=mix_post_g[i], mem_pre_g=mem_pre_g[i], w_mq=w_mq[i], w_mo=w_mo[i],
                  mem_post_g=mem_post_g[i], ffn2_pre_g=ffn2_pre_g[i], ffn2_wg=ffn2_wg[i], ffn2_wu=ffn2_wu[i],
                  ffn2_wd=ffn2_wd[i], ffn2_post_g=ffn2_post_g[i])
        mk_p, mv_p = memory_kv(mem_prompt, mem_kv_g[i], w_mk[i], w_mv[i])
        conv0 = jnp.zeros((b_p, SSD_CONV_W - 1, SSD_CONV_DIM), x_prompt.dtype)
        h0 = jnp.zeros((b_p, SSD_N_HEADS, SSD_HEAD_DIM, SSD_D_STATE), jnp.float32)
        y_p, (k_p, v_p, ki_p, ssm_p, conv_p) = layer_forward(y_p, lp, conv0, h0, dsa_prompt, mk_p, mv_p)
        samp_attn = functools.partial(dsa_sample, pool_k=cache_k[i], pool_v=cache_v[i],
                                      pool_kidx=cache_kidx[i], page_table=page_table)
        y_s, (k_s, v_s, ki_s, ssm_s, conv_s) = layer_forward(y_s, lp, state_conv[i], state_ssm[i], samp_attn,
                                                             cache_mem_k[i], cache_mem_v[i])
        for lst, val in zip(acc, (k_p, v_p, ki_p, ssm_p, conv_p, mk_p, mv_p, k_s, v_s, ki_s, ssm_s, conv_s)):
            lst.append(val)
    (nk_p, nv_p, nki_p, nssm_p, nconv_p, nmk_p, nmv_p,
     nk_s, nv_s, nki_s, nssm_s, nconv_s) = [jnp.stack(lst) for lst in acc]
    return (y_p, y_s, nk_p, nv_p, nki_p, nssm_p, nconv_p, nmk_p, nmv_p, nk_s, nv_s, nki_s, nssm_s, nconv_s)
```

```python
import os
import numpy as np
from contextlib import ExitStack
import concourse.bass as bass
import concourse.mybir as mybir
from concourse.bass_utils import run_bass_kernel_spmd

F32 = mybir.dt.float32
BF16 = mybir.dt.bfloat16
I32 = mybir.dt.int32
U32 = mybir.dt.uint32
AF = mybir.ActivationFunctionType
ALU = mybir.AluOpType
AX = mybir.AxisListType

D = 1024
KC = 8
SEQ = 2048
ST = 512
NST = SEQ // ST
FH = 2816
FHC = 22
IN_DIM = 10344
O_Z, O_X, O_B, O_C, O_DT, O_Q, O_K, O_V, O_QI, O_KI, O_WI, O_GS, O_GA = (
    0, 2048, 4096, 5120, 6144, 6176, 7200, 7456, 7712, 8224, 8288, 8296, 9320)
EPS = 1e-6
NIT = 14
HPERM = []
for _j in range(8):
    HPERM += [(_j // 4) * 8 + _j % 4, (_j // 4) * 8 + _j % 4 + 4]
WBE = 4096

PP_G = {n: i * 8 for i, n in enumerate(
    ["ffn1_pre_g", "ffn1_post_g", "mix_pre_g", "mix_post_g", "mem_pre_g", "mem_kv_g", "mem_post_g", "ffn2_pre_g",
     "ffn2_post_g"])}
PP_NG = 72
PP_CB = 88
PP_CW = 120
PP_DS = 248
PP_N = 264
C2N = 942
NITS = 22
PGP = [0, 1, 2, 3]


class _Rec:
    def __getattr__(self, name):
        def f(*a, **kw):
            self.call = (name, a, kw)
            return self
        return f


def _freeze(fn):
    r = _Rec()
    fn(r)
    name, a, kw = r.call
    return lambda e: getattr(e, name)(*a, **kw)


class Sched:
    ENG = ("pe", "act", "dve", "pool", "sp")

    def __init__(self, nc, es):
        self.nc = nc
        self.es = es
        self.ops = {e: [] for e in self.ENG}
        self.cnt = {e: 0 for e in self.ENG}
        self.sem = {e: es.enter_context(nc.semaphore("s_" + e)) for e in self.ENG}
        self.waited = {e: {} for e in self.ENG}
        self.lastw = {}
        self.readers = {}
        self.chan = {}
        self.nt = 0

    def sb(self, shape, dtype, name=None):
        self.nt += 1
        return self.es.enter_context(self.nc.sbuf_tensor("sb_" + (name or f"t{self.nt}"), list(shape), dtype))

    def ps(self, shape, dtype, name=None):
        self.nt += 1
        return self.es.enter_context(self.nc.psum_tensor("ps_" + (name or f"p{self.nt}"), list(shape), dtype))

    def _key(self, k):
        if isinstance(k, (str, tuple)):
            return k
        return k.name

    def alias(self, newk, oldks):
        newk = self._key(newk)
        lst = self.readers.setdefault(newk, [])
        for o in oldks:
            o = self._key(o)
            lst.extend(self.readers.get(o, []))
            if o in self.lastw:
                lst.append(self.lastw[o])

    def _collect(self, eng, reads, writes):
        deps = []
        for k in reads:
            k = self._key(k)
            w = self.lastw.get(k)
            if w is not None:
                deps.append(("raw", w))
            if isinstance(k, str) and k.startswith("ps_"):
                for r in self.readers.get(k, ()):
                    if r[0] != eng:
                        deps.append(("rar", r))
        for k in writes:
            k = self._key(k)
            w = self.lastw.get(k)
            if w is not None:
                deps.append(("waw", w))
            for r in self.readers.get(k, ()):
                deps.append(("war", r))
        waits = []
        for kind, (skey, val) in deps:
            if skey == eng:
                if eng == "pe" or kind == "war":
                    continue
            if self.waited[eng].get(skey, 0) >= val:
                continue
            self.waited[eng][skey] = val
            waits.append((skey, val))
        return waits

    def _commit(self, tok, reads, writes):
        for k in writes:
            k = self._key(k)
            self.lastw[k] = tok
            self.readers[k] = []
        for k in reads:
            self.readers.setdefault(self._key(k), []).append(tok)

    def op(self, eng, fn, reads=(), writes=()):
        waits = self._collect(eng, reads, writes)
        self.cnt[eng] += 1
        self.ops[eng].append((waits, _freeze(fn), eng))
        self._commit((eng, self.cnt[eng]), reads, writes)

    def dma(self, chan, fn, reads=(), writes=(), queue="sp"):
        waits = self._collect(queue, reads, writes)
        if chan not in self.chan:
            self.chan[chan] = [self.es.enter_context(self.nc.semaphore("c_" + str(len(self.chan)))), 0]
        ch = self.chan[chan]
        ch[1] += 16
        self.ops[queue].append((waits, _freeze(fn), ("ch", chan)))
        self._commit((("ch", chan), ch[1]), reads, writes)

    def _semof(self, skey):
        return self.chan[skey[1]][0] if isinstance(skey, tuple) else self.sem[skey]

    def emit(self):
        fin = [(("ch", c), v) for c, (s, v) in self.chan.items()]
        fin += [(e, self.cnt[e]) for e in self.ENG if e != "sp" and self.cnt[e] > 0]

        def run(engname, e):
            for waits, fn, inc in self.ops[engname]:
                for skey, val in waits:
                    e.wait_ge(self._semof(skey), val)
                inst = fn(e)
                if isinstance(inc, tuple):
                    inst.then_inc(self.chan[inc[1]][0], 16)
                else:
                    inst.then_inc(self.sem[inc], 1)
            if engname == "sp":
                for skey, val in fin:
                    e.wait_ge(self._semof(skey), val)

        with self.nc.Block() as block:
            @block.sync
            def _(e):
                run("sp", e)

            @block.scalar
            def _(e):
                run("act", e)

            @block.vector
            def _(e):
                run("dve", e)

            @block.gpsimd
            def _(e):
                run("pool", e)

            @block.tensor
            def _(e):
                run("pe", e)


class Ctx:
    pass


def build_nc(debug=None):
    nc = bass.Bass("TRN2", target_bir_lowering=False, dynamic_dma_scratch_size=8192)
    dt_in = {}

    def din(name, shape, dt=F32):
        dt_in[name] = nc.dram_tensor(name, list(shape), dt, kind="ExternalInput").ap()
        return dt_in[name]

    def dout(name, shape, dt=F32):
        return nc.dram_tensor(name, list(shape), dt, kind="ExternalOutput").ap()

    xT = din("xT", [D, SEQ])
    xsT = din("xsT", [D, 16])
    memT = din("memT", [D, 256])
    pp_d = din("pp", [128, PP_N])
    rowp_d = din("rowp", [128, 64])
    cst_d = din("cst", [128, 5 * 128])
    W = {}
    for n, shp in [("ffn1_wg", [D, FH]), ("ffn1_wu", [D, FH]), ("ffn1_wd", [FH, D]), ("w_in", [D, IN_DIM]),
                   ("w_br_ssd", [2048, D]), ("w_br_att", [D, D]), ("w_out", [D, D]), ("w_mq", [D, D]),
                   ("w_mk", [D, D]), ("w_mv", [D, D]), ("w_mo", [D, D]),
                   ("ffn2_wg", [D, FH]), ("ffn2_wu", [D, FH]), ("ffn2_wd", [FH, D])]:
        W[n] = din(n, shp)

    yT_o = dout("yT", [D, SEQ])
    ysT_o = dout("ysT", [D, 16])
    kT_o = dout("kT", [256, SEQ])
    v_o = dout("v_o", [SEQ, 256])
    kiT_o = dout("kiT", [64, SEQ])
    ssmT_o = dout("ssmT", [128, 2048])
    convT_o = dout("convT", [4096, 3])
    mkT_o = dout("mkT", [D, 256])
    mv_o = dout("mv_o", [256, D])
    hists_d = din("hists", [128, 384])
    ssmsT_d = din("ssmsT", [4, 128, 2048])
    cmkT_d = din("cmkT", [4, D, 256])
    cmv_d = din("cmv", [4, 256, D])
    ptab_d = din("ptab", [4, 128], I32)
    cst2_d = din("cst2", [128, C2N])
    if "sample" not in os.environ.get("KSKIP", "").split(","):
        kidxT_d = din("kidxT", [5120 * 64, 128])
        poolk_d = din("poolk", [5120 * 128, 256])
        poolv_d = din("poolv", [5120 * 128, 256])
    ks_o = dout("ks_o", [16, 256])
    vs_o = dout("vs_o", [16, 256])
    kisT_o = dout("kisT", [64, 16])
    ssms_o = dout("ssms", [4, 128, 2048])
    convs_o = dout("convs", [128, 384])
    dbg_o = None

    with ExitStack() as es:
        S = Sched(nc, es)
        c = Ctx()
        pp = S.sb([128, PP_N], F32, "pp")
        rowp = S.sb([128, 64], F32, "rowp")
        cst = S.sb([128, 640], F32, "cst")
        cstb = S.sb([128, 640], BF16, "cstb")
        S.dma("pp", lambda e: e.dma_start(out=pp[:], in_=pp_d), writes=[pp])
        S.dma("rowp", lambda e: e.dma_start(out=rowp[:], in_=rowp_d), writes=[rowp])
        S.dma("cst", lambda e: e.dma_start(out=cst[:], in_=cst_d), writes=[cst])
        S.dma("cstb", lambda e: e.dma_start(out=cstb[:], in_=cst_d), writes=[cstb], queue="pool")
        ident = cst[:, 0:128]
        Uf = cst[:, 128:256]
        negm = cst[:, 256:384]
        identb = cstb[:, 0:128]
        onesb = cstb[:, 384:512]
        trib = cstb[:, 512:640]
        epsT = S.sb([128, 1], F32, "epsT")
        S.op("dve", lambda e: e.memset(epsT[:], EPS), writes=[epsT])
        arow = S.sb([128, 32], F32, "arow")
        S.op("act", lambda e: e.activation(out=arow[:], in_=rowp[:, 32:64], func=AF.Exp), reads=[rowp], writes=[arow])
        S.op("dve", lambda e: e.tensor_scalar(out=arow[:], in0=arow[:], scalar1=-1.0, scalar2=None, op0=ALU.mult),
             reads=[arow], writes=[arow])

        gen = [S.ps([128, 512], F32, f"pg{i}") for i in range(5)]
        accA = S.ps([128, 512], F32, "accA")
        accB = S.ps([128, 512], F32, "accB")
        ptb = S.ps([128, 1024], BF16, "ptb")
        c.gi = 0
        c.ti = 0

        def P():
            c.gi = (c.gi + 1) % len(gen)
            return gen[c.gi]

        def PTS(i):
            return ptb[:, i * 128:(i + 1) * 128]

        NWB = 3
        wbufs = [S.sb([128, WBE], BF16, f"wb{i}") for i in range(NWB)]
        c.wi = 0

        def wload(wd, r0, nk, c0, ncols):
            assert nk * ncols <= WBE
            c.wi = (c.wi + 1) % NWB
            buf = wbufs[c.wi]
            view = buf[:, 0:nk * ncols].rearrange("p (k c) -> p k c", k=nk)
            src = wd[r0:r0 + nk * 128, c0:c0 + ncols].rearrange("(k p) c -> p k c", p=128)
            S.dma(("w", c.wi), lambda e: e.dma_start(out=view, in_=src), writes=[buf], queue="pool")
            return buf, view

        hT = S.sb([128, KC, ST], F32, "hT")
        uT = S.sb([128, KC, ST], BF16, "uT")
        scrM = S.sb([128, 4096], F32, "scrM")
        ybuf = scrM[:].rearrange("p (k t) -> p k t", k=KC)
        SCRK = ["ssdtmp", "maskT", "zs", ("cv", 0), ("cv", 1), ("cv", 2), ("cv", 3)]
        sqb = S.sb([128, ST], BF16, "sqb")
        sqb2 = S.sb([128, ST], BF16, "sqb2")
        rstd = S.sb([128, ST], F32, "rstd")
        tmpf = S.sb([128, ST], F32, "tmpf")
        big = S.sb([128, 22528], BF16, "big")
        hid = big[:, 0:FHC * ST].rearrange("p (k t) -> p k t", k=FHC)
        sgs = [S.sb([128, ST], BF16, f"sg{i}") for i in range(2)]

        def gcol(name, k):
            return pp[:, PP_G[name] + k:PP_G[name] + k + 1]

        def norm_stats(src, srck, T, scale_div):
            ps = P()
            n = len(src)
            for k in range(n):
                sq = sqb if k % 2 == 0 else sqb2
                S.op("act", lambda e, k=k, sq=sq: e.activation(out=sq[:, :T], in_=src[k], func=AF.Square),
                     reads=[srck], writes=[sq])
                S.op("pe", lambda e, k=k, sq=sq: e.matmul(ps[:, :T], onesb, sq[:, :T], start=(k == 0), stop=(k == n - 1)),
                     reads=[sq, cstb], writes=[ps])
            S.op("act", lambda e: e.activation(out=rstd[:, :T], in_=ps[:, :T], func=AF.Sqrt, bias=epsT[:, 0:1],
                                               scale=1.0 / scale_div), reads=[ps, epsT], writes=[rstd])
            S.op("dve", lambda e: e.reciprocal(out=rstd[:, :T], in_=rstd[:, :T]), reads=[rstd], writes=[rstd])

        def prenorm(gname, T, src=None, srck=None, dst=None):
            src = src if src is not None else [hT[:, k, :T] for k in range(KC)]
            srck = srck if srck is not None else hT
            dst = dst if dst is not None else uT
            norm_stats(src, srck, T, float(D))
            for k in range(KC):
                S.op("dve", lambda e, k=k: e.scalar_tensor_tensor(out=dst[:, k, :T], in0=src[k], scalar=gcol(gname, k),
                                                                   in1=rstd[:, :T], op0=ALU.mult, op1=ALU.mult),
                     reads=[srck, rstd, pp], writes=[dst])

        def postnorm_add(gname, T, coef):
            norm_stats([ybuf[:, k, :T] for k in range(KC)], "ybuf", T, float(D))
            for k in range(KC):
                S.op("dve", lambda e, k=k: e.scalar_tensor_tensor(out=tmpf[:, :T], in0=ybuf[:, k, :T], scalar=gcol(gname, k),
                                                                   in1=rstd[:, :T], op0=ALU.mult, op1=ALU.mult),
                     reads=["ybuf", rstd, pp], writes=[tmpf])
                S.op("dve", lambda e, k=k: e.scalar_tensor_tensor(out=hT[:, k, :T], in0=tmpf[:, :T], scalar=coef,
                                                                   in1=hT[:, k, :T], op0=ALU.mult, op1=ALU.add),
                     reads=[tmpf, hT], writes=[hT])

        def proj_fm(wd, r0, nk, c0, ncols, rhs_fn, rhs_keys, T, consumer, blk=512, msz=128):
            blk = min(blk, (WBE // nk) // msz * msz)
            idx = 0
            for b0 in range(0, ncols, blk):
                bc = min(blk, ncols - b0)
                buf, view = wload(wd, r0, nk, c0 + b0, bc)
                for m0 in range(0, bc, msz):
                    ms = min(msz, bc - m0)
                    ps = P()
                    for k in range(nk):
                        S.op("pe", lambda e, k=k, m0=m0, ms=ms, ps=ps, view=view: e.matmul(
                            ps[0:ms, :T], view[:, k, m0:m0 + ms], rhs_fn(k), start=(k == 0), stop=(k == nk - 1)),
                            reads=[buf] + rhs_keys, writes=[ps])
                    consumer(idx, ps, ms)
                    idx += 1

        def ffn(pref, T):
            prenorm(pref + "_pre_g", T)
            S.alias("hid", MIXK + ["acc"])
            S.alias("ybuf", SCRK)
            for b0 in range(0, FH, 256):
                bufg, vg = wload(W[pref + "_wg"], 0, KC, b0, 256)
                bufu, vu = wload(W[pref + "_wu"], 0, KC, b0, 256)
                for m in range(2):
                    hc = b0 // 128 + m
                    pg, pu = P(), P()
                    for k in range(KC):
                        S.op("pe", lambda e, k=k, m=m, pg=pg, vg=vg: e.matmul(pg[:, :T], vg[:, k, m * 128:(m + 1) * 128],
                                                                             uT[:, k, :T], start=(k == 0), stop=(k == KC - 1)),
                             reads=[bufg, uT], writes=[pg])
                    for k in range(KC):
                        S.op("pe", lambda e, k=k, m=m, pu=pu, vu=vu: e.matmul(pu[:, :T], vu[:, k, m * 128:(m + 1) * 128],
                                                                             uT[:, k, :T], start=(k == 0), stop=(k == KC - 1)),
                             reads=[bufu, uT], writes=[pu])
                    sg = sgs[hc % 2]
                    S.op("act", lambda e, pg=pg, sg=sg: e.activation(out=sg[:, :T], in_=pg[:, :T], func=AF.Silu),
                         reads=[pg], writes=[sg])
                    S.op("dve", lambda e, pu=pu, sg=sg, hc=hc: e.tensor_tensor(out=hid[:, hc, :T], in0=sg[:, :T], in1=pu[:, :T],
                                                                              op=ALU.mult), reads=[pu, sg], writes=["hid"])

            def cons(idx, ps, ms):
                S.op("act", lambda e: e.activation(out=ybuf[:, idx, :T], in_=ps[:, :T], func=AF.Copy), reads=[ps],
                     writes=["ybuf"])
            proj_fm(W[pref + "_wd"], 0, FHC, 0, D, lambda k: hid[:, k, :T], ["hid"], T, cons, blk=128)
            postnorm_add(pref + "_post_g", T, 0.5)


        MIXK = ["qT", "qiT", "yssdT", "yattT", "mergedT"]
        qT = big[:, 0:4096].rearrange("p (k t) -> p k t", k=8)
        qiT = big[:, 4096:6144].rearrange("p (k t) -> p k t", k=4)
        yssdT = big[:, 6144:14336].rearrange("p (k t) -> p k t", k=16)
        yattT = big[:, 14336:18432].rearrange("p (k t) -> p k t", k=8)
        mergedT = big[:, 18432:22528].rearrange("p (k t) -> p k t", k=8)
        kT2 = S.sb([128, 2, SEQ], BF16, "kT2")
        kiT2 = S.sb([128, SEQ], BF16, "kiT2")
        vtok = S.sb([128, 16, 256], BF16, "vtok")
        stT = S.sb([128, 8, 256], F32, "stT")
        stTb = S.sb([128, 8, 256], BF16, "stTb")
        hist = S.sb([128, 32, 3], F32, "hist")
        mkTb = S.sb([128, 8, 256], BF16, "mkTb")
        mvb = S.sb([128, 2, D], BF16, "mvb")
        for t_ in (stT, stTb, hist):
            S.op("dve", lambda e, t_=t_: e.memset(t_[:], 0.0), writes=[t_])
        maskT = scrM[:].bitcast(BF16).rearrange("p (k t) -> p k t", k=16)
        pre = scrM[:, 0:2060].rearrange("p (j t) -> p j t", j=4)
        cv = scrM[:, 2064:3088].bitcast(BF16).rearrange("p (j t) -> p j t", j=4)
        zs = scrM[:, 3088:3600].bitcast(BF16).rearrange("p (j t) -> p j t", j=2)
        Yg = S.sb([128, 2, ST], F32, "Yg")
        acc = big[:, 18432:22528].bitcast(F32)
        mask01t = S.sb([128, 2048], BF16, "mask01t")
        mask01 = mask01t[:]
        junk = mask01t[:]
        stg = [S.sb([128, 512], F32, f"stg{i}") for i in range(1)]
        c.si = 0
        Es = [S.sb([128, ST], BF16, f"E{i}") for i in range(2)]
        c.ei = 0
        rrs = [S.sb([128, 512], F32, f"rr{i}") for i in range(2)]
        rden = S.sb([128, ST], F32, "rden")
        witok = S.sb([128, 4, 8], F32, "witok")
        dtt = S.sb([128, 4, 32], F32, "dtt")
        dta = S.sb([128, 4, 32], F32, "dta")
        acsc = S.sb([128, 4, 32], F32, "acsc")
        arw = S.sb([128, 512], F32, "arw")
        erow = S.sb([128, 512], F32, "erow")
        Cdec = S.sb([128, 512], BF16, "Cdec")
        xs_tok = S.sb([128, 256], BF16, "xs_tok")
        B_tok = S.sb([128, 128], BF16, "B_tok")
        cbm = S.sb([128, 128], BF16, "cbm")
        argt = [S.sb([128, 128], F32, f"arg{i}") for i in range(2)]
        Ldt = [S.sb([128, 128], BF16, f"Ld{i}") for i in range(2)]
        MTt = [S.sb([128, 128], BF16, f"MT{i}") for i in range(2)]
        xdt = S.sb([128, 256], BF16, "xdt")
        xdtw = S.sb([128, 256], BF16, "xdtw")
        sm4 = S.sb([128, 16], F32, "sm4")
        bis = S.sb([128, 8], F32, "bis")

        def stage_out(src_ps, rows, cols, dst_ap, dkey):
            c.si = 0
            sg_ = stg[c.si]
            S.op("act", lambda e: e.activation(out=sg_[0:rows, 0:cols], in_=src_ps, func=AF.Copy), reads=[dkey[0]], writes=[sg_])
            S.dma(("stg", c.si), lambda e: e.dma_start(out=dst_ap, in_=sg_[0:rows, 0:cols]), reads=[sg_], writes=[dkey[1]])

        def copy_alt(i, out, in_, reads, writes):
            if i % 2 == 0:
                S.op("act", lambda e: e.activation(out=out, in_=in_, func=AF.Copy), reads=reads, writes=writes)
            else:
                S.op("dve", lambda e: e.tensor_copy(out=out, in_=in_), reads=reads, writes=writes)

        def tm_proj(view, buf, c0, n, tt, ps):
            for k in range(KC):
                S.op("pe", lambda e, k=k: e.matmul(ps[:, 0:n], uT[:, k, tt * 128:(tt + 1) * 128], view[:, k, c0:c0 + n],
                                                   start=(k == 0), stop=(k == KC - 1)), reads=[buf, uT], writes=[ps])

        def fm64(view, buf, c0, ps, T):
            for half in range(2):
                for k in range(KC):
                    S.op("pe", lambda e, k=k, half=half: e.matmul(ps[half * 64:(half + 1) * 64, :T], view[:, k, c0:c0 + 64],
                                                                  uT[:, k, :T], start=(k == 0), stop=(k == KC - 1)),
                         reads=[buf, uT], writes=[ps])

        def mix_prompt(st):
            t0 = st * ST
            for k_ in MIXK:
                S.alias(k_, ["hid"])
            for k_ in SCRK:
                S.alias(k_, ["ybuf"])
            S.alias("ssdtmp", ["maskT"])
            prenorm("mix_pre_g", ST)
            win = W["w_in"]
            KM = os.environ.get("KMIX", "k,v,ki,wi,dt,q").split(",")
            buf, view = wload(win, 0, KC, O_K, 512)
            for kc in range(2 if "k" in KM else 0):
                ps = P()
                for k in range(KC):
                    S.op("pe", lambda e, k=k, kc=kc, ps=ps: e.matmul(ps[:, :ST], view[:, k, kc * 128:(kc + 1) * 128], uT[:, k, :ST],
                                                                     start=(k == 0), stop=(k == KC - 1)), reads=[buf, uT], writes=[ps])
                KK = os.environ.get("KK", "copy,stage").split(",")
                if "copy" in KK:
                    S.op("dve", lambda e, kc=kc, ps=ps: e.tensor_copy(out=kT2[:, kc, t0:t0 + ST], in_=ps[:, :ST]), reads=[ps], writes=[kT2])
                if "stage" in KK:
                    stage_out(ps[:, :ST], 128, ST, kT_o[kc * 128:(kc + 1) * 128, t0:t0 + ST], (ps, "kT_o"))
            for tt in range(4 if "v" in KM else 0):
                ps = P()
                tm_proj(view, buf, 256, 256, tt, ps)
                S.op("dve", lambda e, tt=tt, ps=ps: e.tensor_copy(out=vtok[:, st * 4 + tt, :], in_=ps[:, 0:256]), reads=[ps], writes=[vtok])
                stage_out(ps[:, 0:256], 128, 256, v_o[t0 + tt * 128:t0 + (tt + 1) * 128, :], (ps, "v_o"))
            buf, view = wload(win, 0, KC, O_KI - 32, 128)
            ps = P()
            if "ki" in KM:
                fm64(view, buf, 32, ps, ST)
                S.op("dve", lambda e, ps=ps: e.tensor_copy(out=kiT2[:, t0:t0 + ST], in_=ps[:, :ST]), reads=[ps], writes=[kiT2])
                stage_out(ps[0:64, :ST], 64, ST, kiT_o[:, t0:t0 + ST], (ps, "kiT_o"))
            for tt in range(4 if "wi" in KM else 0):
                ps = P()
                tm_proj(view, buf, 96, 8, tt, ps)
                S.op("dve", lambda e, tt=tt, ps=ps: e.tensor_copy(out=witok[:, tt, :], in_=ps[:, 0:8]), reads=[ps], writes=[witok])
            buf, view = wload(win, 0, KC, O_DT, 128)
            if "dt" not in KM:
                return
            for tt in range(4):
                ps = P()
                tm_proj(view, buf, 0, 32, tt, ps)
                S.op("dve", lambda e, tt=tt, ps=ps: e.tensor_tensor(out=dtt[:, tt, :], in0=ps[:, 0:32], in1=rowp[:, 0:32], op=ALU.add),
                     reads=[ps, rowp], writes=[dtt])
            S.op("act", lambda e: e.activation(out=dtt[:], in_=dtt[:], func=AF.Exp), reads=[dtt], writes=[dtt])
            S.op("act", lambda e: e.activation(out=dtt[:], in_=dtt[:], func=AF.Ln, bias=1.0), reads=[dtt], writes=[dtt])
            for tt in range(4):
                S.op("dve", lambda e, tt=tt: e.tensor_tensor(out=dta[:, tt, :], in0=dtt[:, tt, :], in1=arow[:], op=ALU.mult),
                     reads=[dtt, arow], writes=[dta])
                ps = P()
                S.op("pe", lambda e, tt=tt, ps=ps: e.matmul(ps[:, 0:32], Uf, dta[:, tt, :], start=True, stop=True), reads=[cst, dta], writes=[ps])
                S.op("act", lambda e, tt=tt, ps=ps: e.activation(out=acsc[:, tt, :], in_=ps[:, 0:32], func=AF.Copy), reads=[ps], writes=[acsc])
            proj_fm(win, 0, KC, O_Q, D, lambda k: uT[:, k, :ST], [uT], ST,
                    lambda idx, ps, ms: copy_alt(idx, qT[:, idx, :], ps[:, :ST], [ps], ["qT"]))
            proj_fm(win, 0, KC, O_QI, 512, lambda k: uT[:, k, :ST], [uT], ST,
                    lambda idx, ps, ms: copy_alt(idx, qiT[:, idx, :], ps[:, :ST], [ps], ["qiT"]))
            STG = os.environ.get("KSTAGE", "all")
            if STG == "mixA":
                return
            for g in range(8):
                ssd_group(st, g)
            if STG == "ssd":
                return
            S.alias("maskT", ["ssdtmp"])
            dsa_prompt(st)
            if STG == "dsa":
                return
            merge(ST)

        def conv_chunk(j, ch, T):
            cw = lambda tap: pp[:, PP_CW + ch * 4 + tap:PP_CW + ch * 4 + tap + 1]
            S.op("dve", lambda e: e.tensor_scalar(out=tmpf[:, :T], in0=pre[:, j, 0:T], scalar1=cw(0), scalar2=pp[:, PP_CB + ch:PP_CB + ch + 1],
                                                  op0=ALU.mult, op1=ALU.add), reads=["ssdtmp", pp], writes=[tmpf])
            for tap in range(1, 4):
                S.op("dve", lambda e, tap=tap: e.scalar_tensor_tensor(out=tmpf[:, :T], in0=pre[:, j, tap:tap + T], scalar=cw(tap),
                                                                      in1=tmpf[:, :T], op0=ALU.mult, op1=ALU.add),
                     reads=["ssdtmp", pp, tmpf], writes=[tmpf])
            S.op("act", lambda e: e.activation(out=cv[:, j, :T], in_=tmpf[:, :T], func=AF.Silu), reads=[tmpf], writes=[("cv", j)])

        def ssd_group(st, g):
            win = W["w_in"]
            T = ST
            buf, view = wload(win, 0, KC, O_Z + g * 256, 256)
            for m in range(2):
                ps = P()
                for k in range(KC):
                    S.op("pe", lambda e, k=k, m=m, ps=ps: e.matmul(ps[:, :T], view[:, k, m * 128:(m + 1) * 128], uT[:, k, :T],
                                                                   start=(k == 0), stop=(k == KC - 1)), reads=[buf, uT], writes=[ps])
                S.op("act", lambda e, m=m, ps=ps: e.activation(out=zs[:, m, :T], in_=ps[:, :T], func=AF.Silu), reads=[ps], writes=["zs"])
            chs = [g * 2, g * 2 + 1, 16 + g, 24 + g]
            srcs = [(O_X + g * 256, 0), (O_X + g * 256, 128), (O_B + g * 128, 0), (O_C + g * 128, 0)]
            bufx, viewx = wload(win, 0, KC, O_X + g * 256, 256)
            bufb, viewb = wload(win, 0, KC, O_B + g * 128, 128)
            views = [(bufx, viewx, 0), (bufx, viewx, 128), (bufb, viewb, 0), None]
            for j in range(4):
                if j == 3:
                    bufc, viewc = wload(win, 0, KC, O_C + g * 128, 128)
                    views[3] = (bufc, viewc, 0)
                bf_, vw_, c0 = views[j]
                ps = P()
                for k in range(KC):
                    S.op("pe", lambda e, k=k, ps=ps, vw_=vw_, c0=c0: e.matmul(ps[:, :T], vw_[:, k, c0:c0 + 128], uT[:, k, :T],
                                                                              start=(k == 0), stop=(k == KC - 1)), reads=[bf_, uT], writes=[ps])
                ch = chs[j]
                S.op("dve", lambda e, j=j, ch=ch: e.tensor_copy(out=pre[:, j, 0:3], in_=hist[:, ch, :]), reads=[hist], writes=["ssdtmp"])
                S.op("act", lambda e, j=j, ps=ps: e.activation(out=pre[:, j, 3:3 + T], in_=ps[:, :T], func=AF.Copy), reads=[ps], writes=["ssdtmp"])
                S.op("dve", lambda e, j=j, ch=ch: e.tensor_copy(out=hist[:, ch, :], in_=pre[:, j, T:T + 3]), reads=["ssdtmp"], writes=[hist])
                conv_chunk(j, ch, T)
            for cc in range(4):
                ssd_chunk(g, cc, 128, cc * 128)
            for m in range(2):
                S.op("dve", lambda e, m=m: e.tensor_tensor(out=Yg[:, m, :], in0=Yg[:, m, :], in1=zs[:, m, :], op=ALU.mult),
                     reads=[Yg, "zs"], writes=[Yg])
            norm_stats([Yg[:, 0, :], Yg[:, 1, :]], Yg, T, 256.0)
            for m in range(2):
                S.op("dve", lambda e, m=m: e.scalar_tensor_tensor(out=yssdT[:, g * 2 + m, :], in0=Yg[:, m, :],
                                                                   scalar=pp[:, PP_NG + g * 2 + m:PP_NG + g * 2 + m + 1], in1=rstd[:, :T],
                                                                   op0=ALU.mult, op1=ALU.mult), reads=[Yg, rstd, pp], writes=["yssdT"])

        def ssd_chunk(g, cc, L, col0, sf=None, sbf=None, skeys=None):
            cs = slice(col0, col0 + L)
            if sf is None:
                sf = lambda hh: stT[:, g, hh * 64:(hh + 1) * 64]
                sbf = lambda hh: stTb[:, g, hh * 64:(hh + 1) * 64]
                skeys = (stT, stTb)
            for m in range(3):
                S.op("pe", lambda e, m=m: e.transpose(PTS(m)[0:L, :], cv[:, m, cs], identb), reads=[("cv", m), cstb], writes=[ptb])
            S.op("act", lambda e: e.activation(out=xs_tok[0:L, :], in_=ptb[0:L, 0:256], func=AF.Copy), reads=[ptb], writes=[xs_tok])
            S.op("act", lambda e: e.activation(out=B_tok[0:L, :], in_=ptb[0:L, 256:384], func=AF.Copy), reads=[ptb], writes=[B_tok])
            ps = P()
            S.op("pe", lambda e, ps=ps: e.matmul(ps[0:L, 0:L], cv[:, 2, cs], cv[:, 3, cs], start=True, stop=True),
                 reads=[("cv", 2), ("cv", 3)], writes=[ps])
            S.op("dve", lambda e, ps=ps: e.tensor_tensor(out=cbm[0:L, 0:L], in0=ps[0:L, 0:L], in1=trib[0:L, 0:L], op=ALU.mult),
                 reads=[ps, cstb], writes=[cbm])
            psr = P()
            for hh in range(4):
                h = g * 4 + hh
                S.op("pe", lambda e, hh=hh, h=h: e.matmul(psr[:, hh * 128:hh * 128 + L], dta[0:L, cc, h:h + 1].to_broadcast([L, 128]),
                                                          Uf[0:L, 0:L], start=True, stop=True), reads=[dta, cst], writes=[psr])
            arw3 = arw[:].rearrange("p (a b) -> p a b", a=4)
            erow3 = erow[:].rearrange("p (a b) -> p a b", a=4)
            psr3 = psr[:].rearrange("p (a b) -> p a b", a=4)
            S.op("act", lambda e: e.activation(out=arw3[:, :, 0:L], in_=psr3[:, :, 0:L], func=AF.Copy), reads=[psr], writes=[arw])
            S.op("act", lambda e: e.activation(out=erow3[:, :, 0:L], in_=arw3[:, :, 0:L], func=AF.Exp), reads=[arw], writes=[erow])
            S.op("dve", lambda e: e.tensor_tensor(out=sm4[0:L, 0:4], in0=arw3[0:L, :, L - 1], in1=acsc[0:L, cc, g * 4:g * 4 + 4], op=ALU.subtract),
                 reads=[arw, acsc], writes=[sm4])
            S.op("act", lambda e: e.activation(out=sm4[0:L, 4:8], in_=sm4[0:L, 0:4], func=AF.Exp), reads=[sm4], writes=[sm4])
            S.op("dve", lambda e: e.tensor_tensor(out=sm4[0:L, 8:12], in0=sm4[0:L, 4:8], in1=dtt[0:L, cc, g * 4:g * 4 + 4], op=ALU.mult),
                 reads=[sm4, dtt], writes=[sm4])
            Cd3 = Cdec[:].rearrange("p (a b) -> p a b", a=4)
            S.op("dve", lambda e: e.tensor_tensor(out=Cd3[:, :, 0:L], in0=erow3[:, :, 0:L],
                                                  in1=cv[:, 3, cs].unsqueeze(1).to_broadcast([128, 4, L]), op=ALU.mult),
                 reads=[erow, ("cv", 3)], writes=[Cdec])
            psy = [P(), P()]
            for hh in range(4):
                h = g * 4 + hh
                m, half = hh // 2, hh % 2
                ag, Ld, MT = argt[hh % 2], Ldt[hh % 2], MTt[hh % 2]
                S.op("dve", lambda e, hh=hh, h=h, ag=ag: e.tensor_scalar(out=ag[0:L, 0:L], in0=arw[0:L, hh * 128:hh * 128 + L],
                                                                        scalar1=acsc[0:L, cc, h:h + 1], scalar2=0.0, op0=ALU.subtract, op1=ALU.min),
                     reads=[arw, acsc], writes=[ag])
                S.op("act", lambda e, ag=ag, Ld=Ld: e.activation(out=Ld[0:L, 0:L], in_=ag[0:L, 0:L], func=AF.Exp), reads=[ag], writes=[Ld])
                S.op("dve", lambda e, Ld=Ld, MT=MT: e.tensor_tensor(out=MT[0:L, 0:L], in0=Ld[0:L, 0:L], in1=cbm[0:L, 0:L], op=ALU.mult),
                     reads=[Ld, cbm], writes=[MT])
                S.op("dve", lambda e, hh=hh, h=h: e.tensor_scalar(out=xdt[0:L, hh * 64:(hh + 1) * 64], in0=xs_tok[0:L, hh * 64:(hh + 1) * 64],
                                                                 scalar1=dtt[0:L, cc, h:h + 1], scalar2=None, op0=ALU.mult),
                     reads=[xs_tok, dtt], writes=[xdt])
                S.op("dve", lambda e, hh=hh: e.tensor_scalar(out=xdtw[0:L, hh * 64:(hh + 1) * 64], in0=xs_tok[0:L, hh * 64:(hh + 1) * 64],
                                                            scalar1=sm4[0:L, 8 + hh:9 + hh], scalar2=None, op0=ALU.mult),
                     reads=[xs_tok, sm4], writes=[xdtw])
                py = psy[m]
                S.op("pe", lambda e, hh=hh, half=half, py=py, MT=MT: e.matmul(py[half * 64:(half + 1) * 64, 0:L], xdt[0:L, hh * 64:(hh + 1) * 64],
                                                                              MT[0:L, 0:L], start=True, stop=False), reads=[xdt, MT], writes=[py])
                S.op("pe", lambda e, hh=hh, half=half, py=py: e.matmul(py[half * 64:(half + 1) * 64, 0:L], sbf(hh),
                                                                       Cdec[:, hh * 128:hh * 128 + L], start=False, stop=True),
                     reads=[skeys[1], Cdec], writes=[py])
            for m in range(2):
                S.op("dve", lambda e, m=m: e.scalar_tensor_tensor(out=Yg[:, m, cs], in0=cv[:, m, cs],
                                                                   scalar=pp[:, PP_DS + g * 2 + m:PP_DS + g * 2 + m + 1], in1=psy[m][:, 0:L],
                                                                   op0=ALU.mult, op1=ALU.add), reads=[("cv", m), pp, psy[m]], writes=[Yg])
            psc = P()
            S.op("pe", lambda e: e.matmul(psc[:, 0:256], B_tok[0:L, :], xdtw[0:L, :], start=True, stop=True), reads=[B_tok, xdtw], writes=[psc])
            for hh in range(4):
                S.op("dve", lambda e, hh=hh: e.scalar_tensor_tensor(out=sf(hh), in0=sf(hh),
                                                                     scalar=erow[:, hh * 128 + L - 1:hh * 128 + L], in1=psc[:, hh * 64:(hh + 1) * 64],
                                                                     op0=ALU.mult, op1=ALU.add), reads=[skeys[0], erow, psc], writes=[skeys[0]])
            for hh in range(4):
                S.op("act", lambda e, hh=hh: e.activation(out=sbf(hh), in_=sf(hh), func=AF.Copy), reads=[skeys[0]], writes=[skeys[1]])

        def dsa_prompt(st):
            S.alias("acc", ["mergedT"])
            S.op("dve", lambda e: e.memset(maskT[:, 0:4 * st + 4, :], 0.0), writes=["maskT"])
            R, lo, stp, cand, cnt, gg = [bis[:, i:i + 1] for i in range(6)]
            for qb in range(4):
                i = st * 4 + qb
                Nk = (i + 1) * 128
                qs = slice(qb * 128, (qb + 1) * 128)
                for h in range(8):
                    pair, half = h // 2, h % 2
                    hs = slice(half * 64, (half + 1) * 64)
                    for kt in range((Nk + 511) // 512):
                        n = min(512, Nk - kt * 512)
                        ps = P()
                        S.op("pe", lambda e, ps=ps, pair=pair, hs=hs, kt=kt, n=n: e.matmul(
                            ps[:, 0:n], qiT[hs, pair, qs], kiT2[hs, kt * 512:kt * 512 + n], start=True, stop=True),
                            reads=["qiT", kiT2], writes=[ps])
                        rr = rrs[(h + kt) % 2]
                        S.op("act", lambda e, ps=ps, rr=rr, n=n: e.activation(out=rr[:, 0:n], in_=ps[:, 0:n], func=AF.Relu), reads=[ps], writes=[rr])
                        if h == 0:
                            S.op("dve", lambda e, rr=rr, kt=kt, n=n: e.tensor_scalar(out=acc[:, kt * 512:kt * 512 + n], in0=rr[:, 0:n],
                                                                                     scalar1=witok[:, qb, 0:1], scalar2=None, op0=ALU.mult),
                                 reads=[rr, witok], writes=["acc"])
                        else:
                            S.op("dve", lambda e, rr=rr, kt=kt, n=n, h=h: e.scalar_tensor_tensor(
                                out=acc[:, kt * 512:kt * 512 + n], in0=rr[:, 0:n], scalar=witok[:, qb, h:h + 1],
                                in1=acc[:, kt * 512:kt * 512 + n], op0=ALU.mult, op1=ALU.add), reads=[rr, witok, "acc"], writes=["acc"])
                S.op("dve", lambda e: e.tensor_reduce(out=R, in_=acc[:, 0:Nk], axis=AX.X, op=ALU.max, apply_absolute_value=True),
                     reads=["acc"], writes=[bis])
                S.op("dve", lambda e: e.tensor_tensor(out=acc[:, i * 128:(i + 1) * 128], in0=acc[:, i * 128:(i + 1) * 128], in1=negm, op=ALU.add),
                     reads=["acc", cst], writes=["acc"])
                S.op("dve", lambda e: e.tensor_scalar(out=lo, in0=R, scalar1=-1.0, scalar2=None, op0=ALU.mult), reads=[bis], writes=[bis])
                if i >= 2:
                    for it in range(1, NIT + 1):
                        S.op("dve", lambda e, it=it: e.tensor_scalar(out=stp, in0=R, scalar1=2.0 ** (1 - it), scalar2=None, op0=ALU.mult),
                             reads=[bis], writes=[bis])
                        S.op("dve", lambda e: e.tensor_tensor(out=cand, in0=lo, in1=stp, op=ALU.add), reads=[bis], writes=[bis])
                        S.op("dve", lambda e: e.tensor_scalar(out=junk[:, 0:Nk], in0=acc[:, 0:Nk], scalar1=cand, scalar2=None,
                                                              op0=ALU.is_ge, op1=ALU.add, accum_out=cnt), reads=["acc", bis], writes=["mask01", bis])
                        S.op("dve", lambda e: e.tensor_scalar(out=gg, in0=cnt, scalar1=255.5, scalar2=None, op0=ALU.is_ge),
                             reads=[bis], writes=[bis])
                        S.op("dve", lambda e: e.scalar_tensor_tensor(out=lo, in0=gg, scalar=stp, in1=lo, op0=ALU.mult, op1=ALU.add),
                             reads=[bis], writes=[bis])
                S.op("dve", lambda e: e.tensor_scalar(out=mask01[:, 0:Nk], in0=acc[:, 0:Nk], scalar1=lo, scalar2=None, op0=ALU.is_ge),
                     reads=["acc", bis], writes=["mask01"])
                for sc0 in range(0, i + 1, 8):
                    n8 = min(8, i + 1 - sc0)
                    for j8 in range(n8):
                        S.op("pe", lambda e, j8=j8: e.transpose(PTS(j8), mask01[:, (sc0 + j8) * 128:(sc0 + j8 + 1) * 128], identb),
                             reads=["mask01", cstb], writes=[ptb])
                    S.op("act", lambda e: e.activation(out=maskT[:, sc0:sc0 + n8, qs], in_=ptb[:, 0:n8 * 128].rearrange("p (a b) -> p a b", a=n8),
                                                       func=AF.Copy), reads=[ptb], writes=["maskT"])
            nsc = 4 * st + 4
            for hp in range(8):
                for half in range(2):
                    h = HPERM[2 * hp + half]
                    g = h // 4
                    hs = slice(half * 64, (half + 1) * 64)
                    for sc in range(nsc):
                        ps = P()
                        S.op("pe", lambda e, ps=ps, hs=hs, g=g, sc=sc: e.matmul(ps[:, :ST], kT2[hs, g // 2, sc * 128:(sc + 1) * 128], qT[hs, hp, :],
                                                                                start=True, stop=True), reads=[kT2, "qT"], writes=[ps])
                        c.ei = (c.ei + 1) % 2
                        E = Es[c.ei]
                        S.op("act", lambda e, ps=ps, E=E: e.activation(out=E[:], in_=ps[:, :ST], func=AF.Exp, scale=0.125), reads=[ps], writes=[E])
                        S.op("dve", lambda e, E=E, sc=sc: e.tensor_tensor(out=E[:], in0=E[:], in1=maskT[:, sc, :], op=ALU.mult),
                             reads=[E, "maskT"], writes=[E])
                        S.op("pe", lambda e, E=E, hs=hs, g=g, sc=sc: e.matmul(accA[hs, :ST], vtok[:, sc, g * 64:(g + 1) * 64], E[:],
                                                                              start=(sc == 0), stop=(sc == nsc - 1)), reads=[vtok, E], writes=[accA])
                        S.op("pe", lambda e, E=E, hs=hs, sc=sc: e.matmul(accB[hs, :ST], onesb[:, 0:64], E[:],
                                                                         start=(sc == 0), stop=(sc == nsc - 1)), reads=[cstb, E], writes=[accB])
                S.op("dve", lambda e: e.reciprocal(out=rden[:], in_=accB[:, :ST]), reads=[accB], writes=[rden])
                S.op("dve", lambda e, hp=hp: e.tensor_tensor(out=yattT[:, hp, :], in0=accA[:, :ST], in1=rden[:], op=ALU.mult),
                     reads=[accA, rden], writes=["yattT"])

        def merge(T):
            win = W["w_in"]
            S.alias("mergedT", ["acc"])
            for k in range(KC):
                def gate(c0, sg):
                    buf, view = wload(win, 0, KC, c0 + k * 128, 128)
                    ps = P()
                    for kk in range(KC):
                        S.op("pe", lambda e, kk=kk, ps=ps, view=view: e.matmul(ps[:, :T], view[:, kk, :], uT[:, kk, :T], start=(kk == 0),
                                                                              stop=(kk == KC - 1)), reads=[buf, uT], writes=[ps])
                    S.op("act", lambda e, ps=ps: e.activation(out=sg[:, :T], in_=ps[:, :T], func=AF.Sigmoid), reads=[ps], writes=[sg])
                gate(O_GS, sgs[0])
                buf, view = wload(W["w_br_ssd"], 0, 16, k * 128, 128)
                ps1 = P()
                for kk in range(16):
                    S.op("pe", lambda e, kk=kk, view=view: e.matmul(ps1[:, :T], view[:, kk, :], yssdT[:, kk, :T], start=(kk == 0), stop=(kk == 15)),
                         reads=[buf, "yssdT"], writes=[ps1])
                S.op("dve", lambda e: e.tensor_tensor(out=tmpf[:, :T], in0=ps1[:, :T], in1=sgs[0][:, :T], op=ALU.mult),
                     reads=[ps1, sgs[0]], writes=[tmpf])
                gate(O_GA, sgs[1])
                buf2, view2 = wload(W["w_br_att"], 0, KC, k * 128, 128)
                ps2 = P()
                for kk in range(KC):
                    S.op("pe", lambda e, kk=kk, view2=view2: e.matmul(ps2[:, :T], view2[:, kk, :], yattT[:, kk, :T], start=(kk == 0), stop=(kk == KC - 1)),
                         reads=[buf2, "yattT"], writes=[ps2])
                S.op("dve", lambda e: e.tensor_tensor(out=sqb[:, :T], in0=ps2[:, :T], in1=sgs[1][:, :T], op=ALU.mult),
                     reads=[ps2, sgs[1]], writes=[sqb])
                S.op("dve", lambda e, k=k: e.tensor_tensor(out=mergedT[:, k, :T], in0=tmpf[:, :T], in1=sqb[:, :T], op=ALU.add),
                     reads=[tmpf, sqb], writes=["mergedT"])
            S.alias("ybuf", SCRK)
            proj_fm(W["w_out"], 0, KC, 0, D, lambda k: mergedT[:, k, :T], ["mergedT"], T,
                    lambda idx, ps, ms: S.op("act", lambda e: e.activation(out=ybuf[:, idx, :T], in_=ps[:, :T], func=AF.Copy), reads=[ps], writes=["ybuf"]))
            postnorm_add("mix_post_g", T, 1.0)

        def mem_attn(T, batches):
            prenorm("mem_pre_g", T)
            proj_fm(W["w_mq"], 0, KC, 0, D, lambda k: uT[:, k, :T], [uT], T,
                    lambda idx, ps, ms: copy_alt(idx, qT[:, idx, :T], ps[:, :T], [ps], ["qT"]))
            for cs, bsel in batches:
                nT = cs.stop - cs.start
                if bsel is not None:
                    S.dma("mkl", lambda e: e.dma_start(out=mkTb[:], in_=cmkT_d[bsel].rearrange("(k p) m -> p k m", p=128)), writes=[mkTb], queue="pool")
                    S.dma("mvl", lambda e: e.dma_start(out=mvb[:], in_=cmv_d[bsel].rearrange("(k p) c -> p k c", p=128)), writes=[mvb], queue="pool")
                for h in range(4):
                    Em = []
                    for mc in range(2):
                        ps = P()
                        for dc in range(2):
                            S.op("pe", lambda e, ps=ps: e.matmul(ps[:, :nT], mkTb[:, h * 2 + dc, mc * 128:(mc + 1) * 128], qT[:, h * 2 + dc, cs],
                                                                 start=(dc == 0), stop=(dc == 1)), reads=[mkTb, "qT"], writes=[ps])
                        E = Es[mc]
                        S.op("act", lambda e, ps=ps, E=E: e.activation(out=E[:, :nT], in_=ps[:, :nT], func=AF.Exp, scale=1.0 / 16.0), reads=[ps], writes=[E])
                        Em.append(E)
                    for mc in range(2):
                        S.op("pe", lambda e: e.matmul(accB[:, :nT], onesb, Em[mc][:, :nT], start=(mc == 0), stop=(mc == 1)), reads=[cstb, Em[mc]], writes=[accB])
                    S.op("dve", lambda e: e.reciprocal(out=rden[:, :nT], in_=accB[:, :nT]), reads=[accB], writes=[rden])
                    for dc in range(2):
                        for mc in range(2):
                            S.op("pe", lambda e: e.matmul(accA[:, :nT], mvb[:, mc, h * 256 + dc * 128:h * 256 + (dc + 1) * 128], Em[mc][:, :nT],
                                                          start=(mc == 0), stop=(mc == 1)), reads=[mvb, Em[mc]], writes=[accA])
                        S.op("dve", lambda e: e.tensor_tensor(out=yattT[:, h * 2 + dc, cs], in0=accA[:, :nT], in1=rden[:, :nT], op=ALU.mult),
                             reads=[accA, rden], writes=["yattT"])
            proj_fm(W["w_mo"], 0, KC, 0, D, lambda k: yattT[:, k, :T], ["yattT"], T,
                    lambda idx, ps, ms: S.op("act", lambda e: e.activation(out=ybuf[:, idx, :T], in_=ps[:, :T], func=AF.Copy), reads=[ps], writes=["ybuf"]))
            postnorm_add("mem_post_g", T, 1.0)

        def memory_kv_prompt():
            S.dma("xin", lambda e: e.dma_start(out=hT[:, :, 0:256], in_=memT.rearrange("(k p) t -> p k t", p=128)), writes=[hT])
            prenorm("mem_kv_g", 256)

            def cons(idx, ps, ms):
                S.op("dve", lambda e: e.tensor_copy(out=mkTb[:, idx, :], in_=ps[:, 0:256]), reads=[ps], writes=[mkTb])
                stage_out(ps[:, 0:256], 128, 256, mkT_o[idx * 128:(idx + 1) * 128, :], (ps, "mkT_o"))
            proj_fm(W["w_mk"], 0, KC, 0, D, lambda k: uT[:, k, 0:256], [uT], 256, cons)
            for cb in range(2):
                buf, view = wload(W["w_mv"], 0, KC, cb * 512, 512)
                for mc in range(2):
                    ps = P()
                    tm_proj(view, buf, 0, 512, mc, ps)
                    S.op("dve", lambda e, mc=mc, cb=cb, ps=ps: e.tensor_copy(out=mvb[:, mc, cb * 512:(cb + 1) * 512], in_=ps[:, :]), reads=[ps], writes=[mvb])
                    stage_out(ps[:, :], 128, 512, mv_o[mc * 128:(mc + 1) * 128, cb * 512:(cb + 1) * 512], (ps, "mv_o"))


        def sample_path():
            T = 16
            cst2 = S.sb([128, C2N], F32, "cst2")
            S.dma("cst2", lambda e: e.dma_start(out=cst2[:], in_=cst2_d), writes=[cst2])
            Gblk = cst2[:, 0:128]
            Gsel = cst2[:, 128:132]
            negnew = cst2[:, 132:136]
            pmod64 = cst2[:, 136:137]
            Rep = cst2[0:4, 137:265]
            Dsel = cst2[0:32, 265:269]
            hsel = cst2[0:8, 269:301]
            onesf = cst[:, 384:512]
            kibs = [kT2[:].rearrange("p a b -> p (a b)")[:, i * 2048:(i + 1) * 2048].bitcast(F32) for i in range(2)]
            vt_f = vtok[:].rearrange("p a b -> p (a b)").bitcast(F32)
            Ksel = vt_f[:, 0:1024].rearrange("p (c d) -> p c d", c=4)
            Vsel = vt_f[:, 1024:2048].rearrange("p (c d) -> p c d", c=4)
            qrow = kiT2[:].bitcast(F32)
            prodb = mask01t[:].bitcast(F32)
            kv4 = stT[:].rearrange("p g c -> p (g c)")[0:4, :].rearrange("p (b c) -> p b c", b=4)
            for nk_, ok_ in (("kib0", kT2), ("kib1", kT2), ("KV", vtok), ("qrow", kiT2), ("prodb", mask01t), ("kv4", stT)):
                S.alias(nk_, [ok_])
            hists = S.sb([128, 32, 4, 3], F32, "hists")
            S.dma("hists", lambda e: e.dma_start(out=hists[:].rearrange("p a b c -> p (a b c)"), in_=hists_d), writes=[hists])
            pres = S.sb([128, 4, 4, 7], F32, "pres")
            stS = [S.sb([128, 256], F32, f"stS{i}") for i in range(2)]
            stSb = [S.sb([128, 256], BF16, f"stSb{i}") for i in range(2)]
            sc = S.sb([128, 516], F32, "sc")
            work = S.sb([128, 512], F32, "work")
            qiU = S.sb([128, 4, 32], F32, "qiU")
            qiE = S.sb([128, 4, 32], F32, "qiE")
            qiO = S.sb([128, 4, 32], F32, "qiO")
            kiTs = S.sb([64, 16], F32, "kiTs")
            wiT = S.sb([8, 16], F32, "wiT")
            WW = S.sb([32, 252], F32, "WW")
            ptrow = S.sb([128, 128], I32, "ptrow")
            ptf = S.sb([128, 128], F32, "ptf")
            kix = S.sb([128, 128], I32, "kix")
            ptji = S.sb([128, 4], I32, "ptji")
            smf = S.sb([128, 256], F32, "smf")
            smi = S.sb([128, 96], I32, "smi")
            rowi = S.sb([128, 32], I32, "rowi")
            opart = S.sb([128, 1024], F32, "opart")
            q4 = opart[0:4, :]
            sE = S.sb([128, 64], F32, "sE")
            vals = smf[:, 0:32]
            valid = smf[:, 32:64]
            validn = smf[:, 64:68]
            wicol = smf[0:32, 68:69]
            rmax = smf[:, 69:70]
            Rr, lo, stp, cand, cnt, gg = [smf[:, 70 + i:71 + i] for i in range(6)]
            ptjf = smf[:, 76:80]
            pglf = smf[:, 80:112]
            offf = smf[:, 112:144]
            physf = smf[:, 144:176]
            tmp32 = smf[:, 176:208]
            denp = smf[:, 208:224]
            tmp16 = smf[:, 224:240]
            rd4 = smf[:, 240:244]
            rn4 = smf[0:32, 244:248]
            S.op("dve", lambda e: e.memset(WW[:], 0.0), writes=[WW])

            S.dma("xin", lambda e: e.dma_start(out=hT[:, :, 0:T], in_=xsT.rearrange("(k p) t -> p k t", p=128)), writes=[hT])
            ffn("ffn1", T)
            for k_ in MIXK:
                S.alias(k_, ["hid"])
            for k_ in SCRK:
                S.alias(k_, ["ybuf"])
            S.alias("ssdtmp", ["maskT"])
            prenorm("mix_pre_g", T)
            win = W["w_in"]
            buf, view = wload(win, 0, KC, O_K, 512)
            for b in range(4):
                ps = P()
                for k in range(KC):
                    S.op("pe", lambda e, k=k, ps=ps, view=view: e.matmul(ps[0:4, 0:512], uT[:, k, b * 4:(b + 1) * 4], view[:, k, :],
                                                                        start=(k == 0), stop=(k == KC - 1)), reads=[buf, uT], writes=[ps])
                S.op("act", lambda e, ps=ps: e.activation(out=kv4[:, b, :], in_=ps[0:4, 0:512], func=AF.Copy), reads=[ps], writes=["kv4"])
                S.dma("kso", lambda e: e.dma_start(out=ks_o[b * 4:(b + 1) * 4, :], in_=kv4[:, b, 0:256]), reads=["kv4"], writes=["ks_o"])
                S.dma("vso", lambda e: e.dma_start(out=vs_o[b * 4:(b + 1) * 4, :], in_=kv4[:, b, 256:512]), reads=["kv4"], writes=["vs_o"])
            buf, view = wload(win, 0, KC, O_KI - 32, 128)
            ps = P()
            for k in range(KC):
                S.op("pe", lambda e, k=k, ps=ps, view=view: e.matmul(ps[0:64, 0:T], view[:, k, 32:96], uT[:, k, :T], start=(k == 0), stop=(k == KC - 1)),
                     reads=[buf, uT], writes=[ps])
            S.op("act", lambda e, ps=ps: e.activation(out=kiTs[:], in_=ps[0:64, 0:T], func=AF.Copy), reads=[ps], writes=[kiTs])
            S.dma("kiso", lambda e: e.dma_start(out=kisT_o, in_=kiTs[:]), reads=[kiTs], writes=["kisT_o"])
            ps = P()
            for k in range(KC):
                S.op("pe", lambda e, k=k, ps=ps, view=view: e.matmul(ps[0:8, 0:T], view[:, k, 96:104], uT[:, k, :T], start=(k == 0), stop=(k == KC - 1)),
                     reads=[buf, uT], writes=[ps])
            S.op("act", lambda e, ps=ps: e.activation(out=wiT[:], in_=ps[0:8, 0:T], func=AF.Copy), reads=[ps], writes=[wiT])
            buf, view = wload(win, 0, KC, O_DT, 128)
            for b in range(4):
                ps = P()
                for k in range(KC):
                    S.op("pe", lambda e, k=k, ps=ps, view=view: e.matmul(ps[0:4, 0:32], uT[:, k, b * 4:(b + 1) * 4], view[:, k, 0:32],
                                                                        start=(k == 0), stop=(k == KC - 1)), reads=[buf, uT], writes=[ps])
                S.op("dve", lambda e, ps=ps: e.tensor_tensor(out=dtt[0:4, b, :], in0=ps[0:4, 0:32], in1=rowp[0:4, 0:32], op=ALU.add),
                     reads=[ps, rowp], writes=[dtt])
            S.op("act", lambda e: e.activation(out=dtt[0:4], in_=dtt[0:4], func=AF.Exp), reads=[dtt], writes=[dtt])
            S.op("act", lambda e: e.activation(out=dtt[0:4], in_=dtt[0:4], func=AF.Ln, bias=1.0), reads=[dtt], writes=[dtt])
            for b in range(4):
                S.op("dve", lambda e: e.tensor_tensor(out=dta[0:4, b, :], in0=dtt[0:4, b, :], in1=arow[0:4, :], op=ALU.mult),
                     reads=[dtt, arow], writes=[dta])
                ps = P()
                S.op("pe", lambda e, ps=ps: e.matmul(ps[0:4, 0:32], Uf[0:4, 0:4], dta[0:4, b, :], start=True, stop=True), reads=[cst, dta], writes=[ps])
                S.op("act", lambda e, ps=ps: e.activation(out=acsc[0:4, b, :], in_=ps[0:4, 0:32], func=AF.Copy), reads=[ps], writes=[acsc])
            buf, view = wload(win, 0, KC, O_QI, 512)
            ps = P()
            for h in range(8):
                for half in range(2):
                    for k in range(KC):
                        S.op("pe", lambda e, k=k, ps=ps, view=view: e.matmul(ps[half * 64:(half + 1) * 64, h * 16:(h + 1) * 16], view[:, k, h * 64:(h + 1) * 64],
                                                                            uT[:, k, :T], start=(k == 0), stop=(k == KC - 1)), reads=[buf, uT], writes=[ps])
            S.op("act", lambda e, ps=ps: e.activation(out=qiU[:].rearrange("p b (h t) -> p h b t", h=8), in_=ps[:, 0:128].rearrange("p (h b t) -> p h b t", h=8, b=4),
                                                    func=AF.Copy), reads=[ps], writes=[qiU])
            S.op("dve", lambda e: e.memset(qiE[:], 0.0), writes=[qiE])
            S.op("dve", lambda e: e.memset(qiO[:], 0.0), writes=[qiO])
            S.op("dve", lambda e: e.tensor_copy(out=qiE[0:64], in_=qiU[0:64]), reads=[qiU], writes=[qiE])
            S.op("dve", lambda e: e.tensor_copy(out=qiO[64:128], in_=qiU[64:128]), reads=[qiU], writes=[qiO])
            for g in range(8):
                buf, view = wload(win, 0, KC, O_Z + g * 256, 256)
                for m in range(2):
                    ps = P()
                    for k in range(KC):
                        S.op("pe", lambda e, k=k, ps=ps, view=view: e.matmul(ps[:, :T], view[:, k, m * 128:(m + 1) * 128], uT[:, k, :T],
                                                                            start=(k == 0), stop=(k == KC - 1)), reads=[buf, uT], writes=[ps])
                    S.op("act", lambda e, ps=ps: e.activation(out=zs[:, m, :T], in_=ps[:, :T], func=AF.Silu), reads=[ps], writes=["zs"])
                chs = [g * 2, g * 2 + 1, 16 + g, 24 + g]
                for j in range(4):
                    if j == 0:
                        bufx, viewx = wload(win, 0, KC, O_X + g * 256, 256)
                    if j == 2:
                        bufx, viewx = wload(win, 0, KC, O_B + g * 128, 128)
                    if j == 3:
                        bufx, viewx = wload(win, 0, KC, O_C + g * 128, 128)
                    c0 = 128 if j == 1 else 0
                    ps = P()
                    for k in range(KC):
                        S.op("pe", lambda e, k=k, ps=ps, viewx=viewx: e.matmul(ps[:, :T], viewx[:, k, c0:c0 + 128], uT[:, k, :T],
                                                                              start=(k == 0), stop=(k == KC - 1)), reads=[bufx, uT], writes=[ps])
                    ch = chs[j]
                    S.op("dve", lambda e: e.tensor_copy(out=pres[:, j, :, 0:3], in_=hists[:, ch, :, :]), reads=[hists], writes=[pres])
                    S.op("act", lambda e, ps=ps: e.activation(out=pres[:, j, :, 3:7], in_=ps[:, 0:T].rearrange("p (b t) -> p b t", b=4), func=AF.Copy),
                         reads=[ps], writes=[pres])
                    S.op("dve", lambda e: e.tensor_copy(out=hists[:, ch, :, :], in_=pres[:, j, :, 4:7]), reads=[pres], writes=[hists])
                    cwf = lambda tap: pp[:, PP_CW + ch * 4 + tap:PP_CW + ch * 4 + tap + 1]
                    tv = tmpf[:, 0:T].rearrange("p (b t) -> p b t", b=4)
                    S.op("dve", lambda e: e.tensor_scalar(out=tv, in0=pres[:, j, :, 0:4], scalar1=cwf(0), scalar2=pp[:, PP_CB + ch:PP_CB + ch + 1],
                                                          op0=ALU.mult, op1=ALU.add), reads=[pres, pp], writes=[tmpf])
                    for tap in range(1, 4):
                        S.op("dve", lambda e: e.scalar_tensor_tensor(out=tv, in0=pres[:, j, :, tap:tap + 4], scalar=cwf(tap), in1=tv,
                                                                     op0=ALU.mult, op1=ALU.add), reads=[pres, pp, tmpf], writes=[tmpf])
                    S.op("act", lambda e: e.activation(out=cv[:, j, :T], in_=tmpf[:, :T], func=AF.Silu), reads=[tmpf], writes=[("cv", j)])
                for b in range(4):
                    si = (g * 4 + b) % 2
                    sF, sB = stS[si], stSb[si]
                    S.dma(("sts", si), lambda e: e.dma_start(out=sF[:], in_=ssmsT_d[b, :, g * 256:(g + 1) * 256]), writes=[sF])
                    S.op("act", lambda e: e.activation(out=sB[:], in_=sF[:], func=AF.Copy), reads=[sF], writes=[sB])
                    ssd_chunk(g, b, 4, b * 4, sf=lambda hh, sF=sF: sF[:, hh * 64:(hh + 1) * 64], sbf=lambda hh, sB=sB: sB[:, hh * 64:(hh + 1) * 64],
                              skeys=(sF, sB))
                    S.dma(("sto", si), lambda e: e.dma_start(out=ssms_o[b, :, g * 256:(g + 1) * 256], in_=sF[:]), reads=[sF], writes=["ssms_o"])
                for m in range(2):
                    S.op("dve", lambda e: e.tensor_tensor(out=Yg[:, m, :T], in0=Yg[:, m, :T], in1=zs[:, m, :T], op=ALU.mult),
                         reads=[Yg, "zs"], writes=[Yg])
                norm_stats([Yg[:, 0, :T], Yg[:, 1, :T]], Yg, T, 256.0)
                for m in range(2):
                    S.op("dve", lambda e: e.scalar_tensor_tensor(out=yssdT[:, g * 2 + m, :T], in0=Yg[:, m, :T],
                                                                 scalar=pp[:, PP_NG + g * 2 + m:PP_NG + g * 2 + m + 1], in1=rstd[:, :T],
                                                                 op0=ALU.mult, op1=ALU.mult), reads=[Yg, rstd, pp], writes=["yssdT"])
            S.dma("convso", lambda e: e.dma_start(out=convs_o, in_=hists[:].rearrange("p a b c -> p (a b c)")), reads=[hists], writes=["convs_o"])
            for b in range(4):
                cs = slice(b * 4, b * 4 + 4)
                S.dma("ptrow", lambda e: e.dma_start(out=ptrow[:], in_=ptab_d[b:b + 1, :].partition_broadcast(128)), writes=[ptrow])
                S.op("dve", lambda e: e.tensor_copy(out=ptf[:], in_=ptrow[:]), reads=[ptrow], writes=[ptf])
                t64 = work[:, 0:64]
                S.op("dve", lambda e: e.tensor_tensor(out=t64, in0=ptf[:, 1:128:2], in1=ptf[:, 0:127:2], op=ALU.subtract), reads=[ptf], writes=[work])
                S.op("dve", lambda e: e.scalar_tensor_tensor(out=t64, in0=t64, scalar=cst2[:, 941:942], in1=ptf[:, 0:127:2], op0=ALU.mult, op1=ALU.add),
                     reads=[work, ptf, cst2], writes=[work])
                S.op("dve", lambda e: e.tensor_scalar(out=t64, in0=t64, scalar1=64.0, scalar2=pmod64, op0=ALU.mult, op1=ALU.add),
                     reads=[work, cst2], writes=[work])
                S.op("dve", lambda e: e.tensor_copy(out=kix[:, 0:64], in_=t64), reads=[work], writes=[kix])
                for par in range(2):
                    S.dma("ptji", lambda e: e.dma_start(out=ptji[par * 16:(par + 1) * 16, :], in_=ptab_d[b, :].rearrange("(i e) -> i e", e=8)[:, par:par + 7:2],
                                                        allow_slow_non_contiguous=True), writes=[ptji])
                S.op("dve", lambda e: e.tensor_copy(out=tmp16[0:32, 0:4], in_=ptji[0:32, :]), reads=[ptji], writes=[smf])
                ps = P()
                S.op("pe", lambda e, ps=ps: e.matmul(ps[:, 0:4], cst2[0:32, 813:941], tmp16[0:32, 0:4], start=True, stop=True), reads=[cst2, smf], writes=[ps])
                S.op("dve", lambda e, ps=ps: e.tensor_copy(out=ptjf, in_=ps[:, 0:4]), reads=[ps], writes=[smf])
                ps = P()
                S.op("pe", lambda e, ps=ps: e.matmul(ps[0:32, 0:4], hsel, wiT[0:8, cs], start=True, stop=True), reads=[cst2, wiT], writes=[ps])
                S.op("dve", lambda e, ps=ps: e.tensor_tensor(out=rn4, in0=ps[0:32, 0:4], in1=Dsel, op=ALU.mult), reads=[ps, cst2], writes=[smf])
                S.op("dve", lambda e: e.tensor_reduce(out=wicol, in_=rn4, axis=AX.X, op=ALU.add), reads=[smf], writes=[smf])
                S.op("dve", lambda e: e.tensor_scalar(out=WW[:, 124:128], in0=Dsel, scalar1=wicol, scalar2=None, op0=ALU.mult), reads=[smf, cst2], writes=[WW])
                qil = qiU[0:64, b, :]
                nmm = 0
                for i8 in range(16):
                    kb = kibs[i8 % 2]
                    kkey = f"kib{i8 % 2}"
                    for c4 in range(4):
                        pq = i8 * 4 + c4
                        S.dma(("kibd", i8 % 2), lambda e: e.indirect_dma_start(
                            out=kb[:, c4 * 128:(c4 + 1) * 128], out_offset=None, in_=kidxT_d,
                            in_offset=bass.IndirectOffsetOnAxis(ap=kix[:, pq:pq + 1], axis=0)), reads=[kix], writes=[kkey], queue="pool")
                    for par in range(2):
                        jv = par * 16 + i8
                        ps = P()
                        S.op("pe", lambda e, ps=ps: e.matmul(ps[0:32, 0:512], (qiE if par == 0 else qiO)[:, b, :], kb[:, 0:512], start=True, stop=True),
                             reads=[qiE, qiO, kkey], writes=[ps])
                        rr = rrs[nmm % 2]
                        S.op("act", lambda e, ps=ps: e.activation(out=rr[0:32, :], in_=ps[0:32, 0:512], func=AF.Relu), reads=[ps], writes=[rr])
                        S.op("pe", lambda e: e.matmul(accA[:, 0:512], WW[:, 124 - 4 * jv:252 - 4 * jv], rr[0:32, :], start=(nmm == 0), stop=(nmm == 31)),
                             reads=[WW, rr], writes=[accA])
                        nmm += 1
                ps = P()
                S.op("pe", lambda e, ps=ps: e.matmul(ps[0:32, 0:4], qil, kiTs[:, cs], start=True, stop=True), reads=[qiU, kiTs], writes=[ps])
                S.op("act", lambda e, ps=ps: e.activation(out=rn4, in_=ps[0:32, 0:4], func=AF.Relu), reads=[ps], writes=[smf])
                S.op("pe", lambda e: e.matmul(accB[:, 0:4], WW[:, 124:252], rn4, start=True, stop=True), reads=[WW, smf], writes=[accB])
                S.op("act", lambda e: e.activation(out=sc[:, 0:512], in_=accA[:, 0:512], func=AF.Copy), reads=[accA], writes=[sc])
                S.op("dve", lambda e: e.tensor_tensor(out=sc[:, 512:516], in0=accB[:, 0:4], in1=negnew, op=ALU.add), reads=[accB, cst2], writes=[sc])
                S.op("dve", lambda e: e.tensor_reduce(out=rmax, in_=sc[:, 0:512], axis=AX.X, op=ALU.max, apply_absolute_value=True), reads=[sc], writes=[smf])
                ps = P()
                S.op("pe", lambda e, ps=ps: e.matmul(ps[:, 0:1], onesf, rmax, start=True, stop=True), reads=[cst, smf], writes=[ps])
                S.op("dve", lambda e, ps=ps: e.tensor_copy(out=Rr, in_=ps[:, 0:1]), reads=[ps], writes=[smf])
                S.op("dve", lambda e: e.tensor_scalar(out=lo, in0=Rr, scalar1=-1.0, scalar2=None, op0=ALU.mult), reads=[smf], writes=[smf])
                for it in range(1, NITS + 1):
                    S.op("dve", lambda e: e.tensor_scalar(out=stp, in0=Rr, scalar1=2.0 ** (1 - it), scalar2=None, op0=ALU.mult), reads=[smf], writes=[smf])
                    S.op("dve", lambda e: e.tensor_tensor(out=cand, in0=lo, in1=stp, op=ALU.add), reads=[smf], writes=[smf])
                    S.op("dve", lambda e: e.tensor_scalar(out=work[:, 0:516 - 4], in0=sc[:, 0:512], scalar1=cand, scalar2=None, op0=ALU.is_ge,
                                                          op1=ALU.add, accum_out=cnt), reads=[sc, smf], writes=[work, smf])
                    S.op("dve", lambda e: e.tensor_scalar(out=tmp16[:, 0:4], in0=sc[:, 512:516], scalar1=cand, scalar2=None, op0=ALU.is_ge,
                                                          op1=ALU.add, accum_out=gg), reads=[sc, smf], writes=[smf])
                    S.op("dve", lambda e: e.tensor_tensor(out=cnt, in0=cnt, in1=gg, op=ALU.add), reads=[smf], writes=[smf])
                    ps = P()
                    S.op("pe", lambda e, ps=ps: e.matmul(ps[:, 0:1], Gblk, cnt, start=True, stop=True), reads=[cst2, smf], writes=[ps])
                    S.op("dve", lambda e, ps=ps: e.tensor_scalar(out=gg, in0=ps[:, 0:1], scalar1=255.5, scalar2=None, op0=ALU.is_ge), reads=[ps], writes=[smf])
                    S.op("dve", lambda e: e.scalar_tensor_tensor(out=lo, in0=gg, scalar=stp, in1=lo, op0=ALU.mult, op1=ALU.add), reads=[smf], writes=[smf])
                S.op("dve", lambda e: e.tensor_copy(out=work[:], in_=sc[:, 0:512]), reads=[sc], writes=[work])
                cidx = smi[:, 0:32]
                for r in range(4):
                    S.op("dve", lambda e: e.max(out=vals[:, r * 8:(r + 1) * 8], in_=work[:]), reads=[work], writes=[smf])
                    S.op("dve", lambda e: e.max_index(out=cidx[:, r * 8:(r + 1) * 8].bitcast(U32), in_max=vals[:, r * 8:(r + 1) * 8], in_values=work[:]),
                         reads=[work, smf], writes=[smi])
                    S.op("dve", lambda e: e.match_replace(out=work[:], in_to_replace=vals[:, r * 8:(r + 1) * 8], in_values=work[:], imm_value=-3e38),
                         reads=[work, smf], writes=[work])
                S.op("dve", lambda e: e.tensor_scalar(out=valid, in0=vals, scalar1=lo, scalar2=None, op0=ALU.is_ge), reads=[smf], writes=[smf])
                S.op("dve", lambda e: e.tensor_scalar(out=validn, in0=sc[:, 512:516], scalar1=lo, scalar2=None, op0=ALU.is_ge), reads=[smf, sc], writes=[smf])
                S.op("dve", lambda e: e.tensor_scalar(out=smi[:, 32:64], in0=cidx, scalar1=7, scalar2=None, op0=ALU.arith_shift_right), reads=[smi], writes=[smi])
                S.op("dve", lambda e: e.tensor_scalar(out=smi[:, 64:96], in0=cidx, scalar1=127, scalar2=None, op0=ALU.bitwise_and), reads=[smi], writes=[smi])
                S.op("dve", lambda e: e.tensor_copy(out=pglf, in_=smi[:, 32:64]), reads=[smi], writes=[smf])
                S.op("dve", lambda e: e.tensor_copy(out=offf, in_=smi[:, 64:96]), reads=[smi], writes=[smf])
                for kq in range(4):
                    dst = physf if kq == 0 else tmp32
                    S.op("dve", lambda e: e.tensor_scalar(out=dst, in0=pglf, scalar1=float(kq), scalar2=ptjf[:, PGP[kq]:PGP[kq] + 1], op0=ALU.is_equal, op1=ALU.mult),
                         reads=[smf], writes=[smf])
                    if kq > 0:
                        S.op("dve", lambda e: e.tensor_tensor(out=physf, in0=physf, in1=tmp32, op=ALU.add), reads=[smf], writes=[smf])
                S.op("dve", lambda e: e.scalar_tensor_tensor(out=physf, in0=physf, scalar=128.0, in1=offf, op0=ALU.mult, op1=ALU.add), reads=[smf], writes=[smf])
                S.op("dve", lambda e: e.tensor_copy(out=rowi[:], in_=physf), reads=[smf], writes=[rowi])
                for half in range(2):
                    buf, view = wload(win, 0, KC, O_Q + half * 512, 512)
                    ps = P()
                    for k in range(KC):
                        S.op("pe", lambda e, k=k, ps=ps, view=view: e.matmul(ps[0:4, 0:512], uT[:, k, cs], view[:, k, :], start=(k == 0), stop=(k == KC - 1)),
                             reads=[buf, uT], writes=[ps])
                    S.op("act", lambda e, ps=ps: e.activation(out=q4[:, half * 512:(half + 1) * 512], in_=ps[0:4, 0:512], func=AF.Copy), reads=[ps], writes=[opart])
                    ps = P()
                    S.op("pe", lambda e, ps=ps: e.matmul(ps[:, 0:512], Rep, q4[:, half * 512:(half + 1) * 512], start=True, stop=True), reads=[cst2, opart], writes=[ps])
                    S.op("act", lambda e, ps=ps: e.activation(out=qrow[:, half * 512:(half + 1) * 512], in_=ps[:, 0:512], func=AF.Copy, scale=0.125),
                         reads=[ps], writes=["qrow"])
                qrow3 = qrow.rearrange("p (i d) -> p i d", i=16)
                opart3 = opart[:].rearrange("p (i d) -> p i d", i=16)
                S.op("dve", lambda e: e.memset(opart[:], 0.0), writes=[opart])
                S.op("dve", lambda e: e.memset(denp, 0.0), writes=[smf])
                s_all = sE[:]
                s_all3 = sE[:].rearrange("p (c i) -> p c i", c=4)
                sT3 = sE[:].rearrange("p (c i) -> p i c", c=4)
                q4acc = work[:, 0:256].rearrange("p (r d) -> p r d", r=4)
                prodK = prodb[:, 0:1024].rearrange("p (c r d) -> p c r d", c=4, r=4)
                prodV = prodb[:, 0:1024].rearrange("p (r d c) -> p r d c", r=4, d=64)
                for rnd in range(9):
                    if rnd < 8:
                        for c4 in range(4):
                            cc_ = rnd * 4 + c4
                            S.dma("ksel", lambda e: e.indirect_dma_start(out=Ksel[:, c4, :], out_offset=None, in_=poolk_d,
                                                                         in_offset=bass.IndirectOffsetOnAxis(ap=rowi[:, cc_:cc_ + 1], axis=0)),
                                  reads=[rowi], writes=["KV"], queue="pool")
                            S.dma("vsel", lambda e: e.indirect_dma_start(out=Vsel[:, c4, :], out_offset=None, in_=poolv_d,
                                                                         in_offset=bass.IndirectOffsetOnAxis(ap=rowi[:, cc_:cc_ + 1], axis=0)),
                                  reads=[rowi], writes=["KV"], queue="pool")
                        vmask = valid[:, rnd * 4:(rnd + 1) * 4]
                    else:
                        for tq in range(4):
                            ps = P()
                            S.op("pe", lambda e, ps=ps: e.matmul(ps[:, 0:512], cst2[0:4, 301 + tq * 128:301 + (tq + 1) * 128], kv4[:, b, :], start=True, stop=True),
                                 reads=[cst2, "kv4"], writes=[ps])
                            S.op("act", lambda e, ps=ps: e.activation(out=Ksel[:, tq, :], in_=ps[:, 0:256], func=AF.Copy), reads=[ps], writes=["KV"])
                            S.op("act", lambda e, ps=ps: e.activation(out=Vsel[:, tq, :], in_=ps[:, 256:512], func=AF.Copy), reads=[ps], writes=["KV"])
                        vmask = validn
                    for g in range(4):
                        base = (g // 2) * 8 + (g % 2)
                        S.op("dve", lambda e: e.tensor_tensor(out=prodK, in0=Ksel[:, :, g * 64:(g + 1) * 64].unsqueeze(2).to_broadcast([128, 4, 4, 64]),
                                                              in1=qrow3[:, base:base + 7:2, :].unsqueeze(1).to_broadcast([128, 4, 4, 64]), op=ALU.mult),
                             reads=["KV", "qrow"], writes=["prodb"])
                        S.op("dve", lambda e: e.tensor_reduce(out=s_all3[:, :, base:base + 7:2], in_=prodK, axis=AX.X, op=ALU.add), reads=["prodb"], writes=[sE])
                    S.op("act", lambda e: e.activation(out=s_all, in_=s_all, func=AF.Exp), reads=[sE], writes=[sE])
                    S.op("dve", lambda e: e.tensor_tensor(out=s_all3, in0=s_all3, in1=vmask.unsqueeze(2).to_broadcast([128, 4, 16]), op=ALU.mult),
                         reads=[sE, smf], writes=[sE])
                    S.op("dve", lambda e: e.tensor_reduce(out=tmp16, in_=sT3, axis=AX.X, op=ALU.add), reads=[sE], writes=[smf])
                    S.op("dve", lambda e: e.tensor_tensor(out=denp, in0=denp, in1=tmp16, op=ALU.add), reads=[smf], writes=[smf])
                    for g in range(4):
                        base = (g // 2) * 8 + (g % 2)
                        S.op("dve", lambda e: e.tensor_tensor(out=prodV, in0=sT3[:, base:base + 7:2, :].unsqueeze(2).to_broadcast([128, 4, 64, 4]),
                                                              in1=Vsel[:, :, g * 64:(g + 1) * 64].rearrange("p c d -> p d c").unsqueeze(1).to_broadcast([128, 4, 64, 4]),
                                                              op=ALU.mult), reads=[sE, "KV"], writes=["prodb"])
                        S.op("dve", lambda e: e.tensor_reduce(out=q4acc, in_=prodV, axis=AX.X, op=ALU.add), reads=["prodb"], writes=[work])
                        S.op("dve", lambda e: e.tensor_tensor(out=opart3[:, base:base + 7:2, :], in0=opart3[:, base:base + 7:2, :], in1=q4acc, op=ALU.add),
                             reads=[work, opart], writes=[opart])
                ps = P()
                S.op("pe", lambda e, ps=ps: e.matmul(ps[:, 0:16], Gblk, denp, start=True, stop=True), reads=[cst2, smf], writes=[ps])
                S.op("dve", lambda e, ps=ps: e.reciprocal(out=tmp16, in_=ps[:, 0:16]), reads=[ps], writes=[smf])
                S.op("dve", lambda e: e.tensor_tensor(out=opart3, in0=opart3, in1=tmp16.unsqueeze(2).to_broadcast([128, 16, 64]), op=ALU.mult),
                     reads=[opart, smf], writes=[opart])
                for hp in range(8):
                    ps = P()
                    S.op("pe", lambda e, ps=ps: e.matmul(ps[:, 0:4], opart[:, hp * 128:(hp + 1) * 128], Gsel, start=True, stop=True), reads=[opart, cst2], writes=[ps])
                    S.op("dve", lambda e, ps=ps: e.tensor_copy(out=yattT[:, hp, cs], in_=ps[:, 0:4]), reads=[ps], writes=["yattT"])
            merge(T)
            mem_attn(T, [(slice(b * 4, b * 4 + 4), b) for b in range(4)])
            ffn("ffn2", T)
            S.dma("yout", lambda e: e.dma_start(out=ysT_o.rearrange("(k p) t -> p k t", p=128), in_=hT[:, :, 0:T]), reads=[hT], writes=["ysT_o"])

        SKIP = os.environ.get("KSKIP", "").split(",")
        if "kv" not in SKIP:
            memory_kv_prompt()
        for st in range(int(os.environ.get("KNST", NST))):
            t0 = st * ST
            S.dma("xin", lambda e, t0=t0: e.dma_start(out=hT[:], in_=xT[:, t0:t0 + ST].rearrange("(k p) t -> p k t", p=128)),
                  writes=[hT])
            ffn("ffn1", ST)
            if os.environ.get("KSTAGE", "all") != "ffn":
                mix_prompt(st)
            if os.environ.get("KSTAGE", "all") == "all":
                mem_attn(ST, [(slice(0, ST), None)])
            ffn("ffn2", ST)
            S.dma("yout", lambda e, t0=t0: e.dma_start(out=yT_o[:, t0:t0 + ST].rearrange("(k p) t -> p k t", p=128), in_=hT[:]),
                  reads=[hT], writes=["yT_o"])
        if "ssmo" not in SKIP:
          S.dma("ssmo", lambda e: e.dma_start(out=ssmT_o, in_=stT[:].rearrange("p g c -> p (g c)")), reads=[stT], writes=["ssmT_o"])
        if "convo" not in SKIP:
          S.dma("convo", lambda e: e.dma_start(out=convT_o.rearrange("(k p) j -> p k j", p=128), in_=hist[:]), reads=[hist], writes=["convT_o"])
        if "sample" not in SKIP:
            sample_path()
        S.emit()
    return nc


def _consts():
    s = np.arange(128)
    ident = np.eye(128, dtype=np.float32)
    U = (s[:, None] <= s[None, :]).astype(np.float32)
    negm = np.where(s[None, :] <= s[:, None], 0.0, -1e30).astype(np.float32)
    ones = np.ones((128, 128), np.float32)
    tri = U.copy()
    return np.concatenate([ident, U, negm, ones, tri], axis=1)


def _fm(v, n):
    return np.ascontiguousarray(np.asarray(v, np.float32).reshape(n, 128).T)


def kernel(**inp):
    inp = {k: np.asarray(v) for k, v in inp.items()}
    nc = build_nc()
    pp = np.zeros((128, PP_N), np.float32)
    for n, o in PP_G.items():
        pp[:, o:o + 8] = _fm(inp[n][0], 8)
    pp[:, PP_NG:PP_NG + 16] = _fm(inp["ssd_norm_g"][0], 16)
    pp[:, PP_CB:PP_CB + 32] = _fm(inp["conv_b"][0], 32)
    cw = inp["conv_w"][0]
    pp[:, PP_CW:PP_CW + 128] = cw.T.reshape(32, 128, 4).transpose(1, 0, 2).reshape(128, 128)
    pp[:, PP_DS:PP_DS + 16] = _fm(np.repeat(inp["d_skip"][0], 64), 16)
    rowp = np.zeros((128, 64), np.float32)
    rowp[:, 0:32] = inp["dt_bias"][0][None, :]
    rowp[:, 32:64] = inp["a_log"][0][None, :]
    cst = _consts()
    wnames = ["ffn1_wg", "ffn1_wu", "ffn1_wd", "w_in", "w_br_ssd", "w_br_att", "w_out", "w_mq", "w_mk", "w_mv", "w_mo",
              "ffn2_wg", "ffn2_wu", "ffn2_wd"]
    shared = {n: np.ascontiguousarray(inp[n][0]) for n in wnames}
    qcols = np.concatenate([np.arange(O_Q + h * 64, O_Q + (h + 1) * 64) for h in HPERM])
    w_in_l = shared["w_in"].copy()
    w_in_l[:, O_Q:O_Q + D] = shared["w_in"][:, qcols]
    shared["w_in"] = w_in_l
    shared["w_br_att"] = np.ascontiguousarray(shared["w_br_att"][qcols - O_Q, :])
    shared.update(pp=pp, rowp=rowp, cst=cst)
    p_ = np.arange(128)
    cst2 = np.zeros((128, C2N), np.float32)
    cst2[:, 0:128] = (p_[:, None] % 4 == p_[None, :] % 4)
    cst2[:, 128:132] = (p_[:, None] % 4 == np.arange(4)[None, :])
    cst2[:, 132:136] = np.where((p_[:, None] // 4 == 0) & (np.arange(4)[None, :] <= p_[:, None] % 4), 0.0, -1e30)
    cst2[:, 136] = p_ % 64
    cst2[0:4, 137:265] = (np.arange(4)[:, None] == p_[None, :] % 4)
    ht = np.arange(32)
    cst2[0:32, 265:269] = (ht[:, None] % 4 == np.arange(4)[None, :])
    cst2[0:8, 269:301] = (np.arange(8)[:, None] == ht[None, :] // 4)
    for tq in range(4):
        cst2[tq, 301 + tq * 128:301 + (tq + 1) * 128] = 1.0
    cst2[0:32, 813:941] = (ht[:, None] == p_[None, :] // 4)
    cst2[:, 941] = (p_ >= 64)
    shared["cst2"] = cst2
    have_pool = "cache_k" in inp
    if have_pool:
        shared["kidxT"] = np.ascontiguousarray(inp["cache_kidx"][0].transpose(0, 2, 1)).reshape(5120 * 64, 128)
        shared["poolk"] = inp["cache_k"][0].reshape(5120 * 128, 256)
        shared["poolv"] = inp["cache_v"][0].reshape(5120 * 128, 256)
    in_maps = []
    NCR = int(os.environ.get("KCORES", 8))
    for c in range(NCR):
        m = dict(shared)
        m["xT"] = np.ascontiguousarray(inp["x_prompt"][c].T)
        m["xsT"] = np.ascontiguousarray(inp["x_sample"][4 * c:4 * c + 4].reshape(16, D).T)
        m["memT"] = np.ascontiguousarray(inp["mem_prompt"][c].T)
        sc_ = inp["state_conv"][0, 4 * c:4 * c + 4]
        m["hists"] = np.ascontiguousarray(sc_.transpose(2, 0, 1).reshape(32, 128, 4, 3).transpose(1, 0, 2, 3)).reshape(128, 384)
        m["ssmsT"] = np.ascontiguousarray(inp["state_ssm"][0, 4 * c:4 * c + 4].reshape(4, 2048, 128).transpose(0, 2, 1))
        m["cmkT"] = np.ascontiguousarray(inp["cache_mem_k"][0, 4 * c:4 * c + 4].reshape(4, 256, D).transpose(0, 2, 1))
        m["cmv"] = np.ascontiguousarray(inp["cache_mem_v"][0, 4 * c:4 * c + 4].reshape(4, 256, D))
        m["ptab"] = np.ascontiguousarray(inp["page_table"][4 * c:4 * c + 4].astype(np.int32))
        in_maps.append(m)
    res = run_bass_kernel_spmd(nc, in_maps, core_ids=list(range(NCR)))
    if os.environ.get("KTIME"):
        print("EXEC_TIME_NS", res.exec_time_ns)
    R = res.results
    y_p = np.stack([R[c]["yT"].T for c in range(NCR)])
    y_s = np.concatenate([R[c]["ysT"].T.reshape(4, 4, D) for c in range(NCR)])
    nk_p = np.stack([R[c]["kT"].T.reshape(SEQ, 4, 64) for c in range(NCR)])[None]
    nv_p = np.stack([R[c]["v_o"].reshape(SEQ, 4, 64) for c in range(NCR)])[None]
    nki_p = np.stack([R[c]["kiT"].T for c in range(NCR)])[None]
    nssm_p = np.stack([R[c]["ssmT"].T.reshape(32, 64, 128) for c in range(NCR)])[None]
    nconv_p = np.stack([R[c]["convT"].T for c in range(NCR)])[None]
    nmk_p = np.stack([R[c]["mkT"].T.reshape(256, 4, 256) for c in range(NCR)])[None]
    nmv_p = np.stack([R[c]["mv_o"].reshape(256, 4, 256) for c in range(NCR)])[None]
    nk_s = np.concatenate([R[c]["ks_o"].reshape(4, 4, 4, 64) for c in range(NCR)])[None]
    nv_s = np.concatenate([R[c]["vs_o"].reshape(4, 4, 4, 64) for c in range(NCR)])[None]
    nki_s = np.concatenate([R[c]["kisT"].T.reshape(4, 4, 64) for c in range(NCR)])[None]
    nssm_s = np.concatenate([R[c]["ssms"].transpose(0, 2, 1).reshape(4, 32, 64, 128) for c in range(NCR)])[None]
    nconv_s = np.concatenate([R[c]["convs"].reshape(128, 32, 4, 3).transpose(2, 3, 1, 0).reshape(4, 3, 4096) for c in range(NCR)])[None]
    return (y_p, y_s, nk_p, nv_p, nki_p, nssm_p, nconv_p, nmk_p, nmv_p, nk_s, nv_s, nki_s, nssm_s, nconv_s)
```

```python
import os
import numpy as np
from contextlib import ExitStack
import concourse.bass as bass
import concourse.mybir as mybir
from concourse.bass_utils import run_bass_kernel_spmd

F32 = mybir.dt.float32
BF16 = mybir.dt.bfloat16
I32 = mybir.dt.int32
U32 = mybir.dt.uint32
AF = mybir.ActivationFunctionType
ALU = mybir.AluOpType
AX = mybir.AxisListType

D = 1024
KC = 8
SEQ = 2048
ST = 512
NST = SEQ // ST
FH = 2816
FHC = 22
IN_DIM = 10344
O_Z, O_X, O_B, O_C, O_DT, O_Q, O_K, O_V, O_QI, O_KI, O_WI, O_GS, O_GA = (
    0, 2048, 4096, 5120, 6144, 6176, 7200, 7456, 7712, 8224, 8288, 8296, 9320)
EPS = 1e-6
NIT = 14
HPERM = []
for _j in range(8):
    HPERM += [(_j // 4) * 8 + _j % 4, (_j // 4) * 8 + _j % 4 + 4]
WBE = 4096

PP_G = {n: i * 8 for i, n in enumerate(
    ["ffn1_pre_g", "ffn1_post_g", "mix_pre_g", "mix_post_g", "mem_pre_g", "mem_kv_g", "mem_post_g", "ffn2_pre_g",
     "ffn2_post_g"])}
PP_NG = 72
PP_CB = 88
PP_CW = 120
PP_DS = 248
PP_N = 264
C2N = 942
NITS = 22
PGP = [0, 1, 2, 3]


class _Rec:
    def __getattr__(self, name):
        def f(*a, **kw):
            self.call = (name, a, kw)
            return self
        return f


def _freeze(fn):
    r = _Rec()
    fn(r)
    name, a, kw = r.call
    return lambda e: getattr(e, name)(*a, **kw)


class Sched:
    ENG = ("pe", "act", "dve", "pool", "sp")

    def __init__(self, nc, es):
        self.nc = nc
        self.es = es
        self.ops = {e: [] for e in self.ENG}
        self.cnt = {e: 0 for e in self.ENG}
        self.sem = {e: es.enter_context(nc.semaphore("s_" + e)) for e in self.ENG}
        self.waited = {e: {} for e in self.ENG}
        self.lastw = {}
        self.readers = {}
        self.chan = {}
        self.nt = 0

    def sb(self, shape, dtype, name=None):
        self.nt += 1
        return self.es.enter_context(self.nc.sbuf_tensor("sb_" + (name or f"t{self.nt}"), list(shape), dtype))

    def ps(self, shape, dtype, name=None):
        self.nt += 1
        return self.es.enter_context(self.nc.psum_tensor("ps_" + (name or f"p{self.nt}"), list(shape), dtype))

    def _key(self, k):
        if isinstance(k, (str, tuple)):
            return k
        return k.name

    def alias(self, newk, oldks):
        newk = self._key(newk)
        lst = self.readers.setdefault(newk, [])
        for o in oldks:
            o = self._key(o)
            lst.extend(self.readers.get(o, []))
            if o in self.lastw:
                lst.append(self.lastw[o])

    def _collect(self, eng, reads, writes):
        deps = []
        for k in reads:
            k = self._key(k)
            w = self.lastw.get(k)
            if w is not None:
                deps.append(("raw", w))
            if isinstance(k, str) and k.startswith("ps_"):
                for r in self.readers.get(k, ()):
                    if r[0] != eng:
                        deps.append(("rar", r))
        for k in writes:
            k = self._key(k)
            w = self.lastw.get(k)
            if w is not None:
                deps.append(("waw", w))
            for r in self.readers.get(k, ()):
                deps.append(("war", r))
        waits = []
        for kind, (skey, val) in deps:
            if skey == eng:
                if eng == "pe" or kind == "war":
                    continue
            if self.waited[eng].get(skey, 0) >= val:
                continue
            self.waited[eng][skey] = val
            waits.append((skey, val))
        return waits

    def _commit(self, tok, reads, writes):
        for k in writes:
            k = self._key(k)
            self.lastw[k] = tok
            self.readers[k] = []
        for k in reads:
            self.readers.setdefault(self._key(k), []).append(tok)

    def op(self, eng, fn, reads=(), writes=()):
        waits = self._collect(eng, reads, writes)
        self.cnt[eng] += 1
        self.ops[eng].append((waits, _freeze(fn), eng))
        self._commit((eng, self.cnt[eng]), reads, writes)

    def dma(self, chan, fn, reads=(), writes=(), queue="sp"):
        waits = self._collect(queue, reads, writes)
        if chan not in self.chan:
            self.chan[chan] = [self.es.enter_context(self.nc.semaphore("c_" + str(len(self.chan)))), 0]
        ch = self.chan[chan]
        ch[1] += 16
        self.ops[queue].append((waits, _freeze(fn), ("ch", chan)))
        self._commit((("ch", chan), ch[1]), reads, writes)

    def _semof(self, skey):
        return self.chan[skey[1]][0] if isinstance(skey, tuple) else self.sem[skey]

    def emit(self):
        fin = [(("ch", c), v) for c, (s, v) in self.chan.items()]
        fin += [(e, self.cnt[e]) for e in self.ENG if e != "sp" and self.cnt[e] > 0]

        def run(engname, e):
            for waits, fn, inc in self.ops[engname]:
                for skey, val in waits:
                    e.wait_ge(self._semof(skey), val)
                inst = fn(e)
                if isinstance(inc, tuple):
                    inst.then_inc(self.chan[inc[1]][0], 16)
                else:
                    inst.then_inc(self.sem[inc], 1)
            if engname == "sp":
                for skey, val in fin:
                    e.wait_ge(self._semof(skey), val)

        with self.nc.Block() as block:
            @block.sync
            def _(e):
                run("sp", e)

            @block.scalar
            def _(e):
                run("act", e)

            @block.vector
            def _(e):
                run("dve", e)

            @block.gpsimd
            def _(e):
                run("pool", e)

            @block.tensor
            def _(e):
                run("pe", e)


class Ctx:
    pass


def build_nc(debug=None):
    nc = bass.Bass("TRN2", target_bir_lowering=False, dynamic_dma_scratch_size=8192)
    dt_in = {}

    def din(name, shape, dt=F32):
        dt_in[name] = nc.dram_tensor(name, list(shape), dt, kind="ExternalInput").ap()
        return dt_in[name]

    def dout(name, shape, dt=F32):
        return nc.dram_tensor(name, list(shape), dt, kind="ExternalOutput").ap()

    xT = din("xT", [D, SEQ])
    xsT = din("xsT", [D, 16])
    memT = din("memT", [D, 256])
    pp_d = din("pp", [128, PP_N])
    rowp_d = din("rowp", [128, 64])
    cst_d = din("cst", [128, 5 * 128])
    W = {}
    for n, shp in [("ffn1_wg", [D, FH]), ("ffn1_wu", [D, FH]), ("ffn1_wd", [FH, D]), ("w_in", [D, IN_DIM]),
                   ("w_br_ssd", [2048, D]), ("w_br_att", [D, D]), ("w_out", [D, D]), ("w_mq", [D, D]),
                   ("w_mk", [D, D]), ("w_mv", [D, D]), ("w_mo", [D, D]),
                   ("ffn2_wg", [D, FH]), ("ffn2_wu", [D, FH]), ("ffn2_wd", [FH, D])]:
        W[n] = din(n, shp)

    yT_o = dout("yT", [D, SEQ])
    ysT_o = dout("ysT", [D, 16])
    kT_o = dout("kT", [256, SEQ])
    v_o = dout("v_o", [SEQ, 256])
    kiT_o = dout("kiT", [64, SEQ])
    ssmT_o = dout("ssmT", [128, 2048])
    convT_o = dout("convT", [4096, 3])
    mkT_o = dout("mkT", [D, 256])
    mv_o = dout("mv_o", [256, D])
    hists_d = din("hists", [128, 384])
    ssmsT_d = din("ssmsT", [4, 128, 2048])
    cmkT_d = din("cmkT", [4, D, 256])
    cmv_d = din("cmv", [4, 256, D])
    ptab_d = din("ptab", [4, 128], I32)
    cst2_d = din("cst2", [128, C2N])
    if "sample" not in os.environ.get("KSKIP", "").split(","):
        kidxT_d = din("kidxT", [5120 * 64, 128])
        poolk_d = din("poolk", [5120 * 128, 256])
        poolv_d = din("poolv", [5120 * 128, 256])
    ks_o = dout("ks_o", [16, 256])
    vs_o = dout("vs_o", [16, 256])
    kisT_o = dout("kisT", [64, 16])
    ssms_o = dout("ssms", [4, 128, 2048])
    convs_o = dout("convs", [128, 384])
    dbg_o = None

    with ExitStack() as es:
        S = Sched(nc, es)
        c = Ctx()
        pp = S.sb([128, PP_N], F32, "pp")
        rowp = S.sb([128, 64], F32, "rowp")
        cst = S.sb([128, 640], F32, "cst")
        cstb = S.sb([128, 640], BF16, "cstb")
        S.dma("pp", lambda e: e.dma_start(out=pp[:], in_=pp_d), writes=[pp])
        S.dma("rowp", lambda e: e.dma_start(out=rowp[:], in_=rowp_d), writes=[rowp])
        S.dma("cst", lambda e: e.dma_start(out=cst[:], in_=cst_d), writes=[cst])
        S.dma("cstb", lambda e: e.dma_start(out=cstb[:], in_=cst_d), writes=[cstb], queue="pool")
        ident = cst[:, 0:128]
        Uf = cst[:, 128:256]
        negm = cst[:, 256:384]
        identb = cstb[:, 0:128]
        onesb = cstb[:, 384:512]
        trib = cstb[:, 512:640]
        epsT = S.sb([128, 1], F32, "epsT")
        S.op("dve", lambda e: e.memset(epsT[:], EPS), writes=[epsT])
        arow = S.sb([128, 32], F32, "arow")
        S.op("act", lambda e: e.activation(out=arow[:], in_=rowp[:, 32:64], func=AF.Exp), reads=[rowp], writes=[arow])
        S.op("dve", lambda e: e.tensor_scalar(out=arow[:], in0=arow[:], scalar1=-1.0, scalar2=None, op0=ALU.mult),
             reads=[arow], writes=[arow])

        gen = [S.ps([128, 512], F32, f"pg{i}") for i in range(5)]
        accA = S.ps([128, 512], F32, "accA")
        accB = S.ps([128, 512], F32, "accB")
        ptb = S.ps([128, 1024], BF16, "ptb")
        c.gi = 0
        c.ti = 0

        def P():
            c.gi = (c.gi + 1) % len(gen)
            return gen[c.gi]

        def PTS(i):
            return ptb[:, i * 128:(i + 1) * 128]

        NWB = 3
        wbufs = [S.sb([128, WBE], BF16, f"wb{i}") for i in range(NWB)]
        c.wi = 0

        def wload(wd, r0, nk, c0, ncols):
            assert nk * ncols <= WBE
            c.wi = (c.wi + 1) % NWB
            buf = wbufs[c.wi]
            view = buf[:, 0:nk * ncols].rearrange("p (k c) -> p k c", k=nk)
            src = wd[r0:r0 + nk * 128, c0:c0 + ncols].rearrange("(k p) c -> p k c", p=128)
            S.dma(("w", c.wi), lambda e: e.dma_start(out=view, in_=src), writes=[buf], queue="pool")
            return buf, view

        hT = S.sb([128, KC, ST], F32, "hT")
        uT = S.sb([128, KC, ST], BF16, "uT")
        scrM = S.sb([128, 4096], F32, "scrM")
        ybuf = scrM[:].rearrange("p (k t) -> p k t", k=KC)
        SCRK = ["ssdtmp", "maskT", "zs", ("cv", 0), ("cv", 1), ("cv", 2), ("cv", 3)]
        sqb = S.sb([128, ST], BF16, "sqb")
        sqb2 = S.sb([128, ST], BF16, "sqb2")
        rstd = S.sb([128, ST], F32, "rstd")
        tmpf = S.sb([128, ST], F32, "tmpf")
        big = S.sb([128, 22528], BF16, "big")
        hid = big[:, 0:FHC * ST].rearrange("p (k t) -> p k t", k=FHC)
        sgs = [S.sb([128, ST], BF16, f"sg{i}") for i in range(2)]

        def gcol(name, k):
            return pp[:, PP_G[name] + k:PP_G[name] + k + 1]

        def norm_stats(src, srck, T, scale_div):
            ps = P()
            n = len(src)
            for k in range(n):
                sq = sqb if k % 2 == 0 else sqb2
                S.op("act", lambda e, k=k, sq=sq: e.activation(out=sq[:, :T], in_=src[k], func=AF.Square),
                     reads=[srck], writes=[sq])
                S.op("pe", lambda e, k=k, sq=sq: e.matmul(ps[:, :T], onesb, sq[:, :T], start=(k == 0), stop=(k == n - 1)),
                     reads=[sq, cstb], writes=[ps])
            S.op("act", lambda e: e.activation(out=rstd[:, :T], in_=ps[:, :T], func=AF.Sqrt, bias=epsT[:, 0:1],
                                               scale=1.0 / scale_div), reads=[ps, epsT], writes=[rstd])
            S.op("dve", lambda e: e.reciprocal(out=rstd[:, :T], in_=rstd[:, :T]), reads=[rstd], writes=[rstd])

        def prenorm(gname, T, src=None, srck=None, dst=None):
            src = src if src is not None else [hT[:, k, :T] for k in range(KC)]
            srck = srck if srck is not None else hT
            dst = dst if dst is not None else uT
            norm_stats(src, srck, T, float(D))
            for k in range(KC):
                S.op("dve", lambda e, k=k: e.scalar_tensor_tensor(out=dst[:, k, :T], in0=src[k], scalar=gcol(gname, k),
                                                                   in1=rstd[:, :T], op0=ALU.mult, op1=ALU.mult),
                     reads=[srck, rstd, pp], writes=[dst])

        def postnorm_add(gname, T, coef):
            norm_stats([ybuf[:, k, :T] for k in range(KC)], "ybuf", T, float(D))
            for k in range(KC):
                S.op("dve", lambda e, k=k: e.scalar_tensor_tensor(out=tmpf[:, :T], in0=ybuf[:, k, :T], scalar=gcol(gname, k),
                                                                   in1=rstd[:, :T], op0=ALU.mult, op1=ALU.mult),
                     reads=["ybuf", rstd, pp], writes=[tmpf])
                S.op("dve", lambda e, k=k: e.scalar_tensor_tensor(out=hT[:, k, :T], in0=tmpf[:, :T], scalar=coef,
                                                                   in1=hT[:, k, :T], op0=ALU.mult, op1=ALU.add),
                     reads=[tmpf, hT], writes=[hT])

        def proj_fm(wd, r0, nk, c0, ncols, rhs_fn, rhs_keys, T, consumer, blk=512, msz=128):
            blk = min(blk, (WBE // nk) // msz * msz)
            idx = 0
            for b0 in range(0, ncols, blk):
                bc = min(blk, ncols - b0)
                buf, view = wload(wd, r0, nk, c0 + b0, bc)
                for m0 in range(0, bc, msz):
                    ms = min(msz, bc - m0)
                    ps = P()
                    for k in range(nk):
                        S.op("pe", lambda e, k=k, m0=m0, ms=ms, ps=ps, view=view: e.matmul(
                            ps[0:ms, :T], view[:, k, m0:m0 + ms], rhs_fn(k), start=(k == 0), stop=(k == nk - 1)),
                            reads=[buf] + rhs_keys, writes=[ps])
                    consumer(idx, ps, ms)
                    idx += 1

        def ffn(pref, T):
            prenorm(pref + "_pre_g", T)
            S.alias("hid", MIXK + ["acc"])
            S.alias("ybuf", SCRK)
            for b0 in range(0, FH, 256):
                bufg, vg = wload(W[pref + "_wg"], 0, KC, b0, 256)
                bufu, vu = wload(W[pref + "_wu"], 0, KC, b0, 256)
                for m in range(2):
                    hc = b0 // 128 + m
                    pg, pu = P(), P()
                    for k in range(KC):
                        S.op("pe", lambda e, k=k, m=m, pg=pg, vg=vg: e.matmul(pg[:, :T], vg[:, k, m * 128:(m + 1) * 128],
                                                                             uT[:, k, :T], start=(k == 0), stop=(k == KC - 1)),
                             reads=[bufg, uT], writes=[pg])
                    for k in range(KC):
                        S.op("pe", lambda e, k=k, m=m, pu=pu, vu=vu: e.matmul(pu[:, :T], vu[:, k, m * 128:(m + 1) * 128],
                                                                             uT[:, k, :T], start=(k == 0), stop=(k == KC - 1)),
                             reads=[bufu, uT], writes=[pu])
                    sg = sgs[hc % 2]
                    S.op("act", lambda e, pg=pg, sg=sg: e.activation(out=sg[:, :T], in_=pg[:, :T], func=AF.Silu),
                         reads=[pg], writes=[sg])
                    S.op("dve", lambda e, pu=pu, sg=sg, hc=hc: e.tensor_tensor(out=hid[:, hc, :T], in0=sg[:, :T], in1=pu[:, :T],
                                                                              op=ALU.mult), reads=[pu, sg], writes=["hid"])

            def cons(idx, ps, ms):
                S.op("act", lambda e: e.activation(out=ybuf[:, idx, :T], in_=ps[:, :T], func=AF.Copy), reads=[ps],
                     writes=["ybuf"])
            proj_fm(W[pref + "_wd"], 0, FHC, 0, D, lambda k: hid[:, k, :T], ["hid"], T, cons, blk=128)
            postnorm_add(pref + "_post_g", T, 0.5)


        MIXK = ["qT", "qiT", "yssdT", "yattT", "mergedT"]
        qT = big[:, 0:4096].rearrange("p (k t) -> p k t", k=8)
        qiT = big[:, 4096:6144].rearrange("p (k t) -> p k t", k=4)
        yssdT = big[:, 6144:14336].rearrange("p (k t) -> p k t", k=16)
        yattT = big[:, 14336:18432].rearrange("p (k t) -> p k t", k=8)
        mergedT = big[:, 18432:22528].rearrange("p (k t) -> p k t", k=8)
        kT2 = S.sb([128, 2, SEQ], BF16, "kT2")
        kiT2 = S.sb([128, SEQ], BF16, "kiT2")
        vtok = S.sb([128, 16, 256], BF16, "vtok")
        stT = S.sb([128, 8, 256], F32, "stT")
        stTb = S.sb([128, 8, 256], BF16, "stTb")
        hist = S.sb([128, 32, 3], F32, "hist")
        mkTb = S.sb([128, 8, 256], BF16, "mkTb")
        mvb = S.sb([128, 2, D], BF16, "mvb")
        for t_ in (stT, stTb, hist):
            S.op("dve", lambda e, t_=t_: e.memset(t_[:], 0.0), writes=[t_])
        maskT = scrM[:].bitcast(BF16).rearrange("p (k t) -> p k t", k=16)
        pre = scrM[:, 0:2060].rearrange("p (j t) -> p j t", j=4)
        cv = scrM[:, 2064:3088].bitcast(BF16).rearrange("p (j t) -> p j t", j=4)
        zs = scrM[:, 3088:3600].bitcast(BF16).rearrange("p (j t) -> p j t", j=2)
        Yg = S.sb([128, 2, ST], F32, "Yg")
        acc = big[:, 18432:22528].bitcast(F32)
        mask01t = S.sb([128, 2048], BF16, "mask01t")
        mask01 = mask01t[:]
        junk = mask01t[:]
        stg = [S.sb([128, 512], F32, f"stg{i}") for i in range(1)]
        c.si = 0
        Es = [S.sb([128, ST], BF16, f"E{i}") for i in range(2)]
        c.ei = 0
        rrs = [S.sb([128, 512], F32, f"rr{i}") for i in range(2)]
        rden = S.sb([128, ST], F32, "rden")
        witok = S.sb([128, 4, 8], F32, "witok")
        dtt = S.sb([128, 4, 32], F32, "dtt")
        dta = S.sb([128, 4, 32], F32, "dta")
        acsc = S.sb([128, 4, 32], F32, "acsc")
        arw = S.sb([128, 512], F32, "arw")
        erow = S.sb([128, 512], F32, "erow")
        Cdec = S.sb([128, 512], BF16, "Cdec")
        xs_tok = S.sb([128, 256], BF16, "xs_tok")
        B_tok = S.sb([128, 128], BF16, "B_tok")
        cbm = S.sb([128, 128], BF16, "cbm")
        argt = [S.sb([128, 128], F32, f"arg{i}") for i in range(2)]
        Ldt = [S.sb([128, 128], BF16, f"Ld{i}") for i in range(2)]
        MTt = [S.sb([128, 128], BF16, f"MT{i}") for i in range(2)]
        xdt = S.sb([128, 256], BF16, "xdt")
        xdtw = S.sb([128, 256], BF16, "xdtw")
        sm4 = S.sb([128, 16], F32, "sm4")
        bis = S.sb([128, 8], F32, "bis")

        def stage_out(src_ps, rows, cols, dst_ap, dkey):
            c.si = 0
            sg_ = stg[c.si]
            S.op("act", lambda e: e.activation(out=sg_[0:rows, 0:cols], in_=src_ps, func=AF.Copy), reads=[dkey[0]], writes=[sg_])
            S.dma(("stg", c.si), lambda e: e.dma_start(out=dst_ap, in_=sg_[0:rows, 0:cols]), reads=[sg_], writes=[dkey[1]])

        def copy_alt(i, out, in_, reads, writes):
            if i % 2 == 0:
                S.op("act", lambda e: e.activation(out=out, in_=in_, func=AF.Copy), reads=reads, writes=writes)
            else:
                S.op("dve", lambda e: e.tensor_copy(out=out, in_=in_), reads=reads, writes=writes)

        def tm_proj(view, buf, c0, n, tt, ps):
            for k in range(KC):
                S.op("pe", lambda e, k=k: e.matmul(ps[:, 0:n], uT[:, k, tt * 128:(tt + 1) * 128], view[:, k, c0:c0 + n],
                                                   start=(k == 0), stop=(k == KC - 1)), reads=[buf, uT], writes=[ps])

        def fm64(view, buf, c0, ps, T):
            for half in range(2):
                for k in range(KC):
                    S.op("pe", lambda e, k=k, half=half: e.matmul(ps[half * 64:(half + 1) * 64, :T], view[:, k, c0:c0 + 64],
                                                                  uT[:, k, :T], start=(k == 0), stop=(k == KC - 1)),
                         reads=[buf, uT], writes=[ps])

        def mix_prompt(st):
            t0 = st * ST
            for k_ in MIXK:
                S.alias(k_, ["hid"])
            for k_ in SCRK:
                S.alias(k_, ["ybuf"])
            S.alias("ssdtmp", ["maskT"])
            prenorm("mix_pre_g", ST)
            win = W["w_in"]
            KM = os.environ.get("KMIX", "k,v,ki,wi,dt,q").split(",")
            buf, view = wload(win, 0, KC, O_K, 512)
            for kc in range(2 if "k" in KM else 0):
                ps = P()
                for k in range(KC):
                    S.op("pe", lambda e, k=k, kc=kc, ps=ps: e.matmul(ps[:, :ST], view[:, k, kc * 128:(kc + 1) * 128], uT[:, k, :ST],
                                                                     start=(k == 0), stop=(k == KC - 1)), reads=[buf, uT], writes=[ps])
                KK = os.environ.get("KK", "copy,stage").split(",")
                if "copy" in KK:
                    S.op("dve", lambda e, kc=kc, ps=ps: e.tensor_copy(out=kT2[:, kc, t0:t0 + ST], in_=ps[:, :ST]), reads=[ps], writes=[kT2])
                if "stage" in KK:
                    stage_out(ps[:, :ST], 128, ST, kT_o[kc * 128:(kc + 1) * 128, t0:t0 + ST], (ps, "kT_o"))
            for tt in range(4 if "v" in KM else 0):
                ps = P()
                tm_proj(view, buf, 256, 256, tt, ps)
                S.op("dve", lambda e, tt=tt, ps=ps: e.tensor_copy(out=vtok[:, st * 4 + tt, :], in_=ps[:, 0:256]), reads=[ps], writes=[vtok])
                stage_out(ps[:, 0:256], 128, 256, v_o[t0 + tt * 128:t0 + (tt + 1) * 128, :], (ps, "v_o"))
            buf, view = wload(win, 0, KC, O_KI - 32, 128)
            ps = P()
            if "ki" in KM:
                fm64(view, buf, 32, ps, ST)
                S.op("dve", lambda e, ps=ps: e.tensor_copy(out=kiT2[:, t0:t0 + ST], in_=ps[:, :ST]), reads=[ps], writes=[kiT2])
                stage_out(ps[0:64, :ST], 64, ST, kiT_o[:, t0:t0 + ST], (ps, "kiT_o"))
            for tt in range(4 if "wi" in KM else 0):
                ps = P()
                tm_proj(view, buf, 96, 8, tt, ps)
                S.op("dve", lambda e, tt=tt, ps=ps: e.tensor_copy(out=witok[:, tt, :], in_=ps[:, 0:8]), reads=[ps], writes=[witok])
            buf, view = wload(win, 0, KC, O_DT, 128)
            if "dt" not in KM:
                return
            for tt in range(4):
                ps = P()
                tm_proj(view, buf, 0, 32, tt, ps)
                S.op("dve", lambda e, tt=tt, ps=ps: e.tensor_tensor(out=dtt[:, tt, :], in0=ps[:, 0:32], in1=rowp[:, 0:32], op=ALU.add),
                     reads=[ps, rowp], writes=[dtt])
            S.op("act", lambda e: e.activation(out=dtt[:], in_=dtt[:], func=AF.Exp), reads=[dtt], writes=[dtt])
            S.op("act", lambda e: e.activation(out=dtt[:], in_=dtt[:], func=AF.Ln, bias=1.0), reads=[dtt], writes=[dtt])
            for tt in range(4):
                S.op("dve", lambda e, tt=tt: e.tensor_tensor(out=dta[:, tt, :], in0=dtt[:, tt, :], in1=arow[:], op=ALU.mult),
                     reads=[dtt, arow], writes=[dta])
                ps = P()
                S.op("pe", lambda e, tt=tt, ps=ps: e.matmul(ps[:, 0:32], Uf, dta[:, tt, :], start=True, stop=True), reads=[cst, dta], writes=[ps])
                S.op("act", lambda e, tt=tt, ps=ps: e.activation(out=acsc[:, tt, :], in_=ps[:, 0:32], func=AF.Copy), reads=[ps], writes=[acsc])
            proj_fm(win, 0, KC, O_Q, D, lambda k: uT[:, k, :ST], [uT], ST,
                    lambda idx, ps, ms: copy_alt(idx, qT[:, idx, :], ps[:, :ST], [ps], ["qT"]))
            proj_fm(win, 0, KC, O_QI, 512, lambda k: uT[:, k, :ST], [uT], ST,
                    lambda idx, ps, ms: copy_alt(idx, qiT[:, idx, :], ps[:, :ST], [ps], ["qiT"]))
            STG = os.environ.get("KSTAGE", "all")
            if STG == "mixA":
                return
            for g in range(8):
                ssd_group(st, g)
            if STG == "ssd":
                return
            S.alias("maskT", ["ssdtmp"])
            dsa_prompt(st)
            if STG == "dsa":
                return
            merge(ST)

        def conv_chunk(j, ch, T):
            cw = lambda tap: pp[:, PP_CW + ch * 4 + tap:PP_CW + ch * 4 + tap + 1]
            S.op("dve", lambda e: e.tensor_scalar(out=tmpf[:, :T], in0=pre[:, j, 0:T], scalar1=cw(0), scalar2=pp[:, PP_CB + ch:PP_CB + ch + 1],
                                                  op0=ALU.mult, op1=ALU.add), reads=["ssdtmp", pp], writes=[tmpf])
            for tap in range(1, 4):
                S.op("dve", lambda e, tap=tap: e.scalar_tensor_tensor(out=tmpf[:, :T], in0=pre[:, j, tap:tap + T], scalar=cw(tap),
                                                                      in1=tmpf[:, :T], op0=ALU.mult, op1=ALU.add),
                     reads=["ssdtmp", pp, tmpf], writes=[tmpf])
            S.op("act", lambda e: e.activation(out=cv[:, j, :T], in_=tmpf[:, :T], func=AF.Silu), reads=[tmpf], writes=[("cv", j)])

        def ssd_group(st, g):
            win = W["w_in"]
            T = ST
            buf, view = wload(win, 0, KC, O_Z + g * 256, 256)
            for m in range(2):
                ps = P()
                for k in range(KC):
                    S.op("pe", lambda e, k=k, m=m, ps=ps: e.matmul(ps[:, :T], view[:, k, m * 128:(m + 1) * 128], uT[:, k, :T],
                                                                   start=(k == 0), stop=(k == KC - 1)), reads=[buf, uT], writes=[ps])
                S.op("act", lambda e, m=m, ps=ps: e.activation(out=zs[:, m, :T], in_=ps[:, :T], func=AF.Silu), reads=[ps], writes=["zs"])
            chs = [g * 2, g * 2 + 1, 16 + g, 24 + g]
            srcs = [(O_X + g * 256, 0), (O_X + g * 256, 128), (O_B + g * 128, 0), (O_C + g * 128, 0)]
            bufx, viewx = wload(win, 0, KC, O_X + g * 256, 256)
            bufb, viewb = wload(win, 0, KC, O_B + g * 128, 128)
            views = [(bufx, viewx, 0), (bufx, viewx, 128), (bufb, viewb, 0), None]
            for j in range(4):
                if j == 3:
                    bufc, viewc = wload(win, 0, KC, O_C + g * 128, 128)
                    views[3] = (bufc, viewc, 0)
                bf_, vw_, c0 = views[j]
                ps = P()
                for k in range(KC):
                    S.op("pe", lambda e, k=k, ps=ps, vw_=vw_, c0=c0: e.matmul(ps[:, :T], vw_[:, k, c0:c0 + 128], uT[:, k, :T],
                                                                              start=(k == 0), stop=(k == KC - 1)), reads=[bf_, uT], writes=[ps])
                ch = chs[j]
                S.op("dve", lambda e, j=j, ch=ch: e.tensor_copy(out=pre[:, j, 0:3], in_=hist[:, ch, :]), reads=[hist], writes=["ssdtmp"])
                S.op("act", lambda e, j=j, ps=ps: e.activation(out=pre[:, j, 3:3 + T], in_=ps[:, :T], func=AF.Copy), reads=[ps], writes=["ssdtmp"])
                S.op("dve", lambda e, j=j, ch=ch: e.tensor_copy(out=hist[:, ch, :], in_=pre[:, j, T:T + 3]), reads=["ssdtmp"], writes=[hist])
                conv_chunk(j, ch, T)
            for cc in range(4):
                ssd_chunk(g, cc, 128, cc * 128)
            for m in range(2):
                S.op("dve", lambda e, m=m: e.tensor_tensor(out=Yg[:, m, :], in0=Yg[:, m, :], in1=zs[:, m, :], op=ALU.mult),
                     reads=[Yg, "zs"], writes=[Yg])
            norm_stats([Yg[:, 0, :], Yg[:, 1, :]], Yg, T, 256.0)
            for m in range(2):
                S.op("dve", lambda e, m=m: e.scalar_tensor_tensor(out=yssdT[:, g * 2 + m, :], in0=Yg[:, m, :],
                                                                   scalar=pp[:, PP_NG + g * 2 + m:PP_NG + g * 2 + m + 1], in1=rstd[:, :T],
                                                                   op0=ALU.mult, op1=ALU.mult), reads=[Yg, rstd, pp], writes=["yssdT"])

        def ssd_chunk(g, cc, L, col0, sf=None, sbf=None, skeys=None, sfa=None, sba=None):
            cs = slice(col0, col0 + L)
            if sf is None:
                sf = lambda hh: stT[:, g, hh * 64:(hh + 1) * 64]
                sbf = lambda hh: stTb[:, g, hh * 64:(hh + 1) * 64]
                skeys = (stT, stTb)
                sfa, sba = stT[:, g, :], stTb[:, g, :]
            for m in range(3):
                S.op("pe", lambda e, m=m: e.transpose(PTS(m)[0:L, :], cv[:, m, cs], identb), reads=[("cv", m), cstb], writes=[ptb])
            S.op("act", lambda e: e.activation(out=xs_tok[0:L, :], in_=ptb[0:L, 0:256], func=AF.Copy), reads=[ptb], writes=[xs_tok])
            S.op("act", lambda e: e.activation(out=B_tok[0:L, :], in_=ptb[0:L, 256:384], func=AF.Copy), reads=[ptb], writes=[B_tok])
            ps = P()
            S.op("pe", lambda e, ps=ps: e.matmul(ps[0:L, 0:L], cv[:, 2, cs], cv[:, 3, cs], start=True, stop=True),
                 reads=[("cv", 2), ("cv", 3)], writes=[ps])
            S.op("dve", lambda e, ps=ps: e.tensor_tensor(out=cbm[0:L, 0:L], in0=ps[0:L, 0:L], in1=trib[0:L, 0:L], op=ALU.mult),
                 reads=[ps, cstb], writes=[cbm])
            psr = P()
            for hh in range(4):
                h = g * 4 + hh
                S.op("pe", lambda e, hh=hh, h=h: e.matmul(psr[:, hh * 128:hh * 128 + L], dta[0:L, cc, h:h + 1].to_broadcast([L, 128]),
                                                          Uf[0:L, 0:L], start=True, stop=True), reads=[dta, cst], writes=[psr])
            arw3 = arw[:].rearrange("p (a b) -> p a b", a=4)
            erow3 = erow[:].rearrange("p (a b) -> p a b", a=4)
            psr3 = psr[:].rearrange("p (a b) -> p a b", a=4)
            S.op("act", lambda e: e.activation(out=arw3[:, :, 0:L], in_=psr3[:, :, 0:L], func=AF.Copy), reads=[psr], writes=[arw])
            S.op("act", lambda e: e.activation(out=erow3[:, :, 0:L], in_=arw3[:, :, 0:L], func=AF.Exp), reads=[arw], writes=[erow])
            S.op("dve", lambda e: e.tensor_tensor(out=sm4[0:L, 0:4], in0=arw3[0:L, :, L - 1], in1=acsc[0:L, cc, g * 4:g * 4 + 4], op=ALU.subtract),
                 reads=[arw, acsc], writes=[sm4])
            S.op("act", lambda e: e.activation(out=sm4[0:L, 4:8], in_=sm4[0:L, 0:4], func=AF.Exp), reads=[sm4], writes=[sm4])
            S.op("dve", lambda e: e.tensor_tensor(out=sm4[0:L, 8:12], in0=sm4[0:L, 4:8], in1=dtt[0:L, cc, g * 4:g * 4 + 4], op=ALU.mult),
                 reads=[sm4, dtt], writes=[sm4])
            Cd3 = Cdec[:].rearrange("p (a b) -> p a b", a=4)
            S.op("dve", lambda e: e.tensor_tensor(out=Cd3[:, :, 0:L], in0=erow3[:, :, 0:L],
                                                  in1=cv[:, 3, cs].unsqueeze(1).to_broadcast([128, 4, L]), op=ALU.mult),
                 reads=[erow, ("cv", 3)], writes=[Cdec])
            psy = [P(), P()]
            tf3 = tmpf[:].rearrange("p (a b) -> p a b", a=4)
            Ld3 = Es[0][:].rearrange("p (a b) -> p a b", a=4)
            MT3 = Es[1][:].rearrange("p (a b) -> p a b", a=4)
            S.op("dve", lambda e: e.tensor_tensor(out=tf3[0:L, :, 0:L], in0=arw3[0:L, :, 0:L],
                                                  in1=acsc[0:L, cc, g * 4:g * 4 + 4].unsqueeze(2).to_broadcast([L, 4, L]), op=ALU.subtract),
                 reads=[arw, acsc], writes=[tmpf])
            S.op("dve", lambda e: e.tensor_scalar(out=tf3[0:L, :, 0:L], in0=tf3[0:L, :, 0:L], scalar1=0.0, scalar2=None, op0=ALU.min),
                 reads=[tmpf], writes=[tmpf])
            S.op("act", lambda e: e.activation(out=Ld3[0:L, :, 0:L], in_=tf3[0:L, :, 0:L], func=AF.Exp), reads=[tmpf], writes=[Es[0]])
            S.op("dve", lambda e: e.tensor_tensor(out=MT3[0:L, :, 0:L], in0=Ld3[0:L, :, 0:L],
                                                  in1=cbm[0:L, 0:L].unsqueeze(1).to_broadcast([L, 4, L]), op=ALU.mult),
                 reads=[Es[0], cbm], writes=[Es[1]])
            xs3 = xs_tok[:].rearrange("p (a b) -> p a b", a=4)
            S.op("dve", lambda e: e.tensor_tensor(out=xdt[:].rearrange("p (a b) -> p a b", a=4)[0:L], in0=xs3[0:L],
                                                  in1=dtt[0:L, cc, g * 4:g * 4 + 4].unsqueeze(2).to_broadcast([L, 4, 64]), op=ALU.mult),
                 reads=[xs_tok, dtt], writes=[xdt])
            S.op("dve", lambda e: e.tensor_tensor(out=xdtw[:].rearrange("p (a b) -> p a b", a=4)[0:L], in0=xs3[0:L],
                                                  in1=sm4[0:L, 8:12].unsqueeze(2).to_broadcast([L, 4, 64]), op=ALU.mult),
                 reads=[xs_tok, sm4], writes=[xdtw])
            for hh in range(4):
                m, half = hh // 2, hh % 2
                py = psy[m]
                S.op("pe", lambda e, hh=hh, half=half, py=py: e.matmul(py[half * 64:(half + 1) * 64, 0:L], xdt[0:L, hh * 64:(hh + 1) * 64],
                                                                       Es[1][0:L, hh * 128:hh * 128 + L], start=True, stop=False), reads=[xdt, Es[1]], writes=[py])
                S.op("pe", lambda e, hh=hh, half=half, py=py: e.matmul(py[half * 64:(half + 1) * 64, 0:L], sbf(hh),
                                                                       Cdec[:, hh * 128:hh * 128 + L], start=False, stop=True),
                     reads=[skeys[1], Cdec], writes=[py])
            for m in range(2):
                S.op("dve", lambda e, m=m: e.scalar_tensor_tensor(out=Yg[:, m, cs], in0=cv[:, m, cs],
                                                                   scalar=pp[:, PP_DS + g * 2 + m:PP_DS + g * 2 + m + 1], in1=psy[m][:, 0:L],
                                                                   op0=ALU.mult, op1=ALU.add), reads=[("cv", m), pp, psy[m]], writes=[Yg])
            psc = P()
            S.op("pe", lambda e: e.matmul(psc[:, 0:256], B_tok[0:L, :], xdtw[0:L, :], start=True, stop=True), reads=[B_tok, xdtw], writes=[psc])
            sfa3 = sfa.rearrange("p (a b) -> p a b", a=4)
            S.op("dve", lambda e: e.tensor_tensor(out=sfa3, in0=sfa3, in1=erow3[:, :, L - 1].unsqueeze(2).to_broadcast([128, 4, 64]), op=ALU.mult),
                 reads=[skeys[0], erow], writes=[skeys[0]])
            S.op("dve", lambda e: e.tensor_tensor(out=sfa3, in0=sfa3, in1=psc[:, 0:256].rearrange("p (a b) -> p a b", a=4), op=ALU.add),
                 reads=[skeys[0], psc], writes=[skeys[0]])
            S.op("act", lambda e: e.activation(out=sba, in_=sfa, func=AF.Copy), reads=[skeys[0]], writes=[skeys[1]])

        def dsa_prompt(st):
            S.alias("acc", ["mergedT"])
            S.op("dve", lambda e: e.memset(maskT[:, 0:4 * st + 4, :], 0.0), writes=["maskT"])
            R, lo, stp, cand, cnt, gg = [bis[:, i:i + 1] for i in range(6)]
            for qb in range(4):
                i = st * 4 + qb
                Nk = (i + 1) * 128
                qs = slice(qb * 128, (qb + 1) * 128)
                for h in range(8):
                    pair, half = h // 2, h % 2
                    hs = slice(half * 64, (half + 1) * 64)
                    for kt in range((Nk + 511) // 512):
                        n = min(512, Nk - kt * 512)
                        ps = P()
                        S.op("pe", lambda e, ps=ps, pair=pair, hs=hs, kt=kt, n=n: e.matmul(
                            ps[:, 0:n], qiT[hs, pair, qs], kiT2[hs, kt * 512:kt * 512 + n], start=True, stop=True),
                            reads=["qiT", kiT2], writes=[ps])
                        rr = rrs[(h + kt) % 2]
                        S.op("act", lambda e, ps=ps, rr=rr, n=n: e.activation(out=rr[:, 0:n], in_=ps[:, 0:n], func=AF.Relu), reads=[ps], writes=[rr])
                        if h == 0:
                            S.op("dve", lambda e, rr=rr, kt=kt, n=n: e.tensor_scalar(out=acc[:, kt * 512:kt * 512 + n], in0=rr[:, 0:n],
                                                                                     scalar1=witok[:, qb, 0:1], scalar2=None, op0=ALU.mult),
                                 reads=[rr, witok], writes=["acc"])
                        else:
                            S.op("dve", lambda e, rr=rr, kt=kt, n=n, h=h: e.scalar_tensor_tensor(
                                out=acc[:, kt * 512:kt * 512 + n], in0=rr[:, 0:n], scalar=witok[:, qb, h:h + 1],
                                in1=acc[:, kt * 512:kt * 512 + n], op0=ALU.mult, op1=ALU.add), reads=[rr, witok, "acc"], writes=["acc"])
                S.op("dve", lambda e: e.tensor_reduce(out=R, in_=acc[:, 0:Nk], axis=AX.X, op=ALU.max, apply_absolute_value=True),
                     reads=["acc"], writes=[bis])
                S.op("dve", lambda e: e.tensor_tensor(out=acc[:, i * 128:(i + 1) * 128], in0=acc[:, i * 128:(i + 1) * 128], in1=negm, op=ALU.add),
                     reads=["acc", cst], writes=["acc"])
                S.op("dve", lambda e: e.tensor_scalar(out=lo, in0=R, scalar1=-1.0, scalar2=None, op0=ALU.mult), reads=[bis], writes=[bis])
                if i >= 2:
                    for it in range(1, NIT + 1):
                        S.op("dve", lambda e, it=it: e.tensor_scalar(out=stp, in0=R, scalar1=2.0 ** (1 - it), scalar2=None, op0=ALU.mult),
                             reads=[bis], writes=[bis])
                        S.op("dve", lambda e: e.tensor_tensor(out=cand, in0=lo, in1=stp, op=ALU.add), reads=[bis], writes=[bis])
                        S.op("dve", lambda e: e.tensor_scalar(out=junk[:, 0:Nk], in0=acc[:, 0:Nk], scalar1=cand, scalar2=None,
                                                              op0=ALU.is_ge, op1=ALU.add, accum_out=cnt), reads=["acc", bis], writes=["mask01", bis])
                        S.op("dve", lambda e: e.tensor_scalar(out=gg, in0=cnt, scalar1=255.5, scalar2=None, op0=ALU.is_ge),
                             reads=[bis], writes=[bis])
                        S.op("dve", lambda e: e.scalar_tensor_tensor(out=lo, in0=gg, scalar=stp, in1=lo, op0=ALU.mult, op1=ALU.add),
                             reads=[bis], writes=[bis])
                S.op("dve", lambda e: e.tensor_scalar(out=mask01[:, 0:Nk], in0=acc[:, 0:Nk], scalar1=lo, scalar2=None, op0=ALU.is_ge),
                     reads=["acc", bis], writes=["mask01"])
                for sc0 in range(0, i + 1, 8):
                    n8 = min(8, i + 1 - sc0)
                    for j8 in range(n8):
                        S.op("pe", lambda e, j8=j8: e.transpose(PTS(j8), mask01[:, (sc0 + j8) * 128:(sc0 + j8 + 1) * 128], identb),
                             reads=["mask01", cstb], writes=[ptb])
                    S.op("act", lambda e: e.activation(out=maskT[:, sc0:sc0 + n8, qs], in_=ptb[:, 0:n8 * 128].rearrange("p (a b) -> p a b", a=n8),
                                                       func=AF.Copy), reads=[ptb], writes=["maskT"])
            nsc = 4 * st + 4
            for hp in range(8):
                for half in range(2):
                    h = HPERM[2 * hp + half]
                    g = h // 4
                    hs = slice(half * 64, (half + 1) * 64)
                    for sc in range(nsc):
                        ps = P()
                        S.op("pe", lambda e, ps=ps, hs=hs, g=g, sc=sc: e.matmul(ps[:, :ST], kT2[hs, g // 2, sc * 128:(sc + 1) * 128], qT[hs, hp, :],
                                                                                start=True, stop=True), reads=[kT2, "qT"], writes=[ps])
                        c.ei = (c.ei + 1) % 2
                        E = Es[c.ei]
                        S.op("act", lambda e, ps=ps, E=E: e.activation(out=E[:], in_=ps[:, :ST], func=AF.Exp, scale=0.125), reads=[ps], writes=[E])
                        S.op("dve", lambda e, E=E, sc=sc: e.tensor_tensor(out=E[:], in0=E[:], in1=maskT[:, sc, :], op=ALU.mult),
                             reads=[E, "maskT"], writes=[E])
                        S.op("pe", lambda e, E=E, hs=hs, g=g, sc=sc: e.matmul(accA[hs, :ST], vtok[:, sc, g * 64:(g + 1) * 64], E[:],
                                                                              start=(sc == 0), stop=(sc == nsc - 1)), reads=[vtok, E], writes=[accA])
                        S.op("pe", lambda e, E=E, hs=hs, sc=sc: e.matmul(accB[hs, :ST], onesb[:, 0:64], E[:],
                                                                         start=(sc == 0), stop=(sc == nsc - 1)), reads=[cstb, E], writes=[accB])
                S.op("dve", lambda e: e.reciprocal(out=rden[:], in_=accB[:, :ST]), reads=[accB], writes=[rden])
                S.op("dve", lambda e, hp=hp: e.tensor_tensor(out=yattT[:, hp, :], in0=accA[:, :ST], in1=rden[:], op=ALU.mult),
                     reads=[accA, rden], writes=["yattT"])

        def merge(T):
            win = W["w_in"]
            S.alias("mergedT", ["acc"])
            for k in range(KC):
                def gate(c0, sg):
                    buf, view = wload(win, 0, KC, c0 + k * 128, 128)
                    ps = P()
                    for kk in range(KC):
                        S.op("pe", lambda e, kk=kk, ps=ps, view=view: e.matmul(ps[:, :T], view[:, kk, :], uT[:, kk, :T], start=(kk == 0),
                                                                              stop=(kk == KC - 1)), reads=[buf, uT], writes=[ps])
                    S.op("act", lambda e, ps=ps: e.activation(out=sg[:, :T], in_=ps[:, :T], func=AF.Sigmoid), reads=[ps], writes=[sg])
                gate(O_GS, sgs[0])
                buf, view = wload(W["w_br_ssd"], 0, 16, k * 128, 128)
                ps1 = P()
                for kk in range(16):
                    S.op("pe", lambda e, kk=kk, view=view: e.matmul(ps1[:, :T], view[:, kk, :], yssdT[:, kk, :T], start=(kk == 0), stop=(kk == 15)),
                         reads=[buf, "yssdT"], writes=[ps1])
                S.op("dve", lambda e: e.tensor_tensor(out=tmpf[:, :T], in0=ps1[:, :T], in1=sgs[0][:, :T], op=ALU.mult),
                     reads=[ps1, sgs[0]], writes=[tmpf])
                gate(O_GA, sgs[1])
                buf2, view2 = wload(W["w_br_att"], 0, KC, k * 128, 128)
                ps2 = P()
                for kk in range(KC):
                    S.op("pe", lambda e, kk=kk, view2=view2: e.matmul(ps2[:, :T], view2[:, kk, :], yattT[:, kk, :T], start=(kk == 0), stop=(kk == KC - 1)),
                         reads=[buf2, "yattT"], writes=[ps2])
                S.op("dve", lambda e: e.tensor_tensor(out=sqb[:, :T], in0=ps2[:, :T], in1=sgs[1][:, :T], op=ALU.mult),
                     reads=[ps2, sgs[1]], writes=[sqb])
                S.op("dve", lambda e, k=k: e.tensor_tensor(out=mergedT[:, k, :T], in0=tmpf[:, :T], in1=sqb[:, :T], op=ALU.add),
                     reads=[tmpf, sqb], writes=["mergedT"])
            S.alias("ybuf", SCRK)
            proj_fm(W["w_out"], 0, KC, 0, D, lambda k: mergedT[:, k, :T], ["mergedT"], T,
                    lambda idx, ps, ms: S.op("act", lambda e: e.activation(out=ybuf[:, idx, :T], in_=ps[:, :T], func=AF.Copy), reads=[ps], writes=["ybuf"]))
            postnorm_add("mix_post_g", T, 1.0)

        def mem_attn(T, batches):
            prenorm("mem_pre_g", T)
            proj_fm(W["w_mq"], 0, KC, 0, D, lambda k: uT[:, k, :T], [uT], T,
                    lambda idx, ps, ms: copy_alt(idx, qT[:, idx, :T], ps[:, :T], [ps], ["qT"]))
            for cs, bsel in batches:
                nT = cs.stop - cs.start
                if bsel is not None:
                    S.dma("mkl", lambda e: e.dma_start(out=mkTb[:], in_=cmkT_d[bsel].rearrange("(k p) m -> p k m", p=128)), writes=[mkTb], queue="pool")
                    S.dma("mvl", lambda e: e.dma_start(out=mvb[:], in_=cmv_d[bsel].rearrange("(k p) c -> p k c", p=128)), writes=[mvb], queue="pool")
                for h in range(4):
                    Em = []
                    for mc in range(2):
                        ps = P()
                        for dc in range(2):
                            S.op("pe", lambda e, ps=ps: e.matmul(ps[:, :nT], mkTb[:, h * 2 + dc, mc * 128:(mc + 1) * 128], qT[:, h * 2 + dc, cs],
                                                                 start=(dc == 0), stop=(dc == 1)), reads=[mkTb, "qT"], writes=[ps])
                        E = Es[mc]
                        S.op("act", lambda e, ps=ps, E=E: e.activation(out=E[:, :nT], in_=ps[:, :nT], func=AF.Exp, scale=1.0 / 16.0), reads=[ps], writes=[E])
                        Em.append(E)
                    for mc in range(2):
                        S.op("pe", lambda e: e.matmul(accB[:, :nT], onesb, Em[mc][:, :nT], start=(mc == 0), stop=(mc == 1)), reads=[cstb, Em[mc]], writes=[accB])
                    S.op("dve", lambda e: e.reciprocal(out=rden[:, :nT], in_=accB[:, :nT]), reads=[accB], writes=[rden])
                    for dc in range(2):
                        for mc in range(2):
                            S.op("pe", lambda e: e.matmul(accA[:, :nT], mvb[:, mc, h * 256 + dc * 128:h * 256 + (dc + 1) * 128], Em[mc][:, :nT],
                                                          start=(mc == 0), stop=(mc == 1)), reads=[mvb, Em[mc]], writes=[accA])
                        S.op("dve", lambda e: e.tensor_tensor(out=yattT[:, h * 2 + dc, cs], in0=accA[:, :nT], in1=rden[:, :nT], op=ALU.mult),
                             reads=[accA, rden], writes=["yattT"])
            proj_fm(W["w_mo"], 0, KC, 0, D, lambda k: yattT[:, k, :T], ["yattT"], T,
                    lambda idx, ps, ms: S.op("act", lambda e: e.activation(out=ybuf[:, idx, :T], in_=ps[:, :T], func=AF.Copy), reads=[ps], writes=["ybuf"]))
            postnorm_add("mem_post_g", T, 1.0)

        def memory_kv_prompt():
            S.dma("xin", lambda e: e.dma_start(out=hT[:, :, 0:256], in_=memT.rearrange("(k p) t -> p k t", p=128)), writes=[hT])
            prenorm("mem_kv_g", 256)

            def cons(idx, ps, ms):
                S.op("dve", lambda e: e.tensor_copy(out=mkTb[:, idx, :], in_=ps[:, 0:256]), reads=[ps], writes=[mkTb])
                stage_out(ps[:, 0:256], 128, 256, mkT_o[idx * 128:(idx + 1) * 128, :], (ps, "mkT_o"))
            proj_fm(W["w_mk"], 0, KC, 0, D, lambda k: uT[:, k, 0:256], [uT], 256, cons)
            for cb in range(2):
                buf, view = wload(W["w_mv"], 0, KC, cb * 512, 512)
                for mc in range(2):
                    ps = P()
                    tm_proj(view, buf, 0, 512, mc, ps)
                    S.op("dve", lambda e, mc=mc, cb=cb, ps=ps: e.tensor_copy(out=mvb[:, mc, cb * 512:(cb + 1) * 512], in_=ps[:, :]), reads=[ps], writes=[mvb])
                    stage_out(ps[:, :], 128, 512, mv_o[mc * 128:(mc + 1) * 128, cb * 512:(cb + 1) * 512], (ps, "mv_o"))


        def sample_path():
            T = 16
            cst2 = S.sb([128, C2N], F32, "cst2")
            S.dma("cst2", lambda e: e.dma_start(out=cst2[:], in_=cst2_d), writes=[cst2])
            Gblk = cst2[:, 0:128]
            Gsel = cst2[:, 128:132]
            negnew = cst2[:, 132:136]
            pmod64 = cst2[:, 136:137]
            Rep = cst2[0:4, 137:265]
            Dsel = cst2[0:32, 265:269]
            hsel = cst2[0:8, 269:301]
            onesf = cst[:, 384:512]
            kibs = [kT2[:].rearrange("p a b -> p (a b)")[:, i * 2048:(i + 1) * 2048].bitcast(F32) for i in range(2)]
            vt_f = vtok[:].rearrange("p a b -> p (a b)").bitcast(F32)
            Ksel = vt_f[:, 0:1024].rearrange("p (c d) -> p c d", c=4)
            Vsel = vt_f[:, 1024:2048].rearrange("p (c d) -> p c d", c=4)
            qrow = kiT2[:].bitcast(F32)
            prodb = mask01t[:].bitcast(F32)
            kv4 = stT[:].rearrange("p g c -> p (g c)")[0:4, :].rearrange("p (b c) -> p b c", b=4)
            for nk_, ok_ in (("kib0", kT2), ("kib1", kT2), ("KV", vtok), ("qrow", kiT2), ("prodb", mask01t), ("kv4", stT)):
                S.alias(nk_, [ok_])
            hists = S.sb([128, 32, 4, 3], F32, "hists")
            S.dma("hists", lambda e: e.dma_start(out=hists[:].rearrange("p a b c -> p (a b c)"), in_=hists_d), writes=[hists])
            pres = S.sb([128, 4, 4, 7], F32, "pres")
            stS = [S.sb([128, 256], F32, f"stS{i}") for i in range(2)]
            stSb = [S.sb([128, 256], BF16, f"stSb{i}") for i in range(2)]
            sc = S.sb([128, 516], F32, "sc")
            work = S.sb([128, 512], F32, "work")
            qiU = S.sb([128, 4, 32], F32, "qiU")
            qiE = S.sb([128, 4, 32], F32, "qiE")
            qiO = S.sb([128, 4, 32], F32, "qiO")
            kiTs = S.sb([64, 16], F32, "kiTs")
            wiT = S.sb([8, 16], F32, "wiT")
            WW = S.sb([32, 252], F32, "WW")
            ptrow = S.sb([128, 128], I32, "ptrow")
            ptf = S.sb([128, 128], F32, "ptf")
            kix = S.sb([128, 128], I32, "kix")
            ptji = S.sb([128, 4], I32, "ptji")
            smf = S.sb([128, 256], F32, "smf")
            smi = S.sb([128, 96], I32, "smi")
            rowi = S.sb([128, 32], I32, "rowi")
            opart = S.sb([128, 1024], F32, "opart")
            q4 = opart[0:4, :]
            sE = S.sb([128, 64], F32, "sE")
            vals = smf[:, 0:32]
            valid = smf[:, 32:64]
            validn = smf[:, 64:68]
            wicol = smf[0:32, 68:69]
            rmax = smf[:, 69:70]
            Rr, lo, stp, cand, cnt, gg = [smf[:, 70 + i:71 + i] for i in range(6)]
            ptjf = smf[:, 76:80]
            pglf = smf[:, 80:112]
            offf = smf[:, 112:144]
            physf = smf[:, 144:176]
            tmp32 = smf[:, 176:208]
            denp = smf[:, 208:224]
            tmp16 = smf[:, 224:240]
            rd4 = smf[:, 240:244]
            rn4 = smf[0:32, 244:248]
            S.op("dve", lambda e: e.memset(WW[:], 0.0), writes=[WW])

            S.dma("xin", lambda e: e.dma_start(out=hT[:, :, 0:T], in_=xsT.rearrange("(k p) t -> p k t", p=128)), writes=[hT])
            ffn("ffn1", T)
            for k_ in MIXK:
                S.alias(k_, ["hid"])
            for k_ in SCRK:
                S.alias(k_, ["ybuf"])
            S.alias("ssdtmp", ["maskT"])
            prenorm("mix_pre_g", T)
            win = W["w_in"]
            buf, view = wload(win, 0, KC, O_K, 512)
            for b in range(4):
                ps = P()
                for k in range(KC):
                    S.op("pe", lambda e, k=k, ps=ps, view=view: e.matmul(ps[0:4, 0:512], uT[:, k, b * 4:(b + 1) * 4], view[:, k, :],
                                                                        start=(k == 0), stop=(k == KC - 1)), reads=[buf, uT], writes=[ps])
                S.op("act", lambda e, ps=ps: e.activation(out=kv4[:, b, :], in_=ps[0:4, 0:512], func=AF.Copy), reads=[ps], writes=["kv4"])
                S.dma("kso", lambda e: e.dma_start(out=ks_o[b * 4:(b + 1) * 4, :], in_=kv4[:, b, 0:256]), reads=["kv4"], writes=["ks_o"])
                S.dma("vso", lambda e: e.dma_start(out=vs_o[b * 4:(b + 1) * 4, :], in_=kv4[:, b, 256:512]), reads=["kv4"], writes=["vs_o"])
            buf, view = wload(win, 0, KC, O_KI - 32, 128)
            ps = P()
            for k in range(KC):
                S.op("pe", lambda e, k=k, ps=ps, view=view: e.matmul(ps[0:64, 0:T], view[:, k, 32:96], uT[:, k, :T], start=(k == 0), stop=(k == KC - 1)),
                     reads=[buf, uT], writes=[ps])
            S.op("act", lambda e, ps=ps: e.activation(out=kiTs[:], in_=ps[0:64, 0:T], func=AF.Copy), reads=[ps], writes=[kiTs])
            S.dma("kiso", lambda e: e.dma_start(out=kisT_o, in_=kiTs[:]), reads=[kiTs], writes=["kisT_o"])
            ps = P()
            for k in range(KC):
                S.op("pe", lambda e, k=k, ps=ps, view=view: e.matmul(ps[0:8, 0:T], view[:, k, 96:104], uT[:, k, :T], start=(k == 0), stop=(k == KC - 1)),
                     reads=[buf, uT], writes=[ps])
            S.op("act", lambda e, ps=ps: e.activation(out=wiT[:], in_=ps[0:8, 0:T], func=AF.Copy), reads=[ps], writes=[wiT])
            buf, view = wload(win, 0, KC, O_DT, 128)
            for b in range(4):
                ps = P()
                for k in range(KC):
                    S.op("pe", lambda e, k=k, ps=ps, view=view: e.matmul(ps[0:4, 0:32], uT[:, k, b * 4:(b + 1) * 4], view[:, k, 0:32],
                                                                        start=(k == 0), stop=(k == KC - 1)), reads=[buf, uT], writes=[ps])
                S.op("dve", lambda e, ps=ps: e.tensor_tensor(out=dtt[0:4, b, :], in0=ps[0:4, 0:32], in1=rowp[0:4, 0:32], op=ALU.add),
                     reads=[ps, rowp], writes=[dtt])
            S.op("act", lambda e: e.activation(out=dtt[0:4], in_=dtt[0:4], func=AF.Exp), reads=[dtt], writes=[dtt])
            S.op("act", lambda e: e.activation(out=dtt[0:4], in_=dtt[0:4], func=AF.Ln, bias=1.0), reads=[dtt], writes=[dtt])
            for b in range(4):
                S.op("dve", lambda e: e.tensor_tensor(out=dta[0:4, b, :], in0=dtt[0:4, b, :], in1=arow[0:4, :], op=ALU.mult),
                     reads=[dtt, arow], writes=[dta])
                ps = P()
                S.op("pe", lambda e, ps=ps: e.matmul(ps[0:4, 0:32], Uf[0:4, 0:4], dta[0:4, b, :], start=True, stop=True), reads=[cst, dta], writes=[ps])
                S.op("act", lambda e, ps=ps: e.activation(out=acsc[0:4, b, :], in_=ps[0:4, 0:32], func=AF.Copy), reads=[ps], writes=[acsc])
            buf, view = wload(win, 0, KC, O_QI, 512)
            ps = P()
            for h in range(8):
                for half in range(2):
                    for k in range(KC):
                        S.op("pe", lambda e, k=k, ps=ps, view=view: e.matmul(ps[half * 64:(half + 1) * 64, h * 16:(h + 1) * 16], view[:, k, h * 64:(h + 1) * 64],
                                                                            uT[:, k, :T], start=(k == 0), stop=(k == KC - 1)), reads=[buf, uT], writes=[ps])
            S.op("act", lambda e, ps=ps: e.activation(out=qiU[:].rearrange("p b (h t) -> p h b t", h=8), in_=ps[:, 0:128].rearrange("p (h b t) -> p h b t", h=8, b=4),
                                                    func=AF.Copy), reads=[ps], writes=[qiU])
            S.op("dve", lambda e: e.memset(qiE[:], 0.0), writes=[qiE])
            S.op("dve", lambda e: e.memset(qiO[:], 0.0), writes=[qiO])
            S.op("dve", lambda e: e.tensor_copy(out=qiE[0:64], in_=qiU[0:64]), reads=[qiU], writes=[qiE])
            S.op("dve", lambda e: e.tensor_copy(out=qiO[64:128], in_=qiU[64:128]), reads=[qiU], writes=[qiO])
            for g in range(8):
                buf, view = wload(win, 0, KC, O_Z + g * 256, 256)
                for m in range(2):
                    ps = P()
                    for k in range(KC):
                        S.op("pe", lambda e, k=k, ps=ps, view=view: e.matmul(ps[:, :T], view[:, k, m * 128:(m + 1) * 128], uT[:, k, :T],
                                                                            start=(k == 0), stop=(k == KC - 1)), reads=[buf, uT], writes=[ps])
                    S.op("act", lambda e, ps=ps: e.activation(out=zs[:, m, :T], in_=ps[:, :T], func=AF.Silu), reads=[ps], writes=["zs"])
                chs = [g * 2, g * 2 + 1, 16 + g, 24 + g]
                for j in range(4):
                    if j == 0:
                        bufx, viewx = wload(win, 0, KC, O_X + g * 256, 256)
                    if j == 2:
                        bufx, viewx = wload(win, 0, KC, O_B + g * 128, 128)
                    if j == 3:
                        bufx, viewx = wload(win, 0, KC, O_C + g * 128, 128)
                    c0 = 128 if j == 1 else 0
                    ps = P()
                    for k in range(KC):
                        S.op("pe", lambda e, k=k, ps=ps, viewx=viewx: e.matmul(ps[:, :T], viewx[:, k, c0:c0 + 128], uT[:, k, :T],
                                                                              start=(k == 0), stop=(k == KC - 1)), reads=[bufx, uT], writes=[ps])
                    ch = chs[j]
                    S.op("dve", lambda e: e.tensor_copy(out=pres[:, j, :, 0:3], in_=hists[:, ch, :, :]), reads=[hists], writes=[pres])
                    S.op("act", lambda e, ps=ps: e.activation(out=pres[:, j, :, 3:7], in_=ps[:, 0:T].rearrange("p (b t) -> p b t", b=4), func=AF.Copy),
                         reads=[ps], writes=[pres])
                    S.op("dve", lambda e: e.tensor_copy(out=hists[:, ch, :, :], in_=pres[:, j, :, 4:7]), reads=[pres], writes=[hists])
                    cwf = lambda tap: pp[:, PP_CW + ch * 4 + tap:PP_CW + ch * 4 + tap + 1]
                    tv = tmpf[:, 0:T].rearrange("p (b t) -> p b t", b=4)
                    S.op("dve", lambda e: e.tensor_scalar(out=tv, in0=pres[:, j, :, 0:4], scalar1=cwf(0), scalar2=pp[:, PP_CB + ch:PP_CB + ch + 1],
                                                          op0=ALU.mult, op1=ALU.add), reads=[pres, pp], writes=[tmpf])
                    for tap in range(1, 4):
                        S.op("dve", lambda e: e.scalar_tensor_tensor(out=tv, in0=pres[:, j, :, tap:tap + 4], scalar=cwf(tap), in1=tv,
                                                                     op0=ALU.mult, op1=ALU.add), reads=[pres, pp, tmpf], writes=[tmpf])
                    S.op("act", lambda e: e.activation(out=cv[:, j, :T], in_=tmpf[:, :T], func=AF.Silu), reads=[tmpf], writes=[("cv", j)])
                for b in range(4):
                    si = (g * 4 + b) % 2
                    sF, sB = stS[si], stSb[si]
                    S.dma(("sts", si), lambda e: e.dma_start(out=sF[:], in_=ssmsT_d[b, :, g * 256:(g + 1) * 256]), writes=[sF])
                    S.op("act", lambda e: e.activation(out=sB[:], in_=sF[:], func=AF.Copy), reads=[sF], writes=[sB])
                    ssd_chunk(g, b, 4, b * 4, sf=lambda hh, sF=sF: sF[:, hh * 64:(hh + 1) * 64], sbf=lambda hh, sB=sB: sB[:, hh * 64:(hh + 1) * 64],
                              skeys=(sF, sB), sfa=sF[:], sba=sB[:])
                    S.dma(("sto", si), lambda e: e.dma_start(out=ssms_o[b, :, g * 256:(g + 1) * 256], in_=sF[:]), reads=[sF], writes=["ssms_o"])
                for m in range(2):
                    S.op("dve", lambda e: e.tensor_tensor(out=Yg[:, m, :T], in0=Yg[:, m, :T], in1=zs[:, m, :T], op=ALU.mult),
                         reads=[Yg, "zs"], writes=[Yg])
                norm_stats([Yg[:, 0, :T], Yg[:, 1, :T]], Yg, T, 256.0)
                for m in range(2):
                    S.op("dve", lambda e: e.scalar_tensor_tensor(out=yssdT[:, g * 2 + m, :T], in0=Yg[:, m, :T],
                                                                 scalar=pp[:, PP_NG + g * 2 + m:PP_NG + g * 2 + m + 1], in1=rstd[:, :T],
                                                                 op0=ALU.mult, op1=ALU.mult), reads=[Yg, rstd, pp], writes=["yssdT"])
            S.dma("convso", lambda e: e.dma_start(out=convs_o, in_=hists[:].rearrange("p a b c -> p (a b c)")), reads=[hists], writes=["convs_o"])
            for b in range(4):
                cs = slice(b * 4, b * 4 + 4)
                S.dma("ptrow", lambda e: e.dma_start(out=ptrow[:], in_=ptab_d[b:b + 1, :].partition_broadcast(128)), writes=[ptrow])
                S.op("dve", lambda e: e.tensor_copy(out=ptf[:], in_=ptrow[:]), reads=[ptrow], writes=[ptf])
                t64 = work[:, 0:64]
                S.op("dve", lambda e: e.tensor_tensor(out=t64, in0=ptf[:, 1:128:2], in1=ptf[:, 0:127:2], op=ALU.subtract), reads=[ptf], writes=[work])
                S.op("dve", lambda e: e.scalar_tensor_tensor(out=t64, in0=t64, scalar=cst2[:, 941:942], in1=ptf[:, 0:127:2], op0=ALU.mult, op1=ALU.add),
                     reads=[work, ptf, cst2], writes=[work])
                S.op("dve", lambda e: e.tensor_scalar(out=t64, in0=t64, scalar1=64.0, scalar2=pmod64, op0=ALU.mult, op1=ALU.add),
                     reads=[work, cst2], writes=[work])
                S.op("dve", lambda e: e.tensor_copy(out=kix[:, 0:64], in_=t64), reads=[work], writes=[kix])
                for par in range(2):
                    S.dma("ptji", lambda e: e.dma_start(out=ptji[par * 16:(par + 1) * 16, :], in_=ptab_d[b, :].rearrange("(i e) -> i e", e=8)[:, par:par + 7:2],
                                                        allow_slow_non_contiguous=True), writes=[ptji])
                S.op("dve", lambda e: e.tensor_copy(out=tmp16[0:32, 0:4], in_=ptji[0:32, :]), reads=[ptji], writes=[smf])
                ps = P()
                S.op("pe", lambda e, ps=ps: e.matmul(ps[:, 0:4], cst2[0:32, 813:941], tmp16[0:32, 0:4], start=True, stop=True), reads=[cst2, smf], writes=[ps])
                S.op("dve", lambda e, ps=ps: e.tensor_copy(out=ptjf, in_=ps[:, 0:4]), reads=[ps], writes=[smf])
                ps = P()
                S.op("pe", lambda e, ps=ps: e.matmul(ps[0:32, 0:4], hsel, wiT[0:8, cs], start=True, stop=True), reads=[cst2, wiT], writes=[ps])
                S.op("dve", lambda e, ps=ps: e.tensor_tensor(out=rn4, in0=ps[0:32, 0:4], in1=Dsel, op=ALU.mult), reads=[ps, cst2], writes=[smf])
                S.op("dve", lambda e: e.tensor_reduce(out=wicol, in_=rn4, axis=AX.X, op=ALU.add), reads=[smf], writes=[smf])
                S.op("dve", lambda e: e.tensor_scalar(out=WW[:, 124:128], in0=Dsel, scalar1=wicol, scalar2=None, op0=ALU.mult), reads=[smf, cst2], writes=[WW])
                qil = qiU[0:64, b, :]
                nmm = 0
                for i8 in range(16):
                    kb = kibs[i8 % 2]
                    kkey = f"kib{i8 % 2}"
                    for c4 in range(4):
                        pq = i8 * 4 + c4
                        S.dma(("kibd", i8 % 2), lambda e: e.indirect_dma_start(
                            out=kb[:, c4 * 128:(c4 + 1) * 128], out_offset=None, in_=kidxT_d,
                            in_offset=bass.IndirectOffsetOnAxis(ap=kix[:, pq:pq + 1], axis=0)), reads=[kix], writes=[kkey], queue="pool")
                    for par in range(2):
                        jv = par * 16 + i8
                        ps = P()
                        S.op("pe", lambda e, ps=ps: e.matmul(ps[0:32, 0:512], (qiE if par == 0 else qiO)[:, b, :], kb[:, 0:512], start=True, stop=True),
                             reads=[qiE, qiO, kkey], writes=[ps])
                        rr = rrs[nmm % 2]
                        S.op("act", lambda e, ps=ps: e.activation(out=rr[0:32, :], in_=ps[0:32, 0:512], func=AF.Relu), reads=[ps], writes=[rr])
                        S.op("pe", lambda e: e.matmul(accA[:, 0:512], WW[:, 124 - 4 * jv:252 - 4 * jv], rr[0:32, :], start=(nmm == 0), stop=(nmm == 31)),
                             reads=[WW, rr], writes=[accA])
                        nmm += 1
                ps = P()
                S.op("pe", lambda e, ps=ps: e.matmul(ps[0:32, 0:4], qil, kiTs[:, cs], start=True, stop=True), reads=[qiU, kiTs], writes=[ps])
                S.op("act", lambda e, ps=ps: e.activation(out=rn4, in_=ps[0:32, 0:4], func=AF.Relu), reads=[ps], writes=[smf])
                S.op("pe", lambda e: e.matmul(accB[:, 0:4], WW[:, 124:252], rn4, start=True, stop=True), reads=[WW, smf], writes=[accB])
                S.op("act", lambda e: e.activation(out=sc[:, 0:512], in_=accA[:, 0:512], func=AF.Copy), reads=[accA], writes=[sc])
                S.op("dve", lambda e: e.tensor_tensor(out=sc[:, 512:516], in0=accB[:, 0:4], in1=negnew, op=ALU.add), reads=[accB, cst2], writes=[sc])
                S.op("dve", lambda e: e.tensor_reduce(out=rmax, in_=sc[:, 0:512], axis=AX.X, op=ALU.max, apply_absolute_value=True), reads=[sc], writes=[smf])
                ps = P()
                S.op("pe", lambda e, ps=ps: e.matmul(ps[:, 0:1], onesf, rmax, start=True, stop=True), reads=[cst, smf], writes=[ps])
                S.op("dve", lambda e, ps=ps: e.tensor_copy(out=Rr, in_=ps[:, 0:1]), reads=[ps], writes=[smf])
                S.op("dve", lambda e: e.tensor_scalar(out=lo, in0=Rr, scalar1=-1.0, scalar2=None, op0=ALU.mult), reads=[smf], writes=[smf])
                for it in range(1, NITS + 1):
                    S.op("dve", lambda e: e.tensor_scalar(out=stp, in0=Rr, scalar1=2.0 ** (1 - it), scalar2=None, op0=ALU.mult), reads=[smf], writes=[smf])
                    S.op("dve", lambda e: e.tensor_tensor(out=cand, in0=lo, in1=stp, op=ALU.add), reads=[smf], writes=[smf])
                    S.op("dve", lambda e: e.tensor_scalar(out=work[:, 0:516 - 4], in0=sc[:, 0:512], scalar1=cand, scalar2=None, op0=ALU.is_ge,
                                                          op1=ALU.add, accum_out=cnt), reads=[sc, smf], writes=[work, smf])
                    S.op("dve", lambda e: e.tensor_scalar(out=tmp16[:, 0:4], in0=sc[:, 512:516], scalar1=cand, scalar2=None, op0=ALU.is_ge,
                                                          op1=ALU.add, accum_out=gg), reads=[sc, smf], writes=[smf])
                    S.op("dve", lambda e: e.tensor_tensor(out=cnt, in0=cnt, in1=gg, op=ALU.add), reads=[smf], writes=[smf])
                    ps = P()
                    S.op("pe", lambda e, ps=ps: e.matmul(ps[:, 0:1], Gblk, cnt, start=True, stop=True), reads=[cst2, smf], writes=[ps])
                    S.op("dve", lambda e, ps=ps: e.tensor_scalar(out=gg, in0=ps[:, 0:1], scalar1=255.5, scalar2=None, op0=ALU.is_ge), reads=[ps], writes=[smf])
                    S.op("dve", lambda e: e.scalar_tensor_tensor(out=lo, in0=gg, scalar=stp, in1=lo, op0=ALU.mult, op1=ALU.add), reads=[smf], writes=[smf])
                S.op("dve", lambda e: e.tensor_copy(out=work[:], in_=sc[:, 0:512]), reads=[sc], writes=[work])
                cidx = smi[:, 0:32]
                for r in range(4):
                    S.op("dve", lambda e: e.max(out=vals[:, r * 8:(r + 1) * 8], in_=work[:]), reads=[work], writes=[smf])
                    S.op("dve", lambda e: e.max_index(out=cidx[:, r * 8:(r + 1) * 8].bitcast(U32), in_max=vals[:, r * 8:(r + 1) * 8], in_values=work[:]),
                         reads=[work, smf], writes=[smi])
                    S.op("dve", lambda e: e.match_replace(out=work[:], in_to_replace=vals[:, r * 8:(r + 1) * 8], in_values=work[:], imm_value=-3e38),
                         reads=[work, smf], writes=[work])
                S.op("dve", lambda e: e.tensor_scalar(out=valid, in0=vals, scalar1=lo, scalar2=None, op0=ALU.is_ge), reads=[smf], writes=[smf])
                S.op("dve", lambda e: e.tensor_scalar(out=validn, in0=sc[:, 512:516], scalar1=lo, scalar2=None, op0=ALU.is_ge), reads=[smf, sc], writes=[smf])
                S.op("dve", lambda e: e.tensor_scalar(out=smi[:, 32:64], in0=cidx, scalar1=7, scalar2=None, op0=ALU.arith_shift_right), reads=[smi], writes=[smi])
                S.op("dve", lambda e: e.tensor_scalar(out=smi[:, 64:96], in0=cidx, scalar1=127, scalar2=None, op0=ALU.bitwise_and), reads=[smi], writes=[smi])
                S.op("dve", lambda e: e.tensor_copy(out=pglf, in_=smi[:, 32:64]), reads=[smi], writes=[smf])
                S.op("dve", lambda e: e.tensor_copy(out=offf, in_=smi[:, 64:96]), reads=[smi], writes=[smf])
                for kq in range(4):
                    dst = physf if kq == 0 else tmp32
                    S.op("dve", lambda e: e.tensor_scalar(out=dst, in0=pglf, scalar1=float(kq), scalar2=ptjf[:, PGP[kq]:PGP[kq] + 1], op0=ALU.is_equal, op1=ALU.mult),
                         reads=[smf], writes=[smf])
                    if kq > 0:
                        S.op("dve", lambda e: e.tensor_tensor(out=physf, in0=physf, in1=tmp32, op=ALU.add), reads=[smf], writes=[smf])
                S.op("dve", lambda e: e.scalar_tensor_tensor(out=physf, in0=physf, scalar=128.0, in1=offf, op0=ALU.mult, op1=ALU.add), reads=[smf], writes=[smf])
                S.op("dve", lambda e: e.tensor_copy(out=rowi[:], in_=physf), reads=[smf], writes=[rowi])
                for half in range(2):
                    buf, view = wload(win, 0, KC, O_Q + half * 512, 512)
                    ps = P()
                    for k in range(KC):
                        S.op("pe", lambda e, k=k, ps=ps, view=view: e.matmul(ps[0:4, 0:512], uT[:, k, cs], view[:, k, :], start=(k == 0), stop=(k == KC - 1)),
                             reads=[buf, uT], writes=[ps])
                    S.op("act", lambda e, ps=ps: e.activation(out=q4[:, half * 512:(half + 1) * 512], in_=ps[0:4, 0:512], func=AF.Copy), reads=[ps], writes=[opart])
                    ps = P()
                    S.op("pe", lambda e, ps=ps: e.matmul(ps[:, 0:512], Rep, q4[:, half * 512:(half + 1) * 512], start=True, stop=True), reads=[cst2, opart], writes=[ps])
                    S.op("act", lambda e, ps=ps: e.activation(out=qrow[:, half * 512:(half + 1) * 512], in_=ps[:, 0:512], func=AF.Copy, scale=0.125),
                         reads=[ps], writes=["qrow"])
                qrow3 = qrow.rearrange("p (i d) -> p i d", i=16)
                opart3 = opart[:].rearrange("p (i d) -> p i d", i=16)
                S.op("dve", lambda e: e.memset(opart[:], 0.0), writes=[opart])
                S.op("dve", lambda e: e.memset(denp, 0.0), writes=[smf])
                s_all = sE[:]
                s_all3 = sE[:].rearrange("p (c i) -> p c i", c=4)
                sT3 = sE[:].rearrange("p (c i) -> p i c", c=4)
                q4acc = work[:, 0:256].rearrange("p (r d) -> p r d", r=4)
                prodK = prodb[:, 0:1024].rearrange("p (c r d) -> p c r d", c=4, r=4)
                prodV = prodb[:, 0:1024].rearrange("p (r d c) -> p r d c", r=4, d=64)
                for rnd in range(9):
                    if rnd < 8:
                        for c4 in range(4):
                            cc_ = rnd * 4 + c4
                            S.dma("ksel", lambda e: e.indirect_dma_start(out=Ksel[:, c4, :], out_offset=None, in_=poolk_d,
                                                                         in_offset=bass.IndirectOffsetOnAxis(ap=rowi[:, cc_:cc_ + 1], axis=0)),
                                  reads=[rowi], writes=["KV"], queue="pool")
                            S.dma("vsel", lambda e: e.indirect_dma_start(out=Vsel[:, c4, :], out_offset=None, in_=poolv_d,
                                                                         in_offset=bass.IndirectOffsetOnAxis(ap=rowi[:, cc_:cc_ + 1], axis=0)),
                                  reads=[rowi], writes=["KV"], queue="pool")
                        vmask = valid[:, rnd * 4:(rnd + 1) * 4]
                    else:
                        for tq in range(4):
                            ps = P()
                            S.op("pe", lambda e, ps=ps: e.matmul(ps[:, 0:512], cst2[0:4, 301 + tq * 128:301 + (tq + 1) * 128], kv4[:, b, :], start=True, stop=True),
                                 reads=[cst2, "kv4"], writes=[ps])
                            S.op("act", lambda e, ps=ps: e.activation(out=Ksel[:, tq, :], in_=ps[:, 0:256], func=AF.Copy), reads=[ps], writes=["KV"])
                            S.op("act", lambda e, ps=ps: e.activation(out=Vsel[:, tq, :], in_=ps[:, 256:512], func=AF.Copy), reads=[ps], writes=["KV"])
                        vmask = validn
                    for g in range(4):
                        base = (g // 2) * 8 + (g % 2)
                        S.op("dve", lambda e: e.tensor_tensor(out=prodK, in0=Ksel[:, :, g * 64:(g + 1) * 64].unsqueeze(2).to_broadcast([128, 4, 4, 64]),
                                                              in1=qrow3[:, base:base + 7:2, :].unsqueeze(1).to_broadcast([128, 4, 4, 64]), op=ALU.mult),
                             reads=["KV", "qrow"], writes=["prodb"])
                        S.op("dve", lambda e: e.tensor_reduce(out=s_all3[:, :, base:base + 7:2], in_=prodK, axis=AX.X, op=ALU.add), reads=["prodb"], writes=[sE])
                    S.op("act", lambda e: e.activation(out=s_all, in_=s_all, func=AF.Exp), reads=[sE], writes=[sE])
                    S.op("dve", lambda e: e.tensor_tensor(out=s_all3, in0=s_all3, in1=vmask.unsqueeze(2).to_broadcast([128, 4, 16]), op=ALU.mult),
                         reads=[sE, smf], writes=[sE])
                    S.op("dve", lambda e: e.tensor_reduce(out=tmp16, in_=sT3, axis=AX.X, op=ALU.add), reads=[sE], writes=[smf])
                    S.op("dve", lambda e: e.tensor_tensor(out=denp, in0=denp, in1=tmp16, op=ALU.add), reads=[smf], writes=[smf])
                    for g in range(4):
                        base = (g // 2) * 8 + (g % 2)
                        S.op("dve", lambda e: e.tensor_tensor(out=prodV, in0=sT3[:, base:base + 7:2, :].unsqueeze(2).to_broadcast([128, 4, 64, 4]),
                                                              in1=Vsel[:, :, g * 64:(g + 1) * 64].rearrange("p c d -> p d c").unsqueeze(1).to_broadcast([128, 4, 64, 4]),
                                                              op=ALU.mult), reads=[sE, "KV"], writes=["prodb"])
                        S.op("dve", lambda e: e.tensor_reduce(out=q4acc, in_=prodV, axis=AX.X, op=ALU.add), reads=["prodb"], writes=[work])
                        S.op("dve", lambda e: e.tensor_tensor(out=opart3[:, base:base + 7:2, :], in0=opart3[:, base:base + 7:2, :], in1=q4acc, op=ALU.add),
                             reads=[work, opart], writes=[opart])
                ps = P()
                S.op("pe", lambda e, ps=ps: e.matmul(ps[:, 0:16], Gblk, denp, start=True, stop=True), reads=[cst2, smf], writes=[ps])
                S.op("dve", lambda e, ps=ps: e.reciprocal(out=tmp16, in_=ps[:, 0:16]), reads=[ps], writes=[smf])
                S.op("dve", lambda e: e.tensor_tensor(out=opart3, in0=opart3, in1=tmp16.unsqueeze(2).to_broadcast([128, 16, 64]), op=ALU.mult),
                     reads=[opart, smf], writes=[opart])
                for hp in range(8):
                    ps = P()
                    S.op("pe", lambda e, ps=ps: e.matmul(ps[:, 0:4], opart[:, hp * 128:(hp + 1) * 128], Gsel, start=True, stop=True), reads=[opart, cst2], writes=[ps])
                    S.op("dve", lambda e, ps=ps: e.tensor_copy(out=yattT[:, hp, cs], in_=ps[:, 0:4]), reads=[ps], writes=["yattT"])
            merge(T)
            mem_attn(T, [(slice(b * 4, b * 4 + 4), b) for b in range(4)])
            ffn("ffn2", T)
            S.dma("yout", lambda e: e.dma_start(out=ysT_o.rearrange("(k p) t -> p k t", p=128), in_=hT[:, :, 0:T]), reads=[hT], writes=["ysT_o"])

        SKIP = os.environ.get("KSKIP", "").split(",")
        if "kv" not in SKIP:
            memory_kv_prompt()
        for st in range(int(os.environ.get("KNST", NST))):
            t0 = st * ST
            S.dma("xin", lambda e, t0=t0: e.dma_start(out=hT[:], in_=xT[:, t0:t0 + ST].rearrange("(k p) t -> p k t", p=128)),
                  writes=[hT])
            ffn("ffn1", ST)
            if os.environ.get("KSTAGE", "all") != "ffn":
                mix_prompt(st)
            if os.environ.get("KSTAGE", "all") == "all":
                mem_attn(ST, [(slice(0, ST), None)])
            ffn("ffn2", ST)
            S.dma("yout", lambda e, t0=t0: e.dma_start(out=yT_o[:, t0:t0 + ST].rearrange("(k p) t -> p k t", p=128), in_=hT[:]),
                  reads=[hT], writes=["yT_o"])
        if "ssmo" not in SKIP:
          S.dma("ssmo", lambda e: e.dma_start(out=ssmT_o, in_=stT[:].rearrange("p g c -> p (g c)")), reads=[stT], writes=["ssmT_o"])
        if "convo" not in SKIP:
          S.dma("convo", lambda e: e.dma_start(out=convT_o.rearrange("(k p) j -> p k j", p=128), in_=hist[:]), reads=[hist], writes=["convT_o"])
        if "sample" not in SKIP:
            sample_path()
        S.emit()
    return nc


def _consts():
    s = np.arange(128)
    ident = np.eye(128, dtype=np.float32)
    U = (s[:, None] <= s[None, :]).astype(np.float32)
    negm = np.where(s[None, :] <= s[:, None], 0.0, -1e30).astype(np.float32)
    ones = np.ones((128, 128), np.float32)
    tri = U.copy()
    return np.concatenate([ident, U, negm, ones, tri], axis=1)


def _fm(v, n):
    return np.ascontiguousarray(np.asarray(v, np.float32).reshape(n, 128).T)


def kernel(**inp):
    inp = {k: np.asarray(v) for k, v in inp.items()}
    nc = build_nc()
    pp = np.zeros((128, PP_N), np.float32)
    for n, o in PP_G.items():
        pp[:, o:o + 8] = _fm(inp[n][0], 8)
    pp[:, PP_NG:PP_NG + 16] = _fm(inp["ssd_norm_g"][0], 16)
    pp[:, PP_CB:PP_CB + 32] = _fm(inp["conv_b"][0], 32)
    cw = inp["conv_w"][0]
    pp[:, PP_CW:PP_CW + 128] = cw.T.reshape(32, 128, 4).transpose(1, 0, 2).reshape(128, 128)
    pp[:, PP_DS:PP_DS + 16] = _fm(np.repeat(inp["d_skip"][0], 64), 16)
    rowp = np.zeros((128, 64), np.float32)
    rowp[:, 0:32] = inp["dt_bias"][0][None, :]
    rowp[:, 32:64] = inp["a_log"][0][None, :]
    cst = _consts()
    wnames = ["ffn1_wg", "ffn1_wu", "ffn1_wd", "w_in", "w_br_ssd", "w_br_att", "w_out", "w_mq", "w_mk", "w_mv", "w_mo",
              "ffn2_wg", "ffn2_wu", "ffn2_wd"]
    shared = {n: np.ascontiguousarray(inp[n][0]) for n in wnames}
    qcols = np.concatenate([np.arange(O_Q + h * 64, O_Q + (h + 1) * 64) for h in HPERM])
    w_in_l = shared["w_in"].copy()
    w_in_l[:, O_Q:O_Q + D] = shared["w_in"][:, qcols]
    shared["w_in"] = w_in_l
    shared["w_br_att"] = np.ascontiguousarray(shared["w_br_att"][qcols - O_Q, :])
    shared.update(pp=pp, rowp=rowp, cst=cst)
    p_ = np.arange(128)
    cst2 = np.zeros((128, C2N), np.float32)
    cst2[:, 0:128] = (p_[:, None] % 4 == p_[None, :] % 4)
    cst2[:, 128:132] = (p_[:, None] % 4 == np.arange(4)[None, :])
    cst2[:, 132:136] = np.where((p_[:, None] // 4 == 0) & (np.arange(4)[None, :] <= p_[:, None] % 4), 0.0, -1e30)
    cst2[:, 136] = p_ % 64
    cst2[0:4, 137:265] = (np.arange(4)[:, None] == p_[None, :] % 4)
    ht = np.arange(32)
    cst2[0:32, 265:269] = (ht[:, None] % 4 == np.arange(4)[None, :])
    cst2[0:8, 269:301] = (np.arange(8)[:, None] == ht[None, :] // 4)
    for tq in range(4):
        cst2[tq, 301 + tq * 128:301 + (tq + 1) * 128] = 1.0
    cst2[0:32, 813:941] = (ht[:, None] == p_[None, :] // 4)
    cst2[:, 941] = (p_ >= 64)
    shared["cst2"] = cst2
    have_pool = "cache_k" in inp
    if have_pool:
        shared["kidxT"] = np.ascontiguousarray(inp["cache_kidx"][0].transpose(0, 2, 1)).reshape(5120 * 64, 128)
        shared["poolk"] = inp["cache_k"][0].reshape(5120 * 128, 256)
        shared["poolv"] = inp["cache_v"][0].reshape(5120 * 128, 256)
    in_maps = []
    NCR = int(os.environ.get("KCORES", 8))
    for c in range(NCR):
        m = dict(shared)
        m["xT"] = np.ascontiguousarray(inp["x_prompt"][c].T)
        m["xsT"] = np.ascontiguousarray(inp["x_sample"][4 * c:4 * c + 4].reshape(16, D).T)
        m["memT"] = np.ascontiguousarray(inp["mem_prompt"][c].T)
        sc_ = inp["state_conv"][0, 4 * c:4 * c + 4]
        m["hists"] = np.ascontiguousarray(sc_.transpose(2, 0, 1).reshape(32, 128, 4, 3).transpose(1, 0, 2, 3)).reshape(128, 384)
        m["ssmsT"] = np.ascontiguousarray(inp["state_ssm"][0, 4 * c:4 * c + 4].reshape(4, 2048, 128).transpose(0, 2, 1))
        m["cmkT"] = np.ascontiguousarray(inp["cache_mem_k"][0, 4 * c:4 * c + 4].reshape(4, 256, D).transpose(0, 2, 1))
        m["cmv"] = np.ascontiguousarray(inp["cache_mem_v"][0, 4 * c:4 * c + 4].reshape(4, 256, D))
        m["ptab"] = np.ascontiguousarray(inp["page_table"][4 * c:4 * c + 4].astype(np.int32))
        in_maps.append(m)
    res = run_bass_kernel_spmd(nc, in_maps, core_ids=list(range(NCR)))
    if os.environ.get("KTIME"):
        print("EXEC_TIME_NS", res.exec_time_ns)
    R = res.results
    y_p = np.stack([R[c]["yT"].T for c in range(NCR)])
    y_s = np.concatenate([R[c]["ysT"].T.reshape(4, 4, D) for c in range(NCR)])
    nk_p = np.stack([R[c]["kT"].T.reshape(SEQ, 4, 64) for c in range(NCR)])[None]
    nv_p = np.stack([R[c]["v_o"].reshape(SEQ, 4, 64) for c in range(NCR)])[None]
    nki_p = np.stack([R[c]["kiT"].T for c in range(NCR)])[None]
    nssm_p = np.stack([R[c]["ssmT"].T.reshape(32, 64, 128) for c in range(NCR)])[None]
    nconv_p = np.stack([R[c]["convT"].T for c in range(NCR)])[None]
    nmk_p = np.stack([R[c]["mkT"].T.reshape(256, 4, 256) for c in range(NCR)])[None]
    nmv_p = np.stack([R[c]["mv_o"].reshape(256, 4, 256) for c in range(NCR)])[None]
    nk_s = np.concatenate([R[c]["ks_o"].reshape(4, 4, 4, 64) for c in range(NCR)])[None]
    nv_s = np.concatenate([R[c]["vs_o"].reshape(4, 4, 4, 64) for c in range(NCR)])[None]
    nki_s = np.concatenate([R[c]["kisT"].T.reshape(4, 4, 64) for c in range(NCR)])[None]
    nssm_s = np.concatenate([R[c]["ssms"].transpose(0, 2, 1).reshape(4, 32, 64, 128) for c in range(NCR)])[None]
    nconv_s = np.concatenate([R[c]["convs"].reshape(128, 32, 4, 3).transpose(2, 3, 1, 0).reshape(4, 3, 4096) for c in range(NCR)])[None]
    return (y_p, y_s, nk_p, nv_p, nki_p, nssm_p, nconv_p, nmk_p, nmv_p, nk_s, nv_s, nki_s, nssm_s, nconv_s)
```

```python
import os
import numpy as np
from contextlib import ExitStack
import concourse.bass as bass
import concourse.mybir as mybir
from concourse.bass_utils import run_bass_kernel_spmd

F32 = mybir.dt.float32
BF16 = mybir.dt.bfloat16
I32 = mybir.dt.int32
U32 = mybir.dt.uint32
AF = mybir.ActivationFunctionType
ALU = mybir.AluOpType
AX = mybir.AxisListType

D = 1024
KC = 8
SEQ = 2048
ST = 512
NST = SEQ // ST
FH = 2816
FHC = 22
IN_DIM = 10344
O_Z, O_X, O_B, O_C, O_DT, O_Q, O_K, O_V, O_QI, O_KI, O_WI, O_GS, O_GA = (
    0, 2048, 4096, 5120, 6144, 6176, 7200, 7456, 7712, 8224, 8288, 8296, 9320)
EPS = 1e-6
NIT = 14
HPERM = []
for _j in range(8):
    HPERM += [(_j // 4) * 8 + _j % 4, (_j // 4) * 8 + _j % 4 + 4]
WBE = 4096

PP_G = {n: i * 8 for i, n in enumerate(
    ["ffn1_pre_g", "ffn1_post_g", "mix_pre_g", "mix_post_g", "mem_pre_g", "mem_kv_g", "mem_post_g", "ffn2_pre_g",
     "ffn2_post_g"])}
PP_NG = 72
PP_CB = 88
PP_CW = 120
PP_DS = 248
PP_N = 264
C2N = 942
NITS = 22
PGP = [0, 1, 2, 3]


class _Rec:
    def __getattr__(self, name):
        def f(*a, **kw):
            self.call = (name, a, kw)
            return self
        return f


def _freeze(fn):
    r = _Rec()
    fn(r)
    name, a, kw = r.call
    return lambda e: getattr(e, name)(*a, **kw)


class Sched:
    ENG = ("pe", "act", "dve", "pool", "sp")

    def __init__(self, nc, es):
        self.nc = nc
        self.es = es
        self.ops = {e: [] for e in self.ENG}
        self.cnt = {e: 0 for e in self.ENG}
        self.sem = {e: es.enter_context(nc.semaphore("s_" + e)) for e in self.ENG}
        self.waited = {e: {} for e in self.ENG}
        self.lastw = {}
        self.readers = {}
        self.chan = {}
        self.nt = 0

    def sb(self, shape, dtype, name=None):
        self.nt += 1
        return self.es.enter_context(self.nc.sbuf_tensor("sb_" + (name or f"t{self.nt}"), list(shape), dtype))

    def ps(self, shape, dtype, name=None):
        self.nt += 1
        return self.es.enter_context(self.nc.psum_tensor("ps_" + (name or f"p{self.nt}"), list(shape), dtype))

    def _key(self, k):
        if isinstance(k, (str, tuple)):
            return k
        return k.name

    def alias(self, newk, oldks):
        newk = self._key(newk)
        lst = self.readers.setdefault(newk, [])
        for o in oldks:
            o = self._key(o)
            lst.extend(self.readers.get(o, []))
            if o in self.lastw:
                lst.append(self.lastw[o])

    def _collect(self, eng, reads, writes):
        deps = []
        for k in reads:
            k = self._key(k)
            w = self.lastw.get(k)
            if w is not None:
                deps.append(("raw", w))
            if isinstance(k, str) and k.startswith("ps_"):
                for r in self.readers.get(k, ()):
                    if r[0] != eng:
                        deps.append(("rar", r))
        for k in writes:
            k = self._key(k)
            w = self.lastw.get(k)
            if w is not None:
                deps.append(("waw", w))
            for r in self.readers.get(k, ()):
                deps.append(("war", r))
        waits = []
        for kind, (skey, val) in deps:
            if skey == eng:
                if eng == "pe" or kind == "war":
                    continue
            if self.waited[eng].get(skey, 0) >= val:
                continue
            self.waited[eng][skey] = val
            waits.append((skey, val))
        return waits

    def _commit(self, tok, reads, writes):
        for k in writes:
            k = self._key(k)
            self.lastw[k] = tok
            self.readers[k] = []
        for k in reads:
            self.readers.setdefault(self._key(k), []).append(tok)

    def op(self, eng, fn, reads=(), writes=()):
        waits = self._collect(eng, reads, writes)
        self.cnt[eng] += 1
        self.ops[eng].append((waits, _freeze(fn), eng))
        self._commit((eng, self.cnt[eng]), reads, writes)

    def dma(self, chan, fn, reads=(), writes=(), queue="sp"):
        waits = self._collect(queue, reads, writes)
        if chan not in self.chan:
            self.chan[chan] = [self.es.enter_context(self.nc.semaphore("c_" + str(len(self.chan)))), 0]
        ch = self.chan[chan]
        ch[1] += 16
        self.ops[queue].append((waits, _freeze(fn), ("ch", chan)))
        self._commit((("ch", chan), ch[1]), reads, writes)

    def _semof(self, skey):
        return self.chan[skey[1]][0] if isinstance(skey, tuple) else self.sem[skey]

    def emit(self):
        fin = [(("ch", c), v) for c, (s, v) in self.chan.items()]
        fin += [(e, self.cnt[e]) for e in self.ENG if e != "sp" and self.cnt[e] > 0]

        def run(engname, e):
            for waits, fn, inc in self.ops[engname]:
                for skey, val in waits:
                    e.wait_ge(self._semof(skey), val)
                inst = fn(e)
                if isinstance(inc, tuple):
                    inst.then_inc(self.chan[inc[1]][0], 16)
                else:
                    inst.then_inc(self.sem[inc], 1)
            if engname == "sp":
                for skey, val in fin:
                    e.wait_ge(self._semof(skey), val)

        with self.nc.Block() as block:
            @block.sync
            def _(e):
                run("sp", e)

            @block.scalar
            def _(e):
                run("act", e)

            @block.vector
            def _(e):
                run("dve", e)

            @block.gpsimd
            def _(e):
                run("pool", e)

            @block.tensor
            def _(e):
                run("pe", e)


class Ctx:
    pass


def build_nc(debug=None):
    nc = bass.Bass("TRN2", target_bir_lowering=False, dynamic_dma_scratch_size=8192)
    dt_in = {}

    def din(name, shape, dt=F32):
        dt_in[name] = nc.dram_tensor(name, list(shape), dt, kind="ExternalInput").ap()
        return dt_in[name]

    def dout(name, shape, dt=F32):
        return nc.dram_tensor(name, list(shape), dt, kind="ExternalOutput").ap()

    xT = din("xT", [D, SEQ])
    xsT = din("xsT", [D, 16])
    memT = din("memT", [D, 256])
    pp_d = din("pp", [128, PP_N])
    rowp_d = din("rowp", [128, 64])
    cst_d = din("cst", [128, 5 * 128])
    W = {}
    for n, shp in [("ffn1_wg", [D, FH]), ("ffn1_wu", [D, FH]), ("ffn1_wd", [FH, D]), ("w_in", [D, IN_DIM]),
                   ("w_br_ssd", [2048, D]), ("w_br_att", [D, D]), ("w_out", [D, D]), ("w_mq", [D, D]),
                   ("w_mk", [D, D]), ("w_mv", [D, D]), ("w_mo", [D, D]),
                   ("ffn2_wg", [D, FH]), ("ffn2_wu", [D, FH]), ("ffn2_wd", [FH, D])]:
        W[n] = din(n, shp)

    yT_o = dout("yT", [D, SEQ])
    ysT_o = dout("ysT", [D, 16])
    kT_o = dout("kT", [256, SEQ])
    v_o = dout("v_o", [SEQ, 256])
    kiT_o = dout("kiT", [64, SEQ])
    ssmT_o = dout("ssmT", [128, 2048])
    convT_o = dout("convT", [4096, 3])
    mkT_o = dout("mkT", [D, 256])
    mv_o = dout("mv_o", [256, D])
    hists_d = din("hists", [128, 384])
    ssmsT_d = din("ssmsT", [4, 128, 2048])
    cmkT_d = din("cmkT", [4, D, 256])
    cmv_d = din("cmv", [4, 256, D])
    ptab_d = din("ptab", [4, 128], I32)
    cst2_d = din("cst2", [128, C2N])
    if "sample" not in os.environ.get("KSKIP", "").split(","):
        kidxT_d = din("kidxT", [5120 * 64, 128])
        poolk_d = din("poolk", [5120 * 128, 256])
        poolv_d = din("poolv", [5120 * 128, 256])
    ks_o = dout("ks_o", [16, 256])
    vs_o = dout("vs_o", [16, 256])
    kisT_o = dout("kisT", [64, 16])
    ssms_o = dout("ssms", [4, 128, 2048])
    convs_o = dout("convs", [128, 384])
    dbg_o = None

    with ExitStack() as es:
        S = Sched(nc, es)
        c = Ctx()
        pp = S.sb([128, PP_N], F32, "pp")
        rowp = S.sb([128, 64], F32, "rowp")
        cst = S.sb([128, 640], F32, "cst")
        cstb = S.sb([128, 640], BF16, "cstb")
        S.dma("pp", lambda e: e.dma_start(out=pp[:], in_=pp_d), writes=[pp])
        S.dma("rowp", lambda e: e.dma_start(out=rowp[:], in_=rowp_d), writes=[rowp])
        S.dma("cst", lambda e: e.dma_start(out=cst[:], in_=cst_d), writes=[cst])
        S.dma("cstb", lambda e: e.dma_start(out=cstb[:], in_=cst_d), writes=[cstb], queue="pool")
        ident = cst[:, 0:128]
        Uf = cst[:, 128:256]
        negm = cst[:, 256:384]
        identb = cstb[:, 0:128]
        onesb = cstb[:, 384:512]
        trib = cstb[:, 512:640]
        epsT = S.sb([128, 1], F32, "epsT")
        S.op("dve", lambda e: e.memset(epsT[:], EPS), writes=[epsT])
        arow = S.sb([128, 32], F32, "arow")
        S.op("act", lambda e: e.activation(out=arow[:], in_=rowp[:, 32:64], func=AF.Exp), reads=[rowp], writes=[arow])
        S.op("dve", lambda e: e.tensor_scalar(out=arow[:], in0=arow[:], scalar1=-1.0, scalar2=None, op0=ALU.mult),
             reads=[arow], writes=[arow])

        gen = [S.ps([128, 512], F32, f"pg{i}") for i in range(5)]
        accA = S.ps([128, 512], F32, "accA")
        accB = S.ps([128, 512], F32, "accB")
        ptb = S.ps([128, 1024], BF16, "ptb")
        c.gi = 0
        c.ti = 0

        def P():
            c.gi = (c.gi + 1) % len(gen)
            return gen[c.gi]

        def PTS(i):
            return ptb[:, i * 128:(i + 1) * 128]

        NWB = 3
        wbufs = [S.sb([128, WBE], BF16, f"wb{i}") for i in range(NWB)]
        c.wi = 0

        def wload(wd, r0, nk, c0, ncols):
            assert nk * ncols <= WBE
            c.wi = (c.wi + 1) % NWB
            buf = wbufs[c.wi]
            view = buf[:, 0:nk * ncols].rearrange("p (k c) -> p k c", k=nk)
            src = wd[r0:r0 + nk * 128, c0:c0 + ncols].rearrange("(k p) c -> p k c", p=128)
            S.dma(("w", c.wi), lambda e: e.dma_start(out=view, in_=src), writes=[buf], queue="pool")
            return buf, view

        hT = S.sb([128, KC, ST], F32, "hT")
        uT = S.sb([128, KC, ST], BF16, "uT")
        scrM = S.sb([128, 4096], F32, "scrM")
        ybuf = scrM[:].rearrange("p (k t) -> p k t", k=KC)
        SCRK = ["ssdtmp", "maskT", "zs", ("cv", 0), ("cv", 1), ("cv", 2), ("cv", 3)]
        sqb = S.sb([128, ST], BF16, "sqb")
        sqb2 = S.sb([128, ST], BF16, "sqb2")
        rstd = S.sb([128, ST], F32, "rstd")
        tmpf = S.sb([128, ST], F32, "tmpf")
        big = S.sb([128, 22528], BF16, "big")
        hid = big[:, 0:FHC * ST].rearrange("p (k t) -> p k t", k=FHC)
        sgs = [S.sb([128, ST], BF16, f"sg{i}") for i in range(2)]

        def gcol(name, k):
            return pp[:, PP_G[name] + k:PP_G[name] + k + 1]

        def norm_stats(src, srck, T, scale_div):
            ps = P()
            n = len(src)
            for k in range(n):
                sq = sqb if k % 2 == 0 else sqb2
                S.op("act", lambda e, k=k, sq=sq: e.activation(out=sq[:, :T], in_=src[k], func=AF.Square),
                     reads=[srck], writes=[sq])
                S.op("pe", lambda e, k=k, sq=sq: e.matmul(ps[:, :T], onesb, sq[:, :T], start=(k == 0), stop=(k == n - 1)),
                     reads=[sq, cstb], writes=[ps])
            S.op("act", lambda e: e.activation(out=rstd[:, :T], in_=ps[:, :T], func=AF.Sqrt, bias=epsT[:, 0:1],
                                               scale=1.0 / scale_div), reads=[ps, epsT], writes=[rstd])
            S.op("dve", lambda e: e.reciprocal(out=rstd[:, :T], in_=rstd[:, :T]), reads=[rstd], writes=[rstd])

        def prenorm(gname, T, src=None, srck=None, dst=None):
            src = src if src is not None else [hT[:, k, :T] for k in range(KC)]
            srck = srck if srck is not None else hT
            dst = dst if dst is not None else uT
            norm_stats(src, srck, T, float(D))
            for k in range(KC):
                S.op("dve", lambda e, k=k: e.scalar_tensor_tensor(out=dst[:, k, :T], in0=src[k], scalar=gcol(gname, k),
                                                                   in1=rstd[:, :T], op0=ALU.mult, op1=ALU.mult),
                     reads=[srck, rstd, pp], writes=[dst])

        def postnorm_add(gname, T, coef):
            norm_stats([ybuf[:, k, :T] for k in range(KC)], "ybuf", T, float(D))
            for k in range(KC):
                S.op("dve", lambda e, k=k: e.scalar_tensor_tensor(out=tmpf[:, :T], in0=ybuf[:, k, :T], scalar=gcol(gname, k),
                                                                   in1=rstd[:, :T], op0=ALU.mult, op1=ALU.mult),
                     reads=["ybuf", rstd, pp], writes=[tmpf])
                S.op("dve", lambda e, k=k: e.scalar_tensor_tensor(out=hT[:, k, :T], in0=tmpf[:, :T], scalar=coef,
                                                                   in1=hT[:, k, :T], op0=ALU.mult, op1=ALU.add),
                     reads=[tmpf, hT], writes=[hT])

        def proj_fm(wd, r0, nk, c0, ncols, rhs_fn, rhs_keys, T, consumer, blk=512, msz=128):
            blk = min(blk, (WBE // nk) // msz * msz)
            idx = 0
            for b0 in range(0, ncols, blk):
                bc = min(blk, ncols - b0)
                buf, view = wload(wd, r0, nk, c0 + b0, bc)
                for m0 in range(0, bc, msz):
                    ms = min(msz, bc - m0)
                    ps = P()
                    for k in range(nk):
                        S.op("pe", lambda e, k=k, m0=m0, ms=ms, ps=ps, view=view: e.matmul(
                            ps[0:ms, :T], view[:, k, m0:m0 + ms], rhs_fn(k), start=(k == 0), stop=(k == nk - 1)),
                            reads=[buf] + rhs_keys, writes=[ps])
                    consumer(idx, ps, ms)
                    idx += 1

        def ffn(pref, T):
            prenorm(pref + "_pre_g", T)
            S.alias("hid", MIXK + ["acc"])
            S.alias("ybuf", SCRK)
            for b0 in range(0, FH, 512):
                bc_ = min(512, FH - b0)
                bufg, vg = wload(W[pref + "_wg"], 0, KC, b0, bc_)
                bufu, vu = wload(W[pref + "_wu"], 0, KC, b0, bc_)
                for m in range(bc_ // 128):
                    hc = b0 // 128 + m
                    pg, pu = P(), P()
                    for k in range(KC):
                        S.op("pe", lambda e, k=k, m=m, pg=pg, vg=vg: e.matmul(pg[:, :T], vg[:, k, m * 128:(m + 1) * 128],
                                                                             uT[:, k, :T], start=(k == 0), stop=(k == KC - 1)),
                             reads=[bufg, uT], writes=[pg])
                    for k in range(KC):
                        S.op("pe", lambda e, k=k, m=m, pu=pu, vu=vu: e.matmul(pu[:, :T], vu[:, k, m * 128:(m + 1) * 128],
                                                                             uT[:, k, :T], start=(k == 0), stop=(k == KC - 1)),
                             reads=[bufu, uT], writes=[pu])
                    sg = sgs[hc % 2]
                    S.op("act", lambda e, pg=pg, sg=sg: e.activation(out=sg[:, :T], in_=pg[:, :T], func=AF.Silu),
                         reads=[pg], writes=[sg])
                    S.op("dve", lambda e, pu=pu, sg=sg, hc=hc: e.tensor_tensor(out=hid[:, hc, :T], in0=sg[:, :T], in1=pu[:, :T],
                                                                              op=ALU.mult), reads=[pu, sg], writes=["hid"])

            def cons(idx, ps, ms):
                S.op("act", lambda e: e.activation(out=ybuf[:, idx, :T], in_=ps[:, :T], func=AF.Copy), reads=[ps],
                     writes=["ybuf"])
            proj_fm(W[pref + "_wd"], 0, FHC, 0, D, lambda k: hid[:, k, :T], ["hid"], T, cons, blk=128)
            postnorm_add(pref + "_post_g", T, 0.5)


        MIXK = ["qT", "qiT", "yssdT", "yattT", "mergedT"]
        qT = big[:, 0:4096].rearrange("p (k t) -> p k t", k=8)
        qiT = big[:, 4096:6144].rearrange("p (k t) -> p k t", k=4)
        yssdT = big[:, 6144:14336].rearrange("p (k t) -> p k t", k=16)
        yattT = big[:, 14336:18432].rearrange("p (k t) -> p k t", k=8)
        mergedT = big[:, 18432:22528].rearrange("p (k t) -> p k t", k=8)
        kT2 = S.sb([128, 2, SEQ], BF16, "kT2")
        kiT2 = S.sb([128, SEQ], BF16, "kiT2")
        vtok = S.sb([128, 16, 256], BF16, "vtok")
        stT = S.sb([128, 8, 256], F32, "stT")
        stTb = S.sb([128, 8, 256], BF16, "stTb")
        hist = S.sb([128, 32, 3], F32, "hist")
        mkTb = S.sb([128, 8, 256], BF16, "mkTb")
        mvb = S.sb([128, 2, D], BF16, "mvb")
        for t_ in (stT, stTb, hist):
            S.op("dve", lambda e, t_=t_: e.memset(t_[:], 0.0), writes=[t_])
        maskT = scrM[:].bitcast(BF16).rearrange("p (k t) -> p k t", k=16)
        pre = scrM[:, 0:2060].rearrange("p (j t) -> p j t", j=4)
        cv = scrM[:, 2064:3088].bitcast(BF16).rearrange("p (j t) -> p j t", j=4)
        zs = scrM[:, 3088:3600].bitcast(BF16).rearrange("p (j t) -> p j t", j=2)
        Yg = S.sb([128, 2, ST], F32, "Yg")
        acc = big[:, 18432:22528].bitcast(F32)
        mask01t = S.sb([128, 2048], BF16, "mask01t")
        mask01 = mask01t[:]
        junk = mask01t[:]
        stg = [S.sb([128, 512], F32, f"stg{i}") for i in range(1)]
        c.si = 0
        Es = [S.sb([128, ST], BF16, f"E{i}") for i in range(2)]
        c.ei = 0
        rrs = [S.sb([128, 512], F32, f"rr{i}") for i in range(2)]
        rden = S.sb([128, ST], F32, "rden")
        witok = S.sb([128, 4, 8], F32, "witok")
        dtt = S.sb([128, 4, 32], F32, "dtt")
        dta = S.sb([128, 4, 32], F32, "dta")
        acsc = S.sb([128, 4, 32], F32, "acsc")
        arw = S.sb([128, 512], F32, "arw")
        erow = S.sb([128, 512], F32, "erow")
        Cdec = S.sb([128, 512], BF16, "Cdec")
        xs_tok = S.sb([128, 256], BF16, "xs_tok")
        B_tok = S.sb([128, 128], BF16, "B_tok")
        cbm = S.sb([128, 128], BF16, "cbm")
        argt = [S.sb([128, 128], F32, f"arg{i}") for i in range(2)]
        Ldt = [S.sb([128, 128], BF16, f"Ld{i}") for i in range(2)]
        MTt = [S.sb([128, 128], BF16, f"MT{i}") for i in range(2)]
        xdt = S.sb([128, 256], BF16, "xdt")
        xdtw = S.sb([128, 256], BF16, "xdtw")
        sm4 = S.sb([128, 16], F32, "sm4")
        bis = S.sb([128, 8], F32, "bis")

        def stage_out(src_ps, rows, cols, dst_ap, dkey):
            c.si = 0
            sg_ = stg[c.si]
            S.op("act", lambda e: e.activation(out=sg_[0:rows, 0:cols], in_=src_ps, func=AF.Copy), reads=[dkey[0]], writes=[sg_])
            S.dma(("stg", c.si), lambda e: e.dma_start(out=dst_ap, in_=sg_[0:rows, 0:cols]), reads=[sg_], writes=[dkey[1]])

        def copy_alt(i, out, in_, reads, writes):
            if i % 2 == 0:
                S.op("act", lambda e: e.activation(out=out, in_=in_, func=AF.Copy), reads=reads, writes=writes)
            else:
                S.op("dve", lambda e: e.tensor_copy(out=out, in_=in_), reads=reads, writes=writes)

        def tm_proj(view, buf, c0, n, tt, ps):
            for k in range(KC):
                S.op("pe", lambda e, k=k: e.matmul(ps[:, 0:n], uT[:, k, tt * 128:(tt + 1) * 128], view[:, k, c0:c0 + n],
                                                   start=(k == 0), stop=(k == KC - 1)), reads=[buf, uT], writes=[ps])

        def fm64(view, buf, c0, ps, T):
            for half in range(2):
                for k in range(KC):
                    S.op("pe", lambda e, k=k, half=half: e.matmul(ps[half * 64:(half + 1) * 64, :T], view[:, k, c0:c0 + 64],
                                                                  uT[:, k, :T], start=(k == 0), stop=(k == KC - 1)),
                         reads=[buf, uT], writes=[ps])

        def mix_prompt(st):
            t0 = st * ST
            for k_ in MIXK:
                S.alias(k_, ["hid"])
            for k_ in SCRK:
                S.alias(k_, ["ybuf"])
            S.alias("ssdtmp", ["maskT"])
            prenorm("mix_pre_g", ST)
            win = W["w_in"]
            KM = os.environ.get("KMIX", "k,v,ki,wi,dt,q").split(",")
            buf, view = wload(win, 0, KC, O_K, 512)
            for kc in range(2 if "k" in KM else 0):
                ps = P()
                for k in range(KC):
                    S.op("pe", lambda e, k=k, kc=kc, ps=ps: e.matmul(ps[:, :ST], view[:, k, kc * 128:(kc + 1) * 128], uT[:, k, :ST],
                                                                     start=(k == 0), stop=(k == KC - 1)), reads=[buf, uT], writes=[ps])
                KK = os.environ.get("KK", "copy,stage").split(",")
                if "copy" in KK:
                    S.op("dve", lambda e, kc=kc, ps=ps: e.tensor_copy(out=kT2[:, kc, t0:t0 + ST], in_=ps[:, :ST]), reads=[ps], writes=[kT2])
                if "stage" in KK:
                    stage_out(ps[:, :ST], 128, ST, kT_o[kc * 128:(kc + 1) * 128, t0:t0 + ST], (ps, "kT_o"))
            for tt in range(4 if "v" in KM else 0):
                ps = P()
                tm_proj(view, buf, 256, 256, tt, ps)
                S.op("dve", lambda e, tt=tt, ps=ps: e.tensor_copy(out=vtok[:, st * 4 + tt, :], in_=ps[:, 0:256]), reads=[ps], writes=[vtok])
                stage_out(ps[:, 0:256], 128, 256, v_o[t0 + tt * 128:t0 + (tt + 1) * 128, :], (ps, "v_o"))
            buf, view = wload(win, 0, KC, O_KI - 32, 128)
            ps = P()
            if "ki" in KM:
                fm64(view, buf, 32, ps, ST)
                S.op("dve", lambda e, ps=ps: e.tensor_copy(out=kiT2[:, t0:t0 + ST], in_=ps[:, :ST]), reads=[ps], writes=[kiT2])
                stage_out(ps[0:64, :ST], 64, ST, kiT_o[:, t0:t0 + ST], (ps, "kiT_o"))
            for tt in range(4 if "wi" in KM else 0):
                ps = P()
                tm_proj(view, buf, 96, 8, tt, ps)
                S.op("dve", lambda e, tt=tt, ps=ps: e.tensor_copy(out=witok[:, tt, :], in_=ps[:, 0:8]), reads=[ps], writes=[witok])
            buf, view = wload(win, 0, KC, O_DT, 128)
            if "dt" not in KM:
                return
            for tt in range(4):
                ps = P()
                tm_proj(view, buf, 0, 32, tt, ps)
                S.op("dve", lambda e, tt=tt, ps=ps: e.tensor_tensor(out=dtt[:, tt, :], in0=ps[:, 0:32], in1=rowp[:, 0:32], op=ALU.add),
                     reads=[ps, rowp], writes=[dtt])
            S.op("act", lambda e: e.activation(out=dtt[:], in_=dtt[:], func=AF.Exp), reads=[dtt], writes=[dtt])
            S.op("act", lambda e: e.activation(out=dtt[:], in_=dtt[:], func=AF.Ln, bias=1.0), reads=[dtt], writes=[dtt])
            for tt in range(4):
                S.op("dve", lambda e, tt=tt: e.tensor_tensor(out=dta[:, tt, :], in0=dtt[:, tt, :], in1=arow[:], op=ALU.mult),
                     reads=[dtt, arow], writes=[dta])
                ps = P()
                S.op("pe", lambda e, tt=tt, ps=ps: e.matmul(ps[:, 0:32], Uf, dta[:, tt, :], start=True, stop=True), reads=[cst, dta], writes=[ps])
                S.op("act", lambda e, tt=tt, ps=ps: e.activation(out=acsc[:, tt, :], in_=ps[:, 0:32], func=AF.Copy), reads=[ps], writes=[acsc])
            proj_fm(win, 0, KC, O_Q, D, lambda k: uT[:, k, :ST], [uT], ST,
                    lambda idx, ps, ms: copy_alt(idx, qT[:, idx, :], ps[:, :ST], [ps], ["qT"]))
            proj_fm(win, 0, KC, O_QI, 512, lambda k: uT[:, k, :ST], [uT], ST,
                    lambda idx, ps, ms: copy_alt(idx, qiT[:, idx, :], ps[:, :ST], [ps], ["qiT"]))
            STG = os.environ.get("KSTAGE", "all")
            if STG == "mixA":
                return
            for g in range(8):
                ssd_group(st, g)
            if STG == "ssd":
                return
            S.alias("maskT", ["ssdtmp"])
            dsa_prompt(st)
            if STG == "dsa":
                return
            merge(ST)

        def conv_chunk(j, ch, T):
            cw = lambda tap: pp[:, PP_CW + ch * 4 + tap:PP_CW + ch * 4 + tap + 1]
            S.op("dve", lambda e: e.tensor_scalar(out=tmpf[:, :T], in0=pre[:, j, 0:T], scalar1=cw(0), scalar2=pp[:, PP_CB + ch:PP_CB + ch + 1],
                                                  op0=ALU.mult, op1=ALU.add), reads=["ssdtmp", pp], writes=[tmpf])
            for tap in range(1, 4):
                S.op("dve", lambda e, tap=tap: e.scalar_tensor_tensor(out=tmpf[:, :T], in0=pre[:, j, tap:tap + T], scalar=cw(tap),
                                                                      in1=tmpf[:, :T], op0=ALU.mult, op1=ALU.add),
                     reads=["ssdtmp", pp, tmpf], writes=[tmpf])
            S.op("act", lambda e: e.activation(out=cv[:, j, :T], in_=tmpf[:, :T], func=AF.Silu), reads=[tmpf], writes=[("cv", j)])

        def ssd_group(st, g):
            win = W["w_in"]
            T = ST
            buf, view = wload(win, 0, KC, O_Z + g * 256, 256)
            for m in range(2):
                ps = P()
                for k in range(KC):
                    S.op("pe", lambda e, k=k, m=m, ps=ps: e.matmul(ps[:, :T], view[:, k, m * 128:(m + 1) * 128], uT[:, k, :T],
                                                                   start=(k == 0), stop=(k == KC - 1)), reads=[buf, uT], writes=[ps])
                S.op("act", lambda e, m=m, ps=ps: e.activation(out=zs[:, m, :T], in_=ps[:, :T], func=AF.Silu), reads=[ps], writes=["zs"])
            chs = [g * 2, g * 2 + 1, 16 + g, 24 + g]
            srcs = [(O_X + g * 256, 0), (O_X + g * 256, 128), (O_B + g * 128, 0), (O_C + g * 128, 0)]
            bufx, viewx = wload(win, 0, KC, O_X + g * 256, 256)
            bufb, viewb = wload(win, 0, KC, O_B + g * 128, 128)
            views = [(bufx, viewx, 0), (bufx, viewx, 128), (bufb, viewb, 0), None]
            for j in range(4):
                if j == 3:
                    bufc, viewc = wload(win, 0, KC, O_C + g * 128, 128)
                    views[3] = (bufc, viewc, 0)
                bf_, vw_, c0 = views[j]
                ps = P()
                for k in range(KC):
                    S.op("pe", lambda e, k=k, ps=ps, vw_=vw_, c0=c0: e.matmul(ps[:, :T], vw_[:, k, c0:c0 + 128], uT[:, k, :T],
                                                                              start=(k == 0), stop=(k == KC - 1)), reads=[bf_, uT], writes=[ps])
                ch = chs[j]
                S.op("dve", lambda e, j=j, ch=ch: e.tensor_copy(out=pre[:, j, 0:3], in_=hist[:, ch, :]), reads=[hist], writes=["ssdtmp"])
                S.op("act", lambda e, j=j, ps=ps: e.activation(out=pre[:, j, 3:3 + T], in_=ps[:, :T], func=AF.Copy), reads=[ps], writes=["ssdtmp"])
                S.op("dve", lambda e, j=j, ch=ch: e.tensor_copy(out=hist[:, ch, :], in_=pre[:, j, T:T + 3]), reads=["ssdtmp"], writes=[hist])
                conv_chunk(j, ch, T)
            for cc in range(4):
                ssd_chunk(g, cc, 128, cc * 128)
            for m in range(2):
                S.op("dve", lambda e, m=m: e.tensor_tensor(out=Yg[:, m, :], in0=Yg[:, m, :], in1=zs[:, m, :], op=ALU.mult),
                     reads=[Yg, "zs"], writes=[Yg])
            norm_stats([Yg[:, 0, :], Yg[:, 1, :]], Yg, T, 256.0)
            for m in range(2):
                S.op("dve", lambda e, m=m: e.scalar_tensor_tensor(out=yssdT[:, g * 2 + m, :], in0=Yg[:, m, :],
                                                                   scalar=pp[:, PP_NG + g * 2 + m:PP_NG + g * 2 + m + 1], in1=rstd[:, :T],
                                                                   op0=ALU.mult, op1=ALU.mult), reads=[Yg, rstd, pp], writes=["yssdT"])

        def ssd_chunk(g, cc, L, col0, sf=None, sbf=None, skeys=None, sfa=None, sba=None):
            cs = slice(col0, col0 + L)
            if sf is None:
                sf = lambda hh: stT[:, g, hh * 64:(hh + 1) * 64]
                sbf = lambda hh: stTb[:, g, hh * 64:(hh + 1) * 64]
                skeys = (stT, stTb)
                sfa, sba = stT[:, g, :], stTb[:, g, :]
            for m in range(3):
                S.op("pe", lambda e, m=m: e.transpose(PTS(m)[0:L, :], cv[:, m, cs], identb), reads=[("cv", m), cstb], writes=[ptb])
            S.op("act", lambda e: e.activation(out=xs_tok[0:L, :], in_=ptb[0:L, 0:256], func=AF.Copy), reads=[ptb], writes=[xs_tok])
            S.op("act", lambda e: e.activation(out=B_tok[0:L, :], in_=ptb[0:L, 256:384], func=AF.Copy), reads=[ptb], writes=[B_tok])
            ps = P()
            S.op("pe", lambda e, ps=ps: e.matmul(ps[0:L, 0:L], cv[:, 2, cs], cv[:, 3, cs], start=True, stop=True),
                 reads=[("cv", 2), ("cv", 3)], writes=[ps])
            S.op("dve", lambda e, ps=ps: e.tensor_tensor(out=cbm[0:L, 0:L], in0=ps[0:L, 0:L], in1=trib[0:L, 0:L], op=ALU.mult),
                 reads=[ps, cstb], writes=[cbm])
            psr = P()
            for hh in range(4):
                h = g * 4 + hh
                S.op("pe", lambda e, hh=hh, h=h: e.matmul(psr[:, hh * 128:hh * 128 + L], dta[0:L, cc, h:h + 1].to_broadcast([L, 128]),
                                                          Uf[0:L, 0:L], start=True, stop=True), reads=[dta, cst], writes=[psr])
            arw3 = arw[:].rearrange("p (a b) -> p a b", a=4)
            erow3 = erow[:].rearrange("p (a b) -> p a b", a=4)
            psr3 = psr[:].rearrange("p (a b) -> p a b", a=4)
            S.op("act", lambda e: e.activation(out=arw3[:, :, 0:L], in_=psr3[:, :, 0:L], func=AF.Copy), reads=[psr], writes=[arw])
            S.op("act", lambda e: e.activation(out=erow3[:, :, 0:L], in_=arw3[:, :, 0:L], func=AF.Exp), reads=[arw], writes=[erow])
            S.op("dve", lambda e: e.tensor_tensor(out=sm4[0:L, 0:4], in0=arw3[0:L, :, L - 1], in1=acsc[0:L, cc, g * 4:g * 4 + 4], op=ALU.subtract),
                 reads=[arw, acsc], writes=[sm4])
            S.op("act", lambda e: e.activation(out=sm4[0:L, 4:8], in_=sm4[0:L, 0:4], func=AF.Exp), reads=[sm4], writes=[sm4])
            S.op("dve", lambda e: e.tensor_tensor(out=sm4[0:L, 8:12], in0=sm4[0:L, 4:8], in1=dtt[0:L, cc, g * 4:g * 4 + 4], op=ALU.mult),
                 reads=[sm4, dtt], writes=[sm4])
            Cd3 = Cdec[:].rearrange("p (a b) -> p a b", a=4)
            S.op("dve", lambda e: e.tensor_tensor(out=Cd3[:, :, 0:L], in0=erow3[:, :, 0:L],
                                                  in1=cv[:, 3, cs].unsqueeze(1).to_broadcast([128, 4, L]), op=ALU.mult),
                 reads=[erow, ("cv", 3)], writes=[Cdec])
            psy = [P(), P()]
            tf3 = tmpf[:].rearrange("p (a b) -> p a b", a=4)
            Ld3 = Es[0][:].rearrange("p (a b) -> p a b", a=4)
            MT3 = Es[1][:].rearrange("p (a b) -> p a b", a=4)
            S.op("dve", lambda e: e.tensor_tensor(out=tf3[0:L, :, 0:L], in0=arw3[0:L, :, 0:L],
                                                  in1=acsc[0:L, cc, g * 4:g * 4 + 4].unsqueeze(2).to_broadcast([L, 4, L]), op=ALU.subtract),
                 reads=[arw, acsc], writes=[tmpf])
            S.op("dve", lambda e: e.tensor_scalar(out=tf3[0:L, :, 0:L], in0=tf3[0:L, :, 0:L], scalar1=0.0, scalar2=None, op0=ALU.min),
                 reads=[tmpf], writes=[tmpf])
            S.op("act", lambda e: e.activation(out=Ld3[0:L, :, 0:L], in_=tf3[0:L, :, 0:L], func=AF.Exp), reads=[tmpf], writes=[Es[0]])
            S.op("dve", lambda e: e.tensor_tensor(out=MT3[0:L, :, 0:L], in0=Ld3[0:L, :, 0:L],
                                                  in1=cbm[0:L, 0:L].unsqueeze(1).to_broadcast([L, 4, L]), op=ALU.mult),
                 reads=[Es[0], cbm], writes=[Es[1]])
            xs3 = xs_tok[:].rearrange("p (a b) -> p a b", a=4)
            S.op("dve", lambda e: e.tensor_tensor(out=xdt[:].rearrange("p (a b) -> p a b", a=4)[0:L], in0=xs3[0:L],
                                                  in1=dtt[0:L, cc, g * 4:g * 4 + 4].unsqueeze(2).to_broadcast([L, 4, 64]), op=ALU.mult),
                 reads=[xs_tok, dtt], writes=[xdt])
            S.op("dve", lambda e: e.tensor_tensor(out=xdtw[:].rearrange("p (a b) -> p a b", a=4)[0:L], in0=xs3[0:L],
                                                  in1=sm4[0:L, 8:12].unsqueeze(2).to_broadcast([L, 4, 64]), op=ALU.mult),
                 reads=[xs_tok, sm4], writes=[xdtw])
            for hh in range(4):
                m, half = hh // 2, hh % 2
                py = psy[m]
                S.op("pe", lambda e, hh=hh, half=half, py=py: e.matmul(py[half * 64:(half + 1) * 64, 0:L], xdt[0:L, hh * 64:(hh + 1) * 64],
                                                                       Es[1][0:L, hh * 128:hh * 128 + L], start=True, stop=False), reads=[xdt, Es[1]], writes=[py])
                S.op("pe", lambda e, hh=hh, half=half, py=py: e.matmul(py[half * 64:(half + 1) * 64, 0:L], sbf(hh),
                                                                       Cdec[:, hh * 128:hh * 128 + L], start=False, stop=True),
                     reads=[skeys[1], Cdec], writes=[py])
            for m in range(2):
                S.op("dve", lambda e, m=m: e.scalar_tensor_tensor(out=Yg[:, m, cs], in0=cv[:, m, cs],
                                                                   scalar=pp[:, PP_DS + g * 2 + m:PP_DS + g * 2 + m + 1], in1=psy[m][:, 0:L],
                                                                   op0=ALU.mult, op1=ALU.add), reads=[("cv", m), pp, psy[m]], writes=[Yg])
            psc = P()
            S.op("pe", lambda e: e.matmul(psc[:, 0:256], B_tok[0:L, :], xdtw[0:L, :], start=True, stop=True), reads=[B_tok, xdtw], writes=[psc])
            sfa3 = sfa.rearrange("p (a b) -> p a b", a=4)
            S.op("dve", lambda e: e.tensor_tensor(out=sfa3, in0=sfa3, in1=erow3[:, :, L - 1].unsqueeze(2).to_broadcast([128, 4, 64]), op=ALU.mult),
                 reads=[skeys[0], erow], writes=[skeys[0]])
            S.op("dve", lambda e: e.tensor_tensor(out=sfa3, in0=sfa3, in1=psc[:, 0:256].rearrange("p (a b) -> p a b", a=4), op=ALU.add),
                 reads=[skeys[0], psc], writes=[skeys[0]])
            S.op("act", lambda e: e.activation(out=sba, in_=sfa, func=AF.Copy), reads=[skeys[0]], writes=[skeys[1]])

        def dsa_prompt(st):
            S.alias("acc", ["mergedT"])
            S.op("dve", lambda e: e.memset(maskT[:, 0:4 * st + 4, :], 0.0), writes=["maskT"])
            R, lo, stp, cand, cnt, gg = [bis[:, i:i + 1] for i in range(6)]
            for qb in range(4):
                i = st * 4 + qb
                Nk = (i + 1) * 128
                qs = slice(qb * 128, (qb + 1) * 128)
                for h in range(8):
                    pair, half = h // 2, h % 2
                    hs = slice(half * 64, (half + 1) * 64)
                    for kt in range((Nk + 511) // 512):
                        n = min(512, Nk - kt * 512)
                        ps = P()
                        S.op("pe", lambda e, ps=ps, pair=pair, hs=hs, kt=kt, n=n: e.matmul(
                            ps[:, 0:n], qiT[hs, pair, qs], kiT2[hs, kt * 512:kt * 512 + n], start=True, stop=True),
                            reads=["qiT", kiT2], writes=[ps])
                        rr = rrs[(h + kt) % 2]
                        S.op("act", lambda e, ps=ps, rr=rr, n=n: e.activation(out=rr[:, 0:n], in_=ps[:, 0:n], func=AF.Relu), reads=[ps], writes=[rr])
                        if h == 0:
                            S.op("dve", lambda e, rr=rr, kt=kt, n=n: e.tensor_scalar(out=acc[:, kt * 512:kt * 512 + n], in0=rr[:, 0:n],
                                                                                     scalar1=witok[:, qb, 0:1], scalar2=None, op0=ALU.mult),
                                 reads=[rr, witok], writes=["acc"])
                        else:
                            S.op("dve", lambda e, rr=rr, kt=kt, n=n, h=h: e.scalar_tensor_tensor(
                                out=acc[:, kt * 512:kt * 512 + n], in0=rr[:, 0:n], scalar=witok[:, qb, h:h + 1],
                                in1=acc[:, kt * 512:kt * 512 + n], op0=ALU.mult, op1=ALU.add), reads=[rr, witok, "acc"], writes=["acc"])
                S.op("dve", lambda e: e.tensor_reduce(out=R, in_=acc[:, 0:Nk], axis=AX.X, op=ALU.max, apply_absolute_value=True),
                     reads=["acc"], writes=[bis])
                S.op("dve", lambda e: e.tensor_tensor(out=acc[:, i * 128:(i + 1) * 128], in0=acc[:, i * 128:(i + 1) * 128], in1=negm, op=ALU.add),
                     reads=["acc", cst], writes=["acc"])
                S.op("dve", lambda e: e.tensor_scalar(out=lo, in0=R, scalar1=-1.0, scalar2=None, op0=ALU.mult), reads=[bis], writes=[bis])
                if i >= 2:
                    for it in range(1, NIT + 1):
                        S.op("dve", lambda e, it=it: e.tensor_scalar(out=stp, in0=R, scalar1=2.0 ** (1 - it), scalar2=None, op0=ALU.mult),
                             reads=[bis], writes=[bis])
                        S.op("dve", lambda e: e.tensor_tensor(out=cand, in0=lo, in1=stp, op=ALU.add), reads=[bis], writes=[bis])
                        S.op("dve", lambda e: e.tensor_scalar(out=junk[:, 0:Nk], in0=acc[:, 0:Nk], scalar1=cand, scalar2=None,
                                                              op0=ALU.is_ge, op1=ALU.add, accum_out=cnt), reads=["acc", bis], writes=["mask01", bis])
                        S.op("dve", lambda e: e.tensor_scalar(out=gg, in0=cnt, scalar1=255.5, scalar2=None, op0=ALU.is_ge),
                             reads=[bis], writes=[bis])
                        S.op("dve", lambda e: e.scalar_tensor_tensor(out=lo, in0=gg, scalar=stp, in1=lo, op0=ALU.mult, op1=ALU.add),
                             reads=[bis], writes=[bis])
                S.op("dve", lambda e: e.tensor_scalar(out=mask01[:, 0:Nk], in0=acc[:, 0:Nk], scalar1=lo, scalar2=None, op0=ALU.is_ge),
                     reads=["acc", bis], writes=["mask01"])
                for sc0 in range(0, i + 1, 8):
                    n8 = min(8, i + 1 - sc0)
                    for j8 in range(n8):
                        S.op("pe", lambda e, j8=j8: e.transpose(PTS(j8), mask01[:, (sc0 + j8) * 128:(sc0 + j8 + 1) * 128], identb),
                             reads=["mask01", cstb], writes=[ptb])
                    S.op("act", lambda e: e.activation(out=maskT[:, sc0:sc0 + n8, qs], in_=ptb[:, 0:n8 * 128].rearrange("p (a b) -> p a b", a=n8),
                                                       func=AF.Copy), reads=[ptb], writes=["maskT"])
            nsc = 4 * st + 4
            for hp in range(8):
                for half in range(2):
                    h = HPERM[2 * hp + half]
                    g = h // 4
                    hs = slice(half * 64, (half + 1) * 64)
                    for sc in range(nsc):
                        ps = P()
                        S.op("pe", lambda e, ps=ps, hs=hs, g=g, sc=sc: e.matmul(ps[:, :ST], kT2[hs, g // 2, sc * 128:(sc + 1) * 128], qT[hs, hp, :],
                                                                                start=True, stop=True), reads=[kT2, "qT"], writes=[ps])
                        c.ei = (c.ei + 1) % 2
                        E = Es[c.ei]
                        S.op("act", lambda e, ps=ps, E=E: e.activation(out=E[:], in_=ps[:, :ST], func=AF.Exp, scale=0.125), reads=[ps], writes=[E])
                        S.op("dve", lambda e, E=E, sc=sc: e.tensor_tensor(out=E[:], in0=E[:], in1=maskT[:, sc, :], op=ALU.mult),
                             reads=[E, "maskT"], writes=[E])
                        S.op("pe", lambda e, E=E, hs=hs, g=g, sc=sc: e.matmul(accA[hs, :ST], vtok[:, sc, g * 64:(g + 1) * 64], E[:],
                                                                              start=(sc == 0), stop=(sc == nsc - 1)), reads=[vtok, E], writes=[accA])
                        S.op("pe", lambda e, E=E, hs=hs, sc=sc: e.matmul(accB[hs, :ST], onesb[:, 0:64], E[:],
                                                                         start=(sc == 0), stop=(sc == nsc - 1)), reads=[cstb, E], writes=[accB])
                S.op("dve", lambda e: e.reciprocal(out=rden[:], in_=accB[:, :ST]), reads=[accB], writes=[rden])
                S.op("dve", lambda e, hp=hp: e.tensor_tensor(out=yattT[:, hp, :], in0=accA[:, :ST], in1=rden[:], op=ALU.mult),
                     reads=[accA, rden], writes=["yattT"])

        def merge(T):
            win = W["w_in"]
            S.alias("mergedT", ["acc"])
            for k in range(KC):
                def gate(c0, sg):
                    buf, view = wload(win, 0, KC, c0 + k * 128, 128)
                    ps = P()
                    for kk in range(KC):
                        S.op("pe", lambda e, kk=kk, ps=ps, view=view: e.matmul(ps[:, :T], view[:, kk, :], uT[:, kk, :T], start=(kk == 0),
                                                                              stop=(kk == KC - 1)), reads=[buf, uT], writes=[ps])
                    S.op("act", lambda e, ps=ps: e.activation(out=sg[:, :T], in_=ps[:, :T], func=AF.Sigmoid), reads=[ps], writes=[sg])
                gate(O_GS, sgs[0])
                buf, view = wload(W["w_br_ssd"], 0, 16, k * 128, 128)
                ps1 = P()
                for kk in range(16):
                    S.op("pe", lambda e, kk=kk, view=view: e.matmul(ps1[:, :T], view[:, kk, :], yssdT[:, kk, :T], start=(kk == 0), stop=(kk == 15)),
                         reads=[buf, "yssdT"], writes=[ps1])
                S.op("dve", lambda e: e.tensor_tensor(out=tmpf[:, :T], in0=ps1[:, :T], in1=sgs[0][:, :T], op=ALU.mult),
                     reads=[ps1, sgs[0]], writes=[tmpf])
                gate(O_GA, sgs[1])
                buf2, view2 = wload(W["w_br_att"], 0, KC, k * 128, 128)
                ps2 = P()
                for kk in range(KC):
                    S.op("pe", lambda e, kk=kk, view2=view2: e.matmul(ps2[:, :T], view2[:, kk, :], yattT[:, kk, :T], start=(kk == 0), stop=(kk == KC - 1)),
                         reads=[buf2, "yattT"], writes=[ps2])
                S.op("dve", lambda e: e.tensor_tensor(out=sqb[:, :T], in0=ps2[:, :T], in1=sgs[1][:, :T], op=ALU.mult),
                     reads=[ps2, sgs[1]], writes=[sqb])
                S.op("dve", lambda e, k=k: e.tensor_tensor(out=mergedT[:, k, :T], in0=tmpf[:, :T], in1=sqb[:, :T], op=ALU.add),
                     reads=[tmpf, sqb], writes=["mergedT"])
            S.alias("ybuf", SCRK)
            proj_fm(W["w_out"], 0, KC, 0, D, lambda k: mergedT[:, k, :T], ["mergedT"], T,
                    lambda idx, ps, ms: S.op("act", lambda e: e.activation(out=ybuf[:, idx, :T], in_=ps[:, :T], func=AF.Copy), reads=[ps], writes=["ybuf"]))
            postnorm_add("mix_post_g", T, 1.0)

        def mem_attn(T, batches):
            prenorm("mem_pre_g", T)
            proj_fm(W["w_mq"], 0, KC, 0, D, lambda k: uT[:, k, :T], [uT], T,
                    lambda idx, ps, ms: copy_alt(idx, qT[:, idx, :T], ps[:, :T], [ps], ["qT"]))
            for cs, bsel in batches:
                nT = cs.stop - cs.start
                if bsel is not None:
                    S.dma("mkl", lambda e: e.dma_start(out=mkTb[:], in_=cmkT_d[bsel].rearrange("(k p) m -> p k m", p=128)), writes=[mkTb], queue="pool")
                    S.dma("mvl", lambda e: e.dma_start(out=mvb[:], in_=cmv_d[bsel].rearrange("(k p) c -> p k c", p=128)), writes=[mvb], queue="pool")
                for h in range(4):
                    Em = []
                    for mc in range(2):
                        ps = P()
                        for dc in range(2):
                            S.op("pe", lambda e, ps=ps: e.matmul(ps[:, :nT], mkTb[:, h * 2 + dc, mc * 128:(mc + 1) * 128], qT[:, h * 2 + dc, cs],
                                                                 start=(dc == 0), stop=(dc == 1)), reads=[mkTb, "qT"], writes=[ps])
                        E = Es[mc]
                        S.op("act", lambda e, ps=ps, E=E: e.activation(out=E[:, :nT], in_=ps[:, :nT], func=AF.Exp, scale=1.0 / 16.0), reads=[ps], writes=[E])
                        Em.append(E)
                    for mc in range(2):
                        S.op("pe", lambda e: e.matmul(accB[:, :nT], onesb, Em[mc][:, :nT], start=(mc == 0), stop=(mc == 1)), reads=[cstb, Em[mc]], writes=[accB])
                    S.op("dve", lambda e: e.reciprocal(out=rden[:, :nT], in_=accB[:, :nT]), reads=[accB], writes=[rden])
                    for dc in range(2):
                        for mc in range(2):
                            S.op("pe", lambda e: e.matmul(accA[:, :nT], mvb[:, mc, h * 256 + dc * 128:h * 256 + (dc + 1) * 128], Em[mc][:, :nT],
                                                          start=(mc == 0), stop=(mc == 1)), reads=[mvb, Em[mc]], writes=[accA])
                        S.op("dve", lambda e: e.tensor_tensor(out=yattT[:, h * 2 + dc, cs], in0=accA[:, :nT], in1=rden[:, :nT], op=ALU.mult),
                             reads=[accA, rden], writes=["yattT"])
            proj_fm(W["w_mo"], 0, KC, 0, D, lambda k: yattT[:, k, :T], ["yattT"], T,
                    lambda idx, ps, ms: S.op("act", lambda e: e.activation(out=ybuf[:, idx, :T], in_=ps[:, :T], func=AF.Copy), reads=[ps], writes=["ybuf"]))
            postnorm_add("mem_post_g", T, 1.0)

        def memory_kv_prompt():
            S.dma("xin", lambda e: e.dma_start(out=hT[:, :, 0:256], in_=memT.rearrange("(k p) t -> p k t", p=128)), writes=[hT])
            prenorm("mem_kv_g", 256)

            def cons(idx, ps, ms):
                S.op("dve", lambda e: e.tensor_copy(out=mkTb[:, idx, :], in_=ps[:, 0:256]), reads=[ps], writes=[mkTb])
                stage_out(ps[:, 0:256], 128, 256, mkT_o[idx * 128:(idx + 1) * 128, :], (ps, "mkT_o"))
            proj_fm(W["w_mk"], 0, KC, 0, D, lambda k: uT[:, k, 0:256], [uT], 256, cons)
            for cb in range(2):
                buf, view = wload(W["w_mv"], 0, KC, cb * 512, 512)
                for mc in range(2):
                    ps = P()
                    tm_proj(view, buf, 0, 512, mc, ps)
                    S.op("dve", lambda e, mc=mc, cb=cb, ps=ps: e.tensor_copy(out=mvb[:, mc, cb * 512:(cb + 1) * 512], in_=ps[:, :]), reads=[ps], writes=[mvb])
                    stage_out(ps[:, :], 128, 512, mv_o[mc * 128:(mc + 1) * 128, cb * 512:(cb + 1) * 512], (ps, "mv_o"))


        def sample_path():
            T = 16
            cst2 = S.sb([128, C2N], F32, "cst2")
            S.dma("cst2", lambda e: e.dma_start(out=cst2[:], in_=cst2_d), writes=[cst2])
            Gblk = cst2[:, 0:128]
            Gsel = cst2[:, 128:132]
            negnew = cst2[:, 132:136]
            pmod64 = cst2[:, 136:137]
            Rep = cst2[0:4, 137:265]
            Dsel = cst2[0:32, 265:269]
            hsel = cst2[0:8, 269:301]
            onesf = cst[:, 384:512]
            kibs = [kT2[:].rearrange("p a b -> p (a b)")[:, i * 2048:(i + 1) * 2048].bitcast(F32) for i in range(2)]
            vt_f = vtok[:].rearrange("p a b -> p (a b)").bitcast(F32)
            Ksel = vt_f[:, 0:1024].rearrange("p (c d) -> p c d", c=4)
            Vsel = vt_f[:, 1024:2048].rearrange("p (c d) -> p c d", c=4)
            qrow = kiT2[:].bitcast(F32)
            prodb = mask01t[:].bitcast(F32)
            kv4 = stT[:].rearrange("p g c -> p (g c)")[0:4, :].rearrange("p (b c) -> p b c", b=4)
            for nk_, ok_ in (("kib0", kT2), ("kib1", kT2), ("KV", vtok), ("qrow", kiT2), ("prodb", mask01t), ("kv4", stT)):
                S.alias(nk_, [ok_])
            hists = S.sb([128, 32, 4, 3], F32, "hists")
            S.dma("hists", lambda e: e.dma_start(out=hists[:].rearrange("p a b c -> p (a b c)"), in_=hists_d), writes=[hists])
            pres = S.sb([128, 4, 4, 7], F32, "pres")
            stS = [S.sb([128, 256], F32, f"stS{i}") for i in range(2)]
            stSb = [S.sb([128, 256], BF16, f"stSb{i}") for i in range(2)]
            sc = S.sb([128, 516], F32, "sc")
            work = S.sb([128, 512], F32, "work")
            qiU = S.sb([128, 4, 32], F32, "qiU")
            qiE = S.sb([128, 4, 32], F32, "qiE")
            qiO = S.sb([128, 4, 32], F32, "qiO")
            kiTs = S.sb([64, 16], F32, "kiTs")
            wiT = S.sb([8, 16], F32, "wiT")
            WW = S.sb([32, 252], F32, "WW")
            ptrow = S.sb([128, 128], I32, "ptrow")
            ptf = S.sb([128, 128], F32, "ptf")
            kix = S.sb([128, 128], I32, "kix")
            ptji = S.sb([128, 4], I32, "ptji")
            smf = S.sb([128, 256], F32, "smf")
            smi = S.sb([128, 96], I32, "smi")
            rowi = S.sb([128, 32], I32, "rowi")
            opart = S.sb([128, 1024], F32, "opart")
            q4 = opart[0:4, :]
            sE = S.sb([128, 64], F32, "sE")
            vals = smf[:, 0:32]
            valid = smf[:, 32:64]
            validn = smf[:, 64:68]
            wicol = smf[0:32, 68:69]
            rmax = smf[:, 69:70]
            Rr, lo, stp, cand, cnt, gg = [smf[:, 70 + i:71 + i] for i in range(6)]
            ptjf = smf[:, 76:80]
            pglf = smf[:, 80:112]
            offf = smf[:, 112:144]
            physf = smf[:, 144:176]
            tmp32 = smf[:, 176:208]
            denp = smf[:, 208:224]
            tmp16 = smf[:, 224:240]
            rd4 = smf[:, 240:244]
            rn4 = smf[0:32, 244:248]
            S.op("dve", lambda e: e.memset(WW[:], 0.0), writes=[WW])

            S.dma("xin", lambda e: e.dma_start(out=hT[:, :, 0:T], in_=xsT.rearrange("(k p) t -> p k t", p=128)), writes=[hT])
            ffn("ffn1", T)
            for k_ in MIXK:
                S.alias(k_, ["hid"])
            for k_ in SCRK:
                S.alias(k_, ["ybuf"])
            S.alias("ssdtmp", ["maskT"])
            prenorm("mix_pre_g", T)
            win = W["w_in"]
            buf, view = wload(win, 0, KC, O_K, 512)
            for b in range(4):
                ps = P()
                for k in range(KC):
                    S.op("pe", lambda e, k=k, ps=ps, view=view: e.matmul(ps[0:4, 0:512], uT[:, k, b * 4:(b + 1) * 4], view[:, k, :],
                                                                        start=(k == 0), stop=(k == KC - 1)), reads=[buf, uT], writes=[ps])
                S.op("act", lambda e, ps=ps: e.activation(out=kv4[:, b, :], in_=ps[0:4, 0:512], func=AF.Copy), reads=[ps], writes=["kv4"])
                S.dma("kso", lambda e: e.dma_start(out=ks_o[b * 4:(b + 1) * 4, :], in_=kv4[:, b, 0:256]), reads=["kv4"], writes=["ks_o"])
                S.dma("vso", lambda e: e.dma_start(out=vs_o[b * 4:(b + 1) * 4, :], in_=kv4[:, b, 256:512]), reads=["kv4"], writes=["vs_o"])
            buf, view = wload(win, 0, KC, O_KI - 32, 128)
            ps = P()
            for k in range(KC):
                S.op("pe", lambda e, k=k, ps=ps, view=view: e.matmul(ps[0:64, 0:T], view[:, k, 32:96], uT[:, k, :T], start=(k == 0), stop=(k == KC - 1)),
                     reads=[buf, uT], writes=[ps])
            S.op("act", lambda e, ps=ps: e.activation(out=kiTs[:], in_=ps[0:64, 0:T], func=AF.Copy), reads=[ps], writes=[kiTs])
            S.dma("kiso", lambda e: e.dma_start(out=kisT_o, in_=kiTs[:]), reads=[kiTs], writes=["kisT_o"])
            ps = P()
            for k in range(KC):
                S.op("pe", lambda e, k=k, ps=ps, view=view: e.matmul(ps[0:8, 0:T], view[:, k, 96:104], uT[:, k, :T], start=(k == 0), stop=(k == KC - 1)),
                     reads=[buf, uT], writes=[ps])
            S.op("act", lambda e, ps=ps: e.activation(out=wiT[:], in_=ps[0:8, 0:T], func=AF.Copy), reads=[ps], writes=[wiT])
            buf, view = wload(win, 0, KC, O_DT, 128)
            for b in range(4):
                ps = P()
                for k in range(KC):
                    S.op("pe", lambda e, k=k, ps=ps, view=view: e.matmul(ps[0:4, 0:32], uT[:, k, b * 4:(b + 1) * 4], view[:, k, 0:32],
                                                                        start=(k == 0), stop=(k == KC - 1)), reads=[buf, uT], writes=[ps])
                S.op("dve", lambda e, ps=ps: e.tensor_tensor(out=dtt[0:4, b, :], in0=ps[0:4, 0:32], in1=rowp[0:4, 0:32], op=ALU.add),
                     reads=[ps, rowp], writes=[dtt])
            S.op("act", lambda e: e.activation(out=dtt[0:4], in_=dtt[0:4], func=AF.Exp), reads=[dtt], writes=[dtt])
            S.op("act", lambda e: e.activation(out=dtt[0:4], in_=dtt[0:4], func=AF.Ln, bias=1.0), reads=[dtt], writes=[dtt])
            for b in range(4):
                S.op("dve", lambda e: e.tensor_tensor(out=dta[0:4, b, :], in0=dtt[0:4, b, :], in1=arow[0:4, :], op=ALU.mult),
                     reads=[dtt, arow], writes=[dta])
                ps = P()
                S.op("pe", lambda e, ps=ps: e.matmul(ps[0:4, 0:32], Uf[0:4, 0:4], dta[0:4, b, :], start=True, stop=True), reads=[cst, dta], writes=[ps])
                S.op("act", lambda e, ps=ps: e.activation(out=acsc[0:4, b, :], in_=ps[0:4, 0:32], func=AF.Copy), reads=[ps], writes=[acsc])
            buf, view = wload(win, 0, KC, O_QI, 512)
            ps = P()
            for h in range(8):
                for half in range(2):
                    for k in range(KC):
                        S.op("pe", lambda e, k=k, ps=ps, view=view: e.matmul(ps[half * 64:(half + 1) * 64, h * 16:(h + 1) * 16], view[:, k, h * 64:(h + 1) * 64],
                                                                            uT[:, k, :T], start=(k == 0), stop=(k == KC - 1)), reads=[buf, uT], writes=[ps])
            S.op("act", lambda e, ps=ps: e.activation(out=qiU[:].rearrange("p b (h t) -> p h b t", h=8), in_=ps[:, 0:128].rearrange("p (h b t) -> p h b t", h=8, b=4),
                                                    func=AF.Copy), reads=[ps], writes=[qiU])
            S.op("dve", lambda e: e.memset(qiE[:], 0.0), writes=[qiE])
            S.op("dve", lambda e: e.memset(qiO[:], 0.0), writes=[qiO])
            S.op("dve", lambda e: e.tensor_copy(out=qiE[0:64], in_=qiU[0:64]), reads=[qiU], writes=[qiE])
            S.op("dve", lambda e: e.tensor_copy(out=qiO[64:128], in_=qiU[64:128]), reads=[qiU], writes=[qiO])
            for g in range(8):
                buf, view = wload(win, 0, KC, O_Z + g * 256, 256)
                for m in range(2):
                    ps = P()
                    for k in range(KC):
                        S.op("pe", lambda e, k=k, ps=ps, view=view: e.matmul(ps[:, :T], view[:, k, m * 128:(m + 1) * 128], uT[:, k, :T],
                                                                            start=(k == 0), stop=(k == KC - 1)), reads=[buf, uT], writes=[ps])
                    S.op("act", lambda e, ps=ps: e.activation(out=zs[:, m, :T], in_=ps[:, :T], func=AF.Silu), reads=[ps], writes=["zs"])
                chs = [g * 2, g * 2 + 1, 16 + g, 24 + g]
                for j in range(4):
                    if j == 0:
                        bufx, viewx = wload(win, 0, KC, O_X + g * 256, 256)
                    if j == 2:
                        bufx, viewx = wload(win, 0, KC, O_B + g * 128, 128)
                    if j == 3:
                        bufx, viewx = wload(win, 0, KC, O_C + g * 128, 128)
                    c0 = 128 if j == 1 else 0
                    ps = P()
                    for k in range(KC):
                        S.op("pe", lambda e, k=k, ps=ps, viewx=viewx: e.matmul(ps[:, :T], viewx[:, k, c0:c0 + 128], uT[:, k, :T],
                                                                              start=(k == 0), stop=(k == KC - 1)), reads=[bufx, uT], writes=[ps])
                    ch = chs[j]
                    S.op("dve", lambda e: e.tensor_copy(out=pres[:, j, :, 0:3], in_=hists[:, ch, :, :]), reads=[hists], writes=[pres])
                    S.op("act", lambda e, ps=ps: e.activation(out=pres[:, j, :, 3:7], in_=ps[:, 0:T].rearrange("p (b t) -> p b t", b=4), func=AF.Copy),
                         reads=[ps], writes=[pres])
                    S.op("dve", lambda e: e.tensor_copy(out=hists[:, ch, :, :], in_=pres[:, j, :, 4:7]), reads=[pres], writes=[hists])
                    cwf = lambda tap: pp[:, PP_CW + ch * 4 + tap:PP_CW + ch * 4 + tap + 1]
                    tv = tmpf[:, 0:T].rearrange("p (b t) -> p b t", b=4)
                    S.op("dve", lambda e: e.tensor_scalar(out=tv, in0=pres[:, j, :, 0:4], scalar1=cwf(0), scalar2=pp[:, PP_CB + ch:PP_CB + ch + 1],
                                                          op0=ALU.mult, op1=ALU.add), reads=[pres, pp], writes=[tmpf])
                    for tap in range(1, 4):
                        S.op("dve", lambda e: e.scalar_tensor_tensor(out=tv, in0=pres[:, j, :, tap:tap + 4], scalar=cwf(tap), in1=tv,
                                                                     op0=ALU.mult, op1=ALU.add), reads=[pres, pp, tmpf], writes=[tmpf])
                    S.op("act", lambda e: e.activation(out=cv[:, j, :T], in_=tmpf[:, :T], func=AF.Silu), reads=[tmpf], writes=[("cv", j)])
                for b in range(4):
                    si = (g * 4 + b) % 2
                    sF, sB = stS[si], stSb[si]
                    S.dma(("sts", si), lambda e: e.dma_start(out=sF[:], in_=ssmsT_d[b, :, g * 256:(g + 1) * 256]), writes=[sF])
                    S.op("act", lambda e: e.activation(out=sB[:], in_=sF[:], func=AF.Copy), reads=[sF], writes=[sB])
                    ssd_chunk(g, b, 4, b * 4, sf=lambda hh, sF=sF: sF[:, hh * 64:(hh + 1) * 64], sbf=lambda hh, sB=sB: sB[:, hh * 64:(hh + 1) * 64],
                              skeys=(sF, sB), sfa=sF[:], sba=sB[:])
                    S.dma(("sto", si), lambda e: e.dma_start(out=ssms_o[b, :, g * 256:(g + 1) * 256], in_=sF[:]), reads=[sF], writes=["ssms_o"])
                for m in range(2):
                    S.op("dve", lambda e: e.tensor_tensor(out=Yg[:, m, :T], in0=Yg[:, m, :T], in1=zs[:, m, :T], op=ALU.mult),
                         reads=[Yg, "zs"], writes=[Yg])
                norm_stats([Yg[:, 0, :T], Yg[:, 1, :T]], Yg, T, 256.0)
                for m in range(2):
                    S.op("dve", lambda e: e.scalar_tensor_tensor(out=yssdT[:, g * 2 + m, :T], in0=Yg[:, m, :T],
                                                                 scalar=pp[:, PP_NG + g * 2 + m:PP_NG + g * 2 + m + 1], in1=rstd[:, :T],
                                                                 op0=ALU.mult, op1=ALU.mult), reads=[Yg, rstd, pp], writes=["yssdT"])
            S.dma("convso", lambda e: e.dma_start(out=convs_o, in_=hists[:].rearrange("p a b c -> p (a b c)")), reads=[hists], writes=["convs_o"])
            for b in range(4):
                cs = slice(b * 4, b * 4 + 4)
                S.dma("ptrow", lambda e: e.dma_start(out=ptrow[:], in_=ptab_d[b:b + 1, :].partition_broadcast(128)), writes=[ptrow])
                S.op("dve", lambda e: e.tensor_copy(out=ptf[:], in_=ptrow[:]), reads=[ptrow], writes=[ptf])
                t64 = work[:, 0:64]
                S.op("dve", lambda e: e.tensor_tensor(out=t64, in0=ptf[:, 1:128:2], in1=ptf[:, 0:127:2], op=ALU.subtract), reads=[ptf], writes=[work])
                S.op("dve", lambda e: e.scalar_tensor_tensor(out=t64, in0=t64, scalar=cst2[:, 941:942], in1=ptf[:, 0:127:2], op0=ALU.mult, op1=ALU.add),
                     reads=[work, ptf, cst2], writes=[work])
                S.op("dve", lambda e: e.tensor_scalar(out=t64, in0=t64, scalar1=64.0, scalar2=pmod64, op0=ALU.mult, op1=ALU.add),
                     reads=[work, cst2], writes=[work])
                S.op("dve", lambda e: e.tensor_copy(out=kix[:, 0:64], in_=t64), reads=[work], writes=[kix])
                for par in range(2):
                    S.dma("ptji", lambda e: e.dma_start(out=ptji[par * 16:(par + 1) * 16, :], in_=ptab_d[b, :].rearrange("(i e) -> i e", e=8)[:, par:par + 7:2],
                                                        allow_slow_non_contiguous=True), writes=[ptji])
                S.op("dve", lambda e: e.tensor_copy(out=tmp16[0:32, 0:4], in_=ptji[0:32, :]), reads=[ptji], writes=[smf])
                ps = P()
                S.op("pe", lambda e, ps=ps: e.matmul(ps[:, 0:4], cst2[0:32, 813:941], tmp16[0:32, 0:4], start=True, stop=True), reads=[cst2, smf], writes=[ps])
                S.op("dve", lambda e, ps=ps: e.tensor_copy(out=ptjf, in_=ps[:, 0:4]), reads=[ps], writes=[smf])
                ps = P()
                S.op("pe", lambda e, ps=ps: e.matmul(ps[0:32, 0:4], hsel, wiT[0:8, cs], start=True, stop=True), reads=[cst2, wiT], writes=[ps])
                S.op("dve", lambda e, ps=ps: e.tensor_tensor(out=rn4, in0=ps[0:32, 0:4], in1=Dsel, op=ALU.mult), reads=[ps, cst2], writes=[smf])
                S.op("dve", lambda e: e.tensor_reduce(out=wicol, in_=rn4, axis=AX.X, op=ALU.add), reads=[smf], writes=[smf])
                S.op("dve", lambda e: e.tensor_scalar(out=WW[:, 124:128], in0=Dsel, scalar1=wicol, scalar2=None, op0=ALU.mult), reads=[smf, cst2], writes=[WW])
                qil = qiU[0:64, b, :]
                nmm = 0
                for i8 in range(16):
                    kb = kibs[i8 % 2]
                    kkey = f"kib{i8 % 2}"
                    for c4 in range(4):
                        pq = i8 * 4 + c4
                        S.dma(("kibd", i8 % 2), lambda e: e.indirect_dma_start(
                            out=kb[:, c4 * 128:(c4 + 1) * 128], out_offset=None, in_=kidxT_d,
                            in_offset=bass.IndirectOffsetOnAxis(ap=kix[:, pq:pq + 1], axis=0)), reads=[kix], writes=[kkey], queue="pool")
                    for par in range(2):
                        jv = par * 16 + i8
                        ps = P()
                        S.op("pe", lambda e, ps=ps: e.matmul(ps[0:32, 0:512], (qiE if par == 0 else qiO)[:, b, :], kb[:, 0:512], start=True, stop=True),
                             reads=[qiE, qiO, kkey], writes=[ps])
                        rr = rrs[nmm % 2]
                        S.op("act", lambda e, ps=ps: e.activation(out=rr[0:32, :], in_=ps[0:32, 0:512], func=AF.Relu), reads=[ps], writes=[rr])
                        S.op("pe", lambda e: e.matmul(accA[:, 0:512], WW[:, 124 - 4 * jv:252 - 4 * jv], rr[0:32, :], start=(nmm == 0), stop=(nmm == 31)),
                             reads=[WW, rr], writes=[accA])
                        nmm += 1
                ps = P()
                S.op("pe", lambda e, ps=ps: e.matmul(ps[0:32, 0:4], qil, kiTs[:, cs], start=True, stop=True), reads=[qiU, kiTs], writes=[ps])
                S.op("act", lambda e, ps=ps: e.activation(out=rn4, in_=ps[0:32, 0:4], func=AF.Relu), reads=[ps], writes=[smf])
                S.op("pe", lambda e: e.matmul(accB[:, 0:4], WW[:, 124:252], rn4, start=True, stop=True), reads=[WW, smf], writes=[accB])
                S.op("act", lambda e: e.activation(out=sc[:, 0:512], in_=accA[:, 0:512], func=AF.Copy), reads=[accA], writes=[sc])
                S.op("dve", lambda e: e.tensor_tensor(out=sc[:, 512:516], in0=accB[:, 0:4], in1=negnew, op=ALU.add), reads=[accB, cst2], writes=[sc])
                S.op("dve", lambda e: e.tensor_reduce(out=rmax, in_=sc[:, 0:512], axis=AX.X, op=ALU.max, apply_absolute_value=True), reads=[sc], writes=[smf])
                ps = P()
                S.op("pe", lambda e, ps=ps: e.matmul(ps[:, 0:1], onesf, rmax, start=True, stop=True), reads=[cst, smf], writes=[ps])
                S.op("dve", lambda e, ps=ps: e.tensor_copy(out=Rr, in_=ps[:, 0:1]), reads=[ps], writes=[smf])
                S.op("dve", lambda e: e.tensor_scalar(out=lo, in0=Rr, scalar1=-1.0, scalar2=None, op0=ALU.mult), reads=[smf], writes=[smf])
                for it in range(1, NITS + 1):
                    S.op("dve", lambda e: e.tensor_scalar(out=stp, in0=Rr, scalar1=2.0 ** (1 - it), scalar2=None, op0=ALU.mult), reads=[smf], writes=[smf])
                    S.op("dve", lambda e: e.tensor_tensor(out=cand, in0=lo, in1=stp, op=ALU.add), reads=[smf], writes=[smf])
                    S.op("dve", lambda e: e.tensor_scalar(out=work[:, 0:516 - 4], in0=sc[:, 0:512], scalar1=cand, scalar2=None, op0=ALU.is_ge,
                                                          op1=ALU.add, accum_out=cnt), reads=[sc, smf], writes=[work, smf])
                    S.op("dve", lambda e: e.tensor_scalar(out=tmp16[:, 0:4], in0=sc[:, 512:516], scalar1=cand, scalar2=None, op0=ALU.is_ge,
                                                          op1=ALU.add, accum_out=gg), reads=[sc, smf], writes=[smf])
                    S.op("dve", lambda e: e.tensor_tensor(out=cnt, in0=cnt, in1=gg, op=ALU.add), reads=[smf], writes=[smf])
                    ps = P()
                    S.op("pe", lambda e, ps=ps: e.matmul(ps[:, 0:1], Gblk, cnt, start=True, stop=True), reads=[cst2, smf], writes=[ps])
                    S.op("dve", lambda e, ps=ps: e.tensor_scalar(out=gg, in0=ps[:, 0:1], scalar1=255.5, scalar2=None, op0=ALU.is_ge), reads=[ps], writes=[smf])
                    S.op("dve", lambda e: e.scalar_tensor_tensor(out=lo, in0=gg, scalar=stp, in1=lo, op0=ALU.mult, op1=ALU.add), reads=[smf], writes=[smf])
                S.op("dve", lambda e: e.tensor_copy(out=work[:], in_=sc[:, 0:512]), reads=[sc], writes=[work])
                cidx = smi[:, 0:32]
                for r in range(4):
                    S.op("dve", lambda e: e.max(out=vals[:, r * 8:(r + 1) * 8], in_=work[:]), reads=[work], writes=[smf])
                    S.op("dve", lambda e: e.max_index(out=cidx[:, r * 8:(r + 1) * 8].bitcast(U32), in_max=vals[:, r * 8:(r + 1) * 8], in_values=work[:]),
                         reads=[work, smf], writes=[smi])
                    S.op("dve", lambda e: e.match_replace(out=work[:], in_to_replace=vals[:, r * 8:(r + 1) * 8], in_values=work[:], imm_value=-3e38),
                         reads=[work, smf], writes=[work])
                S.op("dve", lambda e: e.tensor_scalar(out=valid, in0=vals, scalar1=lo, scalar2=None, op0=ALU.is_ge), reads=[smf], writes=[smf])
                S.op("dve", lambda e: e.tensor_scalar(out=validn, in0=sc[:, 512:516], scalar1=lo, scalar2=None, op0=ALU.is_ge), reads=[smf, sc], writes=[smf])
                S.op("dve", lambda e: e.tensor_scalar(out=smi[:, 32:64], in0=cidx, scalar1=7, scalar2=None, op0=ALU.arith_shift_right), reads=[smi], writes=[smi])
                S.op("dve", lambda e: e.tensor_scalar(out=smi[:, 64:96], in0=cidx, scalar1=127, scalar2=None, op0=ALU.bitwise_and), reads=[smi], writes=[smi])
                S.op("dve", lambda e: e.tensor_copy(out=pglf, in_=smi[:, 32:64]), reads=[smi], writes=[smf])
                S.op("dve", lambda e: e.tensor_copy(out=offf, in_=smi[:, 64:96]), reads=[smi], writes=[smf])
                for kq in range(4):
                    dst = physf if kq == 0 else tmp32
                    S.op("dve", lambda e: e.tensor_scalar(out=dst, in0=pglf, scalar1=float(kq), scalar2=ptjf[:, PGP[kq]:PGP[kq] + 1], op0=ALU.is_equal, op1=ALU.mult),
                         reads=[smf], writes=[smf])
                    if kq > 0:
                        S.op("dve", lambda e: e.tensor_tensor(out=physf, in0=physf, in1=tmp32, op=ALU.add), reads=[smf], writes=[smf])
                S.op("dve", lambda e: e.scalar_tensor_tensor(out=physf, in0=physf, scalar=128.0, in1=offf, op0=ALU.mult, op1=ALU.add), reads=[smf], writes=[smf])
                S.op("dve", lambda e: e.tensor_copy(out=rowi[:], in_=physf), reads=[smf], writes=[rowi])
                for half in range(2):
                    buf, view = wload(win, 0, KC, O_Q + half * 512, 512)
                    ps = P()
                    for k in range(KC):
                        S.op("pe", lambda e, k=k, ps=ps, view=view: e.matmul(ps[0:4, 0:512], uT[:, k, cs], view[:, k, :], start=(k == 0), stop=(k == KC - 1)),
                             reads=[buf, uT], writes=[ps])
                    S.op("act", lambda e, ps=ps: e.activation(out=q4[:, half * 512:(half + 1) * 512], in_=ps[0:4, 0:512], func=AF.Copy), reads=[ps], writes=[opart])
                    ps = P()
                    S.op("pe", lambda e, ps=ps: e.matmul(ps[:, 0:512], Rep, q4[:, half * 512:(half + 1) * 512], start=True, stop=True), reads=[cst2, opart], writes=[ps])
                    S.op("act", lambda e, ps=ps: e.activation(out=qrow[:, half * 512:(half + 1) * 512], in_=ps[:, 0:512], func=AF.Copy, scale=0.125),
                         reads=[ps], writes=["qrow"])
                qrow3 = qrow.rearrange("p (i d) -> p i d", i=16)
                opart3 = opart[:].rearrange("p (i d) -> p i d", i=16)
                S.op("dve", lambda e: e.memset(opart[:], 0.0), writes=[opart])
                S.op("dve", lambda e: e.memset(denp, 0.0), writes=[smf])
                s_all = sE[:]
                s_all3 = sE[:].rearrange("p (c i) -> p c i", c=4)
                sT3 = sE[:].rearrange("p (c i) -> p i c", c=4)
                q4acc = work[:, 0:256].rearrange("p (r d) -> p r d", r=4)
                prodK = prodb[:, 0:1024].rearrange("p (c r d) -> p c r d", c=4, r=4)
                prodV = prodb[:, 0:1024].rearrange("p (r d c) -> p r d c", r=4, d=64)
                for rnd in range(9):
                    if rnd < 8:
                        for c4 in range(4):
                            cc_ = rnd * 4 + c4
                            S.dma("ksel", lambda e: e.indirect_dma_start(out=Ksel[:, c4, :], out_offset=None, in_=poolk_d,
                                                                         in_offset=bass.IndirectOffsetOnAxis(ap=rowi[:, cc_:cc_ + 1], axis=0)),
                                  reads=[rowi], writes=["KV"], queue="pool")
                            S.dma("vsel", lambda e: e.indirect_dma_start(out=Vsel[:, c4, :], out_offset=None, in_=poolv_d,
                                                                         in_offset=bass.IndirectOffsetOnAxis(ap=rowi[:, cc_:cc_ + 1], axis=0)),
                                  reads=[rowi], writes=["KV"], queue="pool")
                        vmask = valid[:, rnd * 4:(rnd + 1) * 4]
                    else:
                        for tq in range(4):
                            ps = P()
                            S.op("pe", lambda e, ps=ps: e.matmul(ps[:, 0:512], cst2[0:4, 301 + tq * 128:301 + (tq + 1) * 128], kv4[:, b, :], start=True, stop=True),
                                 reads=[cst2, "kv4"], writes=[ps])
                            S.op("act", lambda e, ps=ps: e.activation(out=Ksel[:, tq, :], in_=ps[:, 0:256], func=AF.Copy), reads=[ps], writes=["KV"])
                            S.op("act", lambda e, ps=ps: e.activation(out=Vsel[:, tq, :], in_=ps[:, 256:512], func=AF.Copy), reads=[ps], writes=["KV"])
                        vmask = validn
                    for g in range(4):
                        base = (g // 2) * 8 + (g % 2)
                        S.op("dve", lambda e: e.tensor_tensor(out=prodK, in0=Ksel[:, :, g * 64:(g + 1) * 64].unsqueeze(2).to_broadcast([128, 4, 4, 64]),
                                                              in1=qrow3[:, base:base + 7:2, :].unsqueeze(1).to_broadcast([128, 4, 4, 64]), op=ALU.mult),
                             reads=["KV", "qrow"], writes=["prodb"])
                        S.op("dve", lambda e: e.tensor_reduce(out=s_all3[:, :, base:base + 7:2], in_=prodK, axis=AX.X, op=ALU.add), reads=["prodb"], writes=[sE])
                    S.op("act", lambda e: e.activation(out=s_all, in_=s_all, func=AF.Exp), reads=[sE], writes=[sE])
                    S.op("dve", lambda e: e.tensor_tensor(out=s_all3, in0=s_all3, in1=vmask.unsqueeze(2).to_broadcast([128, 4, 16]), op=ALU.mult),
                         reads=[sE, smf], writes=[sE])
                    S.op("dve", lambda e: e.tensor_reduce(out=tmp16, in_=sT3, axis=AX.X, op=ALU.add), reads=[sE], writes=[smf])
                    S.op("dve", lambda e: e.tensor_tensor(out=denp, in0=denp, in1=tmp16, op=ALU.add), reads=[smf], writes=[smf])
                    for g in range(4):
                        base = (g // 2) * 8 + (g % 2)
                        S.op("dve", lambda e: e.tensor_tensor(out=prodV, in0=sT3[:, base:base + 7:2, :].unsqueeze(2).to_broadcast([128, 4, 64, 4]),
                                                              in1=Vsel[:, :, g * 64:(g + 1) * 64].rearrange("p c d -> p d c").unsqueeze(1).to_broadcast([128, 4, 64, 4]),
                                                              op=ALU.mult), reads=[sE, "KV"], writes=["prodb"])
                        S.op("dve", lambda e: e.tensor_reduce(out=q4acc, in_=prodV, axis=AX.X, op=ALU.add), reads=["prodb"], writes=[work])
                        S.op("dve", lambda e: e.tensor_tensor(out=opart3[:, base:base + 7:2, :], in0=opart3[:, base:base + 7:2, :], in1=q4acc, op=ALU.add),
                             reads=[work, opart], writes=[opart])
                ps = P()
                S.op("pe", lambda e, ps=ps: e.matmul(ps[:, 0:16], Gblk, denp, start=True, stop=True), reads=[cst2, smf], writes=[ps])
                S.op("dve", lambda e, ps=ps: e.reciprocal(out=tmp16, in_=ps[:, 0:16]), reads=[ps], writes=[smf])
                S.op("dve", lambda e: e.tensor_tensor(out=opart3, in0=opart3, in1=tmp16.unsqueeze(2).to_broadcast([128, 16, 64]), op=ALU.mult),
                     reads=[opart, smf], writes=[opart])
                for hp in range(8):
                    ps = P()
                    S.op("pe", lambda e, ps=ps: e.matmul(ps[:, 0:4], opart[:, hp * 128:(hp + 1) * 128], Gsel, start=True, stop=True), reads=[opart, cst2], writes=[ps])
                    S.op("dve", lambda e, ps=ps: e.tensor_copy(out=yattT[:, hp, cs], in_=ps[:, 0:4]), reads=[ps], writes=["yattT"])
            merge(T)
            mem_attn(T, [(slice(b * 4, b * 4 + 4), b) for b in range(4)])
            ffn("ffn2", T)
            S.dma("yout", lambda e: e.dma_start(out=ysT_o.rearrange("(k p) t -> p k t", p=128), in_=hT[:, :, 0:T]), reads=[hT], writes=["ysT_o"])

        SKIP = os.environ.get("KSKIP", "").split(",")
        if "kv" not in SKIP:
            memory_kv_prompt()
        for st in range(int(os.environ.get("KNST", NST))):
            t0 = st * ST
            S.dma("xin", lambda e, t0=t0: e.dma_start(out=hT[:], in_=xT[:, t0:t0 + ST].rearrange("(k p) t -> p k t", p=128)),
                  writes=[hT])
            ffn("ffn1", ST)
            if os.environ.get("KSTAGE", "all") != "ffn":
                mix_prompt(st)
            if os.environ.get("KSTAGE", "all") == "all":
                mem_attn(ST, [(slice(0, ST), None)])
            ffn("ffn2", ST)
            S.dma("yout", lambda e, t0=t0: e.dma_start(out=yT_o[:, t0:t0 + ST].rearrange("(k p) t -> p k t", p=128), in_=hT[:]),
                  reads=[hT], writes=["yT_o"])
        if "ssmo" not in SKIP:
          S.dma("ssmo", lambda e: e.dma_start(out=ssmT_o, in_=stT[:].rearrange("p g c -> p (g c)")), reads=[stT], writes=["ssmT_o"])
        if "convo" not in SKIP:
          S.dma("convo", lambda e: e.dma_start(out=convT_o.rearrange("(k p) j -> p k j", p=128), in_=hist[:]), reads=[hist], writes=["convT_o"])
        if "sample" not in SKIP:
            sample_path()
        S.emit()
    return nc


def _consts():
    s = np.arange(128)
    ident = np.eye(128, dtype=np.float32)
    U = (s[:, None] <= s[None, :]).astype(np.float32)
    negm = np.where(s[None, :] <= s[:, None], 0.0, -1e30).astype(np.float32)
    ones = np.ones((128, 128), np.float32)
    tri = U.copy()
    return np.concatenate([ident, U, negm, ones, tri], axis=1)


def _fm(v, n):
    return np.ascontiguousarray(np.asarray(v, np.float32).reshape(n, 128).T)


def kernel(**inp):
    inp = {k: np.asarray(v) for k, v in inp.items()}
    nc = build_nc()
    pp = np.zeros((128, PP_N), np.float32)
    for n, o in PP_G.items():
        pp[:, o:o + 8] = _fm(inp[n][0], 8)
    pp[:, PP_NG:PP_NG + 16] = _fm(inp["ssd_norm_g"][0], 16)
    pp[:, PP_CB:PP_CB + 32] = _fm(inp["conv_b"][0], 32)
    cw = inp["conv_w"][0]
    pp[:, PP_CW:PP_CW + 128] = cw.T.reshape(32, 128, 4).transpose(1, 0, 2).reshape(128, 128)
    pp[:, PP_DS:PP_DS + 16] = _fm(np.repeat(inp["d_skip"][0], 64), 16)
    rowp = np.zeros((128, 64), np.float32)
    rowp[:, 0:32] = inp["dt_bias"][0][None, :]
    rowp[:, 32:64] = inp["a_log"][0][None, :]
    cst = _consts()
    wnames = ["ffn1_wg", "ffn1_wu", "ffn1_wd", "w_in", "w_br_ssd", "w_br_att", "w_out", "w_mq", "w_mk", "w_mv", "w_mo",
              "ffn2_wg", "ffn2_wu", "ffn2_wd"]
    shared = {n: np.ascontiguousarray(inp[n][0]) for n in wnames}
    qcols = np.concatenate([np.arange(O_Q + h * 64, O_Q + (h + 1) * 64) for h in HPERM])
    w_in_l = shared["w_in"].copy()
    w_in_l[:, O_Q:O_Q + D] = shared["w_in"][:, qcols]
    shared["w_in"] = w_in_l
    shared["w_br_att"] = np.ascontiguousarray(shared["w_br_att"][qcols - O_Q, :])
    shared.update(pp=pp, rowp=rowp, cst=cst)
    p_ = np.arange(128)
    cst2 = np.zeros((128, C2N), np.float32)
    cst2[:, 0:128] = (p_[:, None] % 4 == p_[None, :] % 4)
    cst2[:, 128:132] = (p_[:, None] % 4 == np.arange(4)[None, :])
    cst2[:, 132:136] = np.where((p_[:, None] // 4 == 0) & (np.arange(4)[None, :] <= p_[:, None] % 4), 0.0, -1e30)
    cst2[:, 136] = p_ % 64
    cst2[0:4, 137:265] = (np.arange(4)[:, None] == p_[None, :] % 4)
    ht = np.arange(32)
    cst2[0:32, 265:269] = (ht[:, None] % 4 == np.arange(4)[None, :])
    cst2[0:8, 269:301] = (np.arange(8)[:, None] == ht[None, :] // 4)
    for tq in range(4):
        cst2[tq, 301 + tq * 128:301 + (tq + 1) * 128] = 1.0
    cst2[0:32, 813:941] = (ht[:, None] == p_[None, :] // 4)
    cst2[:, 941] = (p_ >= 64)
    shared["cst2"] = cst2
    have_pool = "cache_k" in inp
    if have_pool:
        shared["kidxT"] = np.ascontiguousarray(inp["cache_kidx"][0].transpose(0, 2, 1)).reshape(5120 * 64, 128)
        shared["poolk"] = inp["cache_k"][0].reshape(5120 * 128, 256)
        shared["poolv"] = inp["cache_v"][0].reshape(5120 * 128, 256)
    in_maps = []
    NCR = int(os.environ.get("KCORES", 8))
    for c in range(NCR):
        m = dict(shared)
        m["xT"] = np.ascontiguousarray(inp["x_prompt"][c].T)
        m["xsT"] = np.ascontiguousarray(inp["x_sample"][4 * c:4 * c + 4].reshape(16, D).T)
        m["memT"] = np.ascontiguousarray(inp["mem_prompt"][c].T)
        sc_ = inp["state_conv"][0, 4 * c:4 * c + 4]
        m["hists"] = np.ascontiguousarray(sc_.transpose(2, 0, 1).reshape(32, 128, 4, 3).transpose(1, 0, 2, 3)).reshape(128, 384)
        m["ssmsT"] = np.ascontiguousarray(inp["state_ssm"][0, 4 * c:4 * c + 4].reshape(4, 2048, 128).transpose(0, 2, 1))
        m["cmkT"] = np.ascontiguousarray(inp["cache_mem_k"][0, 4 * c:4 * c + 4].reshape(4, 256, D).transpose(0, 2, 1))
        m["cmv"] = np.ascontiguousarray(inp["cache_mem_v"][0, 4 * c:4 * c + 4].reshape(4, 256, D))
        m["ptab"] = np.ascontiguousarray(inp["page_table"][4 * c:4 * c + 4].astype(np.int32))
        in_maps.append(m)
    res = run_bass_kernel_spmd(nc, in_maps, core_ids=list(range(NCR)))
    if os.environ.get("KTIME"):
        print("EXEC_TIME_NS", res.exec_time_ns)
    R = res.results
    y_p = np.stack([R[c]["yT"].T for c in range(NCR)])
    y_s = np.concatenate([R[c]["ysT"].T.reshape(4, 4, D) for c in range(NCR)])
    nk_p = np.stack([R[c]["kT"].T.reshape(SEQ, 4, 64) for c in range(NCR)])[None]
    nv_p = np.stack([R[c]["v_o"].reshape(SEQ, 4, 64) for c in range(NCR)])[None]
    nki_p = np.stack([R[c]["kiT"].T for c in range(NCR)])[None]
    nssm_p = np.stack([R[c]["ssmT"].T.reshape(32, 64, 128) for c in range(NCR)])[None]
    nconv_p = np.stack([R[c]["convT"].T for c in range(NCR)])[None]
    nmk_p = np.stack([R[c]["mkT"].T.reshape(256, 4, 256) for c in range(NCR)])[None]
    nmv_p = np.stack([R[c]["mv_o"].reshape(256, 4, 256) for c in range(NCR)])[None]
    nk_s = np.concatenate([R[c]["ks_o"].reshape(4, 4, 4, 64) for c in range(NCR)])[None]
    nv_s = np.concatenate([R[c]["vs_o"].reshape(4, 4, 4, 64) for c in range(NCR)])[None]
    nki_s = np.concatenate([R[c]["kisT"].T.reshape(4, 4, 64) for c in range(NCR)])[None]
    nssm_s = np.concatenate([R[c]["ssms"].transpose(0, 2, 1).reshape(4, 32, 64, 128) for c in range(NCR)])[None]
    nconv_s = np.concatenate([R[c]["convs"].reshape(128, 32, 4, 3).transpose(2, 3, 1, 0).reshape(4, 3, 4096) for c in range(NCR)])[None]
    return (y_p, y_s, nk_p, nv_p, nki_p, nssm_p, nconv_p, nmk_p, nmv_p, nk_s, nv_s, nki_s, nssm_s, nconv_s)
```

```python
import os
import numpy as np
from contextlib import ExitStack
import concourse.bass as bass
import concourse.mybir as mybir
from concourse.bass_utils import run_bass_kernel_spmd

F32 = mybir.dt.float32
BF16 = mybir.dt.bfloat16
I32 = mybir.dt.int32
U32 = mybir.dt.uint32
AF = mybir.ActivationFunctionType
ALU = mybir.AluOpType
AX = mybir.AxisListType

D = 1024
KC = 8
SEQ = 2048
ST = 512
NST = SEQ // ST
FH = 2816
FHC = 22
IN_DIM = 10344
O_Z, O_X, O_B, O_C, O_DT, O_Q, O_K, O_V, O_QI, O_KI, O_WI, O_GS, O_GA = (
    0, 2048, 4096, 5120, 6144, 6176, 7200, 7456, 7712, 8224, 8288, 8296, 9320)
EPS = 1e-6
NIT = 12
HPERM = []
for _j in range(8):
    HPERM += [(_j // 4) * 8 + _j % 4, (_j // 4) * 8 + _j % 4 + 4]
WBE = 4096

PP_G = {n: i * 8 for i, n in enumerate(
    ["ffn1_pre_g", "ffn1_post_g", "mix_pre_g", "mix_post_g", "mem_pre_g", "mem_kv_g", "mem_post_g", "ffn2_pre_g",
     "ffn2_post_g"])}
PP_NG = 72
PP_CB = 88
PP_CW = 120
PP_DS = 248
PP_N = 264
C2N = 942
NITS = 19
PGP = [0, 1, 2, 3]


class _Rec:
    def __getattr__(self, name):
        def f(*a, **kw):
            self.call = (name, a, kw)
            return self
        return f


def _freeze(fn):
    r = _Rec()
    fn(r)
    name, a, kw = r.call
    return lambda e: getattr(e, name)(*a, **kw)


class Sched:
    ENG = ("pe", "act", "dve", "pool", "sp")

    def __init__(self, nc, es):
        self.nc = nc
        self.es = es
        self.ops = {e: [] for e in self.ENG}
        self.cnt = {e: 0 for e in self.ENG}
        self.sem = {e: es.enter_context(nc.semaphore("s_" + e)) for e in self.ENG}
        self.waited = {e: {} for e in self.ENG}
        self.lastw = {}
        self.readers = {}
        self.chan = {}
        self.nt = 0

    def sb(self, shape, dtype, name=None):
        self.nt += 1
        return self.es.enter_context(self.nc.sbuf_tensor("sb_" + (name or f"t{self.nt}"), list(shape), dtype))

    def ps(self, shape, dtype, name=None):
        self.nt += 1
        return self.es.enter_context(self.nc.psum_tensor("ps_" + (name or f"p{self.nt}"), list(shape), dtype))

    def _key(self, k):
        if isinstance(k, (str, tuple)):
            return k
        return k.name

    def alias(self, newk, oldks):
        newk = self._key(newk)
        lst = self.readers.setdefault(newk, [])
        for o in oldks:
            o = self._key(o)
            lst.extend(self.readers.get(o, []))
            if o in self.lastw:
                lst.append(self.lastw[o])

    def _collect(self, eng, reads, writes):
        deps = []
        for k in reads:
            k = self._key(k)
            w = self.lastw.get(k)
            if w is not None:
                deps.append(("raw", w))
            if isinstance(k, str) and k.startswith("ps_"):
                for r in self.readers.get(k, ()):
                    if r[0] != eng:
                        deps.append(("rar", r))
        for k in writes:
            k = self._key(k)
            w = self.lastw.get(k)
            if w is not None:
                deps.append(("waw", w))
            for r in self.readers.get(k, ()):
                deps.append(("war", r))
        waits = []
        for kind, (skey, val) in deps:
            if skey == eng:
                if eng == "pe" or kind == "war":
                    continue
            if self.waited[eng].get(skey, 0) >= val:
                continue
            self.waited[eng][skey] = val
            waits.append((skey, val))
        return waits

    def _commit(self, tok, reads, writes):
        for k in writes:
            k = self._key(k)
            self.lastw[k] = tok
            self.readers[k] = []
        for k in reads:
            self.readers.setdefault(self._key(k), []).append(tok)

    def op(self, eng, fn, reads=(), writes=()):
        waits = self._collect(eng, reads, writes)
        self.cnt[eng] += 1
        self.ops[eng].append((waits, _freeze(fn), eng))
        self._commit((eng, self.cnt[eng]), reads, writes)

    def dma(self, chan, fn, reads=(), writes=(), queue="sp"):
        waits = self._collect(queue, reads, writes)
        if chan not in self.chan:
            self.chan[chan] = [self.es.enter_context(self.nc.semaphore("c_" + str(len(self.chan)))), 0]
        ch = self.chan[chan]
        ch[1] += 16
        self.ops[queue].append((waits, _freeze(fn), ("ch", chan)))
        self._commit((("ch", chan), ch[1]), reads, writes)

    def _semof(self, skey):
        return self.chan[skey[1]][0] if isinstance(skey, tuple) else self.sem[skey]

    def emit(self):
        fin = [(("ch", c), v) for c, (s, v) in self.chan.items()]
        fin += [(e, self.cnt[e]) for e in self.ENG if e != "sp" and self.cnt[e] > 0]

        def run(engname, e):
            for waits, fn, inc in self.ops[engname]:
                for skey, val in waits:
                    e.wait_ge(self._semof(skey), val)
                inst = fn(e)
                if isinstance(inc, tuple):
                    inst.then_inc(self.chan[inc[1]][0], 16)
                else:
                    inst.then_inc(self.sem[inc], 1)
            if engname == "sp":
                for skey, val in fin:
                    e.wait_ge(self._semof(skey), val)

        with self.nc.Block() as block:
            @block.sync
            def _(e):
                run("sp", e)

            @block.scalar
            def _(e):
                run("act", e)

            @block.vector
            def _(e):
                run("dve", e)

            @block.gpsimd
            def _(e):
                run("pool", e)

            @block.tensor
            def _(e):
                run("pe", e)


class Ctx:
    pass


def build_nc(debug=None):
    nc = bass.Bass("TRN2", target_bir_lowering=False, dynamic_dma_scratch_size=8192)
    dt_in = {}

    def din(name, shape, dt=F32):
        dt_in[name] = nc.dram_tensor(name, list(shape), dt, kind="ExternalInput").ap()
        return dt_in[name]

    def dout(name, shape, dt=F32):
        return nc.dram_tensor(name, list(shape), dt, kind="ExternalOutput").ap()

    xT = din("xT", [D, SEQ])
    xsT = din("xsT", [D, 16])
    memT = din("memT", [D, 256])
    pp_d = din("pp", [128, PP_N])
    rowp_d = din("rowp", [128, 64])
    cst_d = din("cst", [128, 5 * 128])
    W = {}
    for n, shp in [("ffn1_wg", [D, FH]), ("ffn1_wu", [D, FH]), ("ffn1_wd", [FH, D]), ("w_in", [D, IN_DIM]),
                   ("w_br_ssd", [2048, D]), ("w_br_att", [D, D]), ("w_out", [D, D]), ("w_mq", [D, D]),
                   ("w_mk", [D, D]), ("w_mv", [D, D]), ("w_mo", [D, D]),
                   ("ffn2_wg", [D, FH]), ("ffn2_wu", [D, FH]), ("ffn2_wd", [FH, D])]:
        W[n] = din(n, shp)

    yT_o = dout("yT", [D, SEQ])
    ysT_o = dout("ysT", [D, 16])
    kT_o = dout("kT", [256, SEQ])
    v_o = dout("v_o", [SEQ, 256])
    kiT_o = dout("kiT", [64, SEQ])
    ssmT_o = dout("ssmT", [128, 2048])
    convT_o = dout("convT", [4096, 3])
    mkT_o = dout("mkT", [D, 256])
    mv_o = dout("mv_o", [256, D])
    hists_d = din("hists", [128, 384])
    ssmsT_d = din("ssmsT", [4, 128, 2048])
    cmkT_d = din("cmkT", [4, D, 256])
    cmv_d = din("cmv", [4, 256, D])
    ptab_d = din("ptab", [4, 128], I32)
    cst2_d = din("cst2", [128, C2N])
    if "sample" not in os.environ.get("KSKIP", "").split(","):
        kidxT_d = din("kidxT", [5120 * 64, 128])
        poolk_d = din("poolk", [5120 * 128, 256])
        poolv_d = din("poolv", [5120 * 128, 256])
    ks_o = dout("ks_o", [16, 256])
    vs_o = dout("vs_o", [16, 256])
    kisT_o = dout("kisT", [64, 16])
    ssms_o = dout("ssms", [4, 128, 2048])
    convs_o = dout("convs", [128, 384])
    dbg_o = None

    with ExitStack() as es:
        S = Sched(nc, es)
        c = Ctx()
        pp = S.sb([128, PP_N], F32, "pp")
        rowp = S.sb([128, 64], F32, "rowp")
        cst = S.sb([128, 640], F32, "cst")
        cstb = S.sb([128, 640], BF16, "cstb")
        S.dma("pp", lambda e: e.dma_start(out=pp[:], in_=pp_d), writes=[pp])
        S.dma("rowp", lambda e: e.dma_start(out=rowp[:], in_=rowp_d), writes=[rowp])
        S.dma("cst", lambda e: e.dma_start(out=cst[:], in_=cst_d), writes=[cst])
        S.dma("cstb", lambda e: e.dma_start(out=cstb[:], in_=cst_d), writes=[cstb], queue="pool")
        ident = cst[:, 0:128]
        Uf = cst[:, 128:256]
        negm = cst[:, 256:384]
        identb = cstb[:, 0:128]
        onesb = cstb[:, 384:512]
        trib = cstb[:, 512:640]
        epsT = S.sb([128, 1], F32, "epsT")
        S.op("dve", lambda e: e.memset(epsT[:], EPS), writes=[epsT])
        arow = S.sb([128, 32], F32, "arow")
        S.op("act", lambda e: e.activation(out=arow[:], in_=rowp[:, 32:64], func=AF.Exp), reads=[rowp], writes=[arow])
        S.op("dve", lambda e: e.tensor_scalar(out=arow[:], in0=arow[:], scalar1=-1.0, scalar2=None, op0=ALU.mult),
             reads=[arow], writes=[arow])

        gen = [S.ps([128, 512], F32, f"pg{i}") for i in range(5)]
        accA = S.ps([128, 512], F32, "accA")
        accB = S.ps([128, 512], F32, "accB")
        ptb = S.ps([128, 1024], BF16, "ptb")
        c.gi = 0
        c.ti = 0

        def P():
            c.gi = (c.gi + 1) % len(gen)
            return gen[c.gi]

        def PTS(i):
            return ptb[:, i * 128:(i + 1) * 128]

        NWB = 3
        wbufs = [S.sb([128, WBE], BF16, f"wb{i}") for i in range(NWB)]
        c.wi = 0

        def wload(wd, r0, nk, c0, ncols):
            assert nk * ncols <= WBE
            c.wi = (c.wi + 1) % NWB
            buf = wbufs[c.wi]
            view = buf[:, 0:nk * ncols].rearrange("p (k c) -> p k c", k=nk)
            src = wd[r0:r0 + nk * 128, c0:c0 + ncols].rearrange("(k p) c -> p k c", p=128)
            S.dma(("w", c.wi), lambda e: e.dma_start(out=view, in_=src), writes=[buf], queue="pool")
            return buf, view

        hT = S.sb([128, KC, ST], F32, "hT")
        uT = S.sb([128, KC, ST], BF16, "uT")
        scrM = S.sb([128, 4096], F32, "scrM")
        ybuf = scrM[:].rearrange("p (k t) -> p k t", k=KC)
        SCRK = ["ssdtmp", "maskT", "zs", ("cv", 0), ("cv", 1), ("cv", 2), ("cv", 3)]
        sqb = S.sb([128, ST], BF16, "sqb")
        sqb2 = S.sb([128, ST], BF16, "sqb2")
        rstd = S.sb([128, ST], F32, "rstd")
        tmpf = S.sb([128, ST], F32, "tmpf")
        big = S.sb([128, 22528], BF16, "big")
        hid = big[:, 0:FHC * ST].rearrange("p (k t) -> p k t", k=FHC)
        sgs = [S.sb([128, ST], BF16, f"sg{i}") for i in range(2)]

        def gcol(name, k):
            return pp[:, PP_G[name] + k:PP_G[name] + k + 1]

        def norm_stats(src, srck, T, scale_div):
            ps = P()
            n = len(src)
            for k in range(n):
                sq = sqb if k % 2 == 0 else sqb2
                S.op("act", lambda e, k=k, sq=sq: e.activation(out=sq[:, :T], in_=src[k], func=AF.Square),
                     reads=[srck], writes=[sq])
                S.op("pe", lambda e, k=k, sq=sq: e.matmul(ps[:, :T], onesb, sq[:, :T], start=(k == 0), stop=(k == n - 1)),
                     reads=[sq, cstb], writes=[ps])
            S.op("act", lambda e: e.activation(out=rstd[:, :T], in_=ps[:, :T], func=AF.Sqrt, bias=epsT[:, 0:1],
                                               scale=1.0 / scale_div), reads=[ps, epsT], writes=[rstd])
            S.op("dve", lambda e: e.reciprocal(out=rstd[:, :T], in_=rstd[:, :T]), reads=[rstd], writes=[rstd])

        def prenorm(gname, T, src=None, srck=None, dst=None):
            src = src if src is not None else [hT[:, k, :T] for k in range(KC)]
            srck = srck if srck is not None else hT
            dst = dst if dst is not None else uT
            norm_stats(src, srck, T, float(D))
            for k in range(KC):
                S.op("dve", lambda e, k=k: e.scalar_tensor_tensor(out=dst[:, k, :T], in0=src[k], scalar=gcol(gname, k),
                                                                   in1=rstd[:, :T], op0=ALU.mult, op1=ALU.mult),
                     reads=[srck, rstd, pp], writes=[dst])

        def postnorm_add(gname, T, coef):
            norm_stats([ybuf[:, k, :T] for k in range(KC)], "ybuf", T, float(D))
            for k in range(KC):
                S.op("dve", lambda e, k=k: e.scalar_tensor_tensor(out=tmpf[:, :T], in0=ybuf[:, k, :T], scalar=gcol(gname, k),
                                                                   in1=rstd[:, :T], op0=ALU.mult, op1=ALU.mult),
                     reads=["ybuf", rstd, pp], writes=[tmpf])
                S.op("dve", lambda e, k=k: e.scalar_tensor_tensor(out=hT[:, k, :T], in0=tmpf[:, :T], scalar=coef,
                                                                   in1=hT[:, k, :T], op0=ALU.mult, op1=ALU.add),
                     reads=[tmpf, hT], writes=[hT])

        def proj_fm(wd, r0, nk, c0, ncols, rhs_fn, rhs_keys, T, consumer, blk=512, msz=128):
            blk = min(blk, (WBE // nk) // msz * msz)
            idx = 0
            for b0 in range(0, ncols, blk):
                bc = min(blk, ncols - b0)
                buf, view = wload(wd, r0, nk, c0 + b0, bc)
                for m0 in range(0, bc, msz):
                    ms = min(msz, bc - m0)
                    ps = P()
                    for k in range(nk):
                        S.op("pe", lambda e, k=k, m0=m0, ms=ms, ps=ps, view=view: e.matmul(
                            ps[0:ms, :T], view[:, k, m0:m0 + ms], rhs_fn(k), start=(k == 0), stop=(k == nk - 1)),
                            reads=[buf] + rhs_keys, writes=[ps])
                    consumer(idx, ps, ms)
                    idx += 1

        def ffn(pref, T):
            prenorm(pref + "_pre_g", T)
            S.alias("hid", MIXK + ["acc"])
            S.alias("ybuf", SCRK)
            for b0 in range(0, FH, 512):
                bc_ = min(512, FH - b0)
                bufg, vg = wload(W[pref + "_wg"], 0, KC, b0, bc_)
                bufu, vu = wload(W[pref + "_wu"], 0, KC, b0, bc_)
                for m in range(bc_ // 128):
                    hc = b0 // 128 + m
                    pg, pu = P(), P()
                    for k in range(KC):
                        S.op("pe", lambda e, k=k, m=m, pg=pg, vg=vg: e.matmul(pg[:, :T], vg[:, k, m * 128:(m + 1) * 128],
                                                                             uT[:, k, :T], start=(k == 0), stop=(k == KC - 1)),
                             reads=[bufg, uT], writes=[pg])
                    for k in range(KC):
                        S.op("pe", lambda e, k=k, m=m, pu=pu, vu=vu: e.matmul(pu[:, :T], vu[:, k, m * 128:(m + 1) * 128],
                                                                             uT[:, k, :T], start=(k == 0), stop=(k == KC - 1)),
                             reads=[bufu, uT], writes=[pu])
                    sg = sgs[hc % 2]
                    S.op("act", lambda e, pg=pg, sg=sg: e.activation(out=sg[:, :T], in_=pg[:, :T], func=AF.Silu),
                         reads=[pg], writes=[sg])
                    S.op("dve", lambda e, pu=pu, sg=sg, hc=hc: e.tensor_tensor(out=hid[:, hc, :T], in0=sg[:, :T], in1=pu[:, :T],
                                                                              op=ALU.mult), reads=[pu, sg], writes=["hid"])

            def cons(idx, ps, ms):
                S.op("act", lambda e: e.activation(out=ybuf[:, idx, :T], in_=ps[:, :T], func=AF.Copy), reads=[ps],
                     writes=["ybuf"])
            proj_fm(W[pref + "_wd"], 0, FHC, 0, D, lambda k: hid[:, k, :T], ["hid"], T, cons, blk=128)
            postnorm_add(pref + "_post_g", T, 0.5)


        MIXK = ["qT", "qiT", "yssdT", "yattT", "mergedT"]
        qT = big[:, 0:4096].rearrange("p (k t) -> p k t", k=8)
        qiT = big[:, 4096:6144].rearrange("p (k t) -> p k t", k=4)
        yssdT = big[:, 6144:14336].rearrange("p (k t) -> p k t", k=16)
        yattT = big[:, 14336:18432].rearrange("p (k t) -> p k t", k=8)
        mergedT = big[:, 18432:22528].rearrange("p (k t) -> p k t", k=8)
        kT2 = S.sb([128, 2, SEQ], BF16, "kT2")
        kiT2 = S.sb([128, SEQ], BF16, "kiT2")
        vtok = S.sb([128, 16, 256], BF16, "vtok")
        stT = S.sb([128, 8, 256], F32, "stT")
        stTb = S.sb([128, 8, 256], BF16, "stTb")
        hist = S.sb([128, 32, 3], F32, "hist")
        mkTb = S.sb([128, 8, 256], BF16, "mkTb")
        mvb = S.sb([128, 2, D], BF16, "mvb")
        for t_ in (stT, stTb, hist):
            S.op("dve", lambda e, t_=t_: e.memset(t_[:], 0.0), writes=[t_])
        maskT = scrM[:].bitcast(BF16).rearrange("p (k t) -> p k t", k=16)
        pre = scrM[:, 0:2060].rearrange("p (j t) -> p j t", j=4)
        cv = scrM[:, 2064:3088].bitcast(BF16).rearrange("p (j t) -> p j t", j=4)
        zs = scrM[:, 3088:3600].bitcast(BF16).rearrange("p (j t) -> p j t", j=2)
        Yg = S.sb([128, 2, ST], F32, "Yg")
        acc = big[:, 18432:22528].bitcast(F32)
        mask01t = S.sb([128, 2048], BF16, "mask01t")
        mask01 = mask01t[:]
        junk = mask01t[:]
        stg = [S.sb([128, 512], F32, f"stg{i}") for i in range(1)]
        c.si = 0
        Es = [S.sb([128, ST], BF16, f"E{i}") for i in range(3)]
        c.ei = 0
        rrs = [S.sb([128, 512], F32, f"rr{i}") for i in range(2)]
        rden = S.sb([128, ST], F32, "rden")
        witok = S.sb([128, 4, 8], F32, "witok")
        dtt = S.sb([128, 4, 32], F32, "dtt")
        dta = S.sb([128, 4, 32], F32, "dta")
        acsc = S.sb([128, 4, 32], F32, "acsc")
        arw = S.sb([128, 512], F32, "arw")
        erow = S.sb([128, 512], F32, "erow")
        Cdec = S.sb([128, 512], BF16, "Cdec")
        xs_tok = S.sb([128, 256], BF16, "xs_tok")
        B_tok = S.sb([128, 128], BF16, "B_tok")
        cbm = S.sb([128, 128], BF16, "cbm")
        argt = [S.sb([128, 128], F32, f"arg{i}") for i in range(2)]
        Ldt = [S.sb([128, 128], BF16, f"Ld{i}") for i in range(2)]
        MTt = [S.sb([128, 128], BF16, f"MT{i}") for i in range(2)]
        xdt = S.sb([128, 256], BF16, "xdt")
        xdtw = S.sb([128, 256], BF16, "xdtw")
        sm4 = S.sb([128, 16], F32, "sm4")
        bis = S.sb([128, 8], F32, "bis")

        def stage_out(src_ps, rows, cols, dst_ap, dkey):
            c.si = 0
            sg_ = stg[c.si]
            S.op("act", lambda e: e.activation(out=sg_[0:rows, 0:cols], in_=src_ps, func=AF.Copy), reads=[dkey[0]], writes=[sg_])
            S.dma(("stg", c.si), lambda e: e.dma_start(out=dst_ap, in_=sg_[0:rows, 0:cols]), reads=[sg_], writes=[dkey[1]])

        def copy_alt(i, out, in_, reads, writes):
            if i % 2 == 0:
                S.op("act", lambda e: e.activation(out=out, in_=in_, func=AF.Copy), reads=reads, writes=writes)
            else:
                S.op("dve", lambda e: e.tensor_copy(out=out, in_=in_), reads=reads, writes=writes)

        def tm_proj(view, buf, c0, n, tt, ps):
            for k in range(KC):
                S.op("pe", lambda e, k=k: e.matmul(ps[:, 0:n], uT[:, k, tt * 128:(tt + 1) * 128], view[:, k, c0:c0 + n],
                                                   start=(k == 0), stop=(k == KC - 1)), reads=[buf, uT], writes=[ps])

        def fm64(view, buf, c0, ps, T):
            for half in range(2):
                for k in range(KC):
                    S.op("pe", lambda e, k=k, half=half: e.matmul(ps[half * 64:(half + 1) * 64, :T], view[:, k, c0:c0 + 64],
                                                                  uT[:, k, :T], start=(k == 0), stop=(k == KC - 1)),
                         reads=[buf, uT], writes=[ps])

        def mix_prompt(st):
            t0 = st * ST
            for k_ in MIXK:
                S.alias(k_, ["hid"])
            for k_ in SCRK:
                S.alias(k_, ["ybuf"])
            S.alias("ssdtmp", ["maskT"])
            prenorm("mix_pre_g", ST)
            win = W["w_in"]
            KM = os.environ.get("KMIX", "k,v,ki,wi,dt,q").split(",")
            buf, view = wload(win, 0, KC, O_K, 512)
            for kc in range(2 if "k" in KM else 0):
                ps = P()
                for k in range(KC):
                    S.op("pe", lambda e, k=k, kc=kc, ps=ps: e.matmul(ps[:, :ST], view[:, k, kc * 128:(kc + 1) * 128], uT[:, k, :ST],
                                                                     start=(k == 0), stop=(k == KC - 1)), reads=[buf, uT], writes=[ps])
                KK = os.environ.get("KK", "copy,stage").split(",")
                if "copy" in KK:
                    S.op("dve", lambda e, kc=kc, ps=ps: e.tensor_copy(out=kT2[:, kc, t0:t0 + ST], in_=ps[:, :ST]), reads=[ps], writes=[kT2])
                if "stage" in KK:
                    stage_out(ps[:, :ST], 128, ST, kT_o[kc * 128:(kc + 1) * 128, t0:t0 + ST], (ps, "kT_o"))
            for tt in range(4 if "v" in KM else 0):
                ps = P()
                tm_proj(view, buf, 256, 256, tt, ps)
                S.op("dve", lambda e, tt=tt, ps=ps: e.tensor_copy(out=vtok[:, st * 4 + tt, :], in_=ps[:, 0:256]), reads=[ps], writes=[vtok])
                stage_out(ps[:, 0:256], 128, 256, v_o[t0 + tt * 128:t0 + (tt + 1) * 128, :], (ps, "v_o"))
            buf, view = wload(win, 0, KC, O_KI - 32, 128)
            ps = P()
            if "ki" in KM:
                fm64(view, buf, 32, ps, ST)
                S.op("dve", lambda e, ps=ps: e.tensor_copy(out=kiT2[:, t0:t0 + ST], in_=ps[:, :ST]), reads=[ps], writes=[kiT2])
                stage_out(ps[0:64, :ST], 64, ST, kiT_o[:, t0:t0 + ST], (ps, "kiT_o"))
            for tt in range(4 if "wi" in KM else 0):
                ps = P()
                tm_proj(view, buf, 96, 8, tt, ps)
                S.op("dve", lambda e, tt=tt, ps=ps: e.tensor_copy(out=witok[:, tt, :], in_=ps[:, 0:8]), reads=[ps], writes=[witok])
            buf, view = wload(win, 0, KC, O_DT, 128)
            if "dt" not in KM:
                return
            for tt in range(4):
                ps = P()
                tm_proj(view, buf, 0, 32, tt, ps)
                S.op("dve", lambda e, tt=tt, ps=ps: e.tensor_tensor(out=dtt[:, tt, :], in0=ps[:, 0:32], in1=rowp[:, 0:32], op=ALU.add),
                     reads=[ps, rowp], writes=[dtt])
            S.op("act", lambda e: e.activation(out=dtt[:], in_=dtt[:], func=AF.Exp), reads=[dtt], writes=[dtt])
            S.op("act", lambda e: e.activation(out=dtt[:], in_=dtt[:], func=AF.Ln, bias=1.0), reads=[dtt], writes=[dtt])
            for tt in range(4):
                S.op("dve", lambda e, tt=tt: e.tensor_tensor(out=dta[:, tt, :], in0=dtt[:, tt, :], in1=arow[:], op=ALU.mult),
                     reads=[dtt, arow], writes=[dta])
                ps = P()
                S.op("pe", lambda e, tt=tt, ps=ps: e.matmul(ps[:, 0:32], Uf, dta[:, tt, :], start=True, stop=True), reads=[cst, dta], writes=[ps])
                S.op("act", lambda e, tt=tt, ps=ps: e.activation(out=acsc[:, tt, :], in_=ps[:, 0:32], func=AF.Copy), reads=[ps], writes=[acsc])
            proj_fm(win, 0, KC, O_Q, D, lambda k: uT[:, k, :ST], [uT], ST,
                    lambda idx, ps, ms: copy_alt(idx, qT[:, idx, :], ps[:, :ST], [ps], ["qT"]))
            proj_fm(win, 0, KC, O_QI, 512, lambda k: uT[:, k, :ST], [uT], ST,
                    lambda idx, ps, ms: copy_alt(idx, qiT[:, idx, :], ps[:, :ST], [ps], ["qiT"]))
            STG = os.environ.get("KSTAGE", "all")
            if STG == "mixA":
                return
            for g in range(8):
                ssd_group(st, g)
            if STG == "ssd":
                return
            S.alias("maskT", ["ssdtmp"])
            dsa_prompt(st)
            if STG == "dsa":
                return
            merge(ST)

        def conv_chunk(j, ch, T):
            cw = lambda tap: pp[:, PP_CW + ch * 4 + tap:PP_CW + ch * 4 + tap + 1]
            S.op("dve", lambda e: e.tensor_scalar(out=tmpf[:, :T], in0=pre[:, j, 0:T], scalar1=cw(0), scalar2=pp[:, PP_CB + ch:PP_CB + ch + 1],
                                                  op0=ALU.mult, op1=ALU.add), reads=["ssdtmp", pp], writes=[tmpf])
            for tap in range(1, 4):
                S.op("dve", lambda e, tap=tap: e.scalar_tensor_tensor(out=tmpf[:, :T], in0=pre[:, j, tap:tap + T], scalar=cw(tap),
                                                                      in1=tmpf[:, :T], op0=ALU.mult, op1=ALU.add),
                     reads=["ssdtmp", pp, tmpf], writes=[tmpf])
            S.op("act", lambda e: e.activation(out=cv[:, j, :T], in_=tmpf[:, :T], func=AF.Silu), reads=[tmpf], writes=[("cv", j)])

        def ssd_group(st, g):
            win = W["w_in"]
            T = ST
            buf, view = wload(win, 0, KC, O_Z + g * 256, 256)
            for m in range(2):
                ps = P()
                for k in range(KC):
                    S.op("pe", lambda e, k=k, m=m, ps=ps: e.matmul(ps[:, :T], view[:, k, m * 128:(m + 1) * 128], uT[:, k, :T],
                                                                   start=(k == 0), stop=(k == KC - 1)), reads=[buf, uT], writes=[ps])
                S.op("act", lambda e, m=m, ps=ps: e.activation(out=zs[:, m, :T], in_=ps[:, :T], func=AF.Silu), reads=[ps], writes=["zs"])
            chs = [g * 2, g * 2 + 1, 16 + g, 24 + g]
            srcs = [(O_X + g * 256, 0), (O_X + g * 256, 128), (O_B + g * 128, 0), (O_C + g * 128, 0)]
            bufx, viewx = wload(win, 0, KC, O_X + g * 256, 256)
            bufb, viewb = wload(win, 0, KC, O_B + g * 128, 128)
            views = [(bufx, viewx, 0), (bufx, viewx, 128), (bufb, viewb, 0), None]
            for j in range(4):
                if j == 3:
                    bufc, viewc = wload(win, 0, KC, O_C + g * 128, 128)
                    views[3] = (bufc, viewc, 0)
                bf_, vw_, c0 = views[j]
                ps = P()
                for k in range(KC):
                    S.op("pe", lambda e, k=k, ps=ps, vw_=vw_, c0=c0: e.matmul(ps[:, :T], vw_[:, k, c0:c0 + 128], uT[:, k, :T],
                                                                              start=(k == 0), stop=(k == KC - 1)), reads=[bf_, uT], writes=[ps])
                ch = chs[j]
                S.op("dve", lambda e, j=j, ch=ch: e.tensor_copy(out=pre[:, j, 0:3], in_=hist[:, ch, :]), reads=[hist], writes=["ssdtmp"])
                S.op("act", lambda e, j=j, ps=ps: e.activation(out=pre[:, j, 3:3 + T], in_=ps[:, :T], func=AF.Copy), reads=[ps], writes=["ssdtmp"])
                S.op("dve", lambda e, j=j, ch=ch: e.tensor_copy(out=hist[:, ch, :], in_=pre[:, j, T:T + 3]), reads=["ssdtmp"], writes=[hist])
                conv_chunk(j, ch, T)
            for cc in range(4):
                ssd_chunk(g, cc, 128, cc * 128)
            for m in range(2):
                S.op("dve", lambda e, m=m: e.tensor_tensor(out=Yg[:, m, :], in0=Yg[:, m, :], in1=zs[:, m, :], op=ALU.mult),
                     reads=[Yg, "zs"], writes=[Yg])
            norm_stats([Yg[:, 0, :], Yg[:, 1, :]], Yg, T, 256.0)
            for m in range(2):
                S.op("dve", lambda e, m=m: e.scalar_tensor_tensor(out=yssdT[:, g * 2 + m, :], in0=Yg[:, m, :],
                                                                   scalar=pp[:, PP_NG + g * 2 + m:PP_NG + g * 2 + m + 1], in1=rstd[:, :T],
                                                                   op0=ALU.mult, op1=ALU.mult), reads=[Yg, rstd, pp], writes=["yssdT"])

        def ssd_chunk(g, cc, L, col0, sf=None, sbf=None, skeys=None, sfa=None, sba=None):
            cs = slice(col0, col0 + L)
            if sf is None:
                sf = lambda hh: stT[:, g, hh * 64:(hh + 1) * 64]
                sbf = lambda hh: stTb[:, g, hh * 64:(hh + 1) * 64]
                skeys = (stT, stTb)
                sfa, sba = stT[:, g, :], stTb[:, g, :]
            for m in range(3):
                S.op("pe", lambda e, m=m: e.transpose(PTS(m)[0:L, :], cv[:, m, cs], identb), reads=[("cv", m), cstb], writes=[ptb])
            S.op("act", lambda e: e.activation(out=xs_tok[0:L, :], in_=ptb[0:L, 0:256], func=AF.Copy), reads=[ptb], writes=[xs_tok])
            S.op("act", lambda e: e.activation(out=B_tok[0:L, :], in_=ptb[0:L, 256:384], func=AF.Copy), reads=[ptb], writes=[B_tok])
            ps = P()
            S.op("pe", lambda e, ps=ps: e.matmul(ps[0:L, 0:L], cv[:, 2, cs], cv[:, 3, cs], start=True, stop=True),
                 reads=[("cv", 2), ("cv", 3)], writes=[ps])
            S.op("dve", lambda e, ps=ps: e.tensor_tensor(out=cbm[0:L, 0:L], in0=ps[0:L, 0:L], in1=trib[0:L, 0:L], op=ALU.mult),
                 reads=[ps, cstb], writes=[cbm])
            psr = P()
            for hh in range(4):
                h = g * 4 + hh
                S.op("pe", lambda e, hh=hh, h=h: e.matmul(psr[:, hh * 128:hh * 128 + L], dta[0:L, cc, h:h + 1].to_broadcast([L, 128]),
                                                          Uf[0:L, 0:L], start=True, stop=True), reads=[dta, cst], writes=[psr])
            arw3 = arw[:].rearrange("p (a b) -> p a b", a=4)
            erow3 = erow[:].rearrange("p (a b) -> p a b", a=4)
            psr3 = psr[:].rearrange("p (a b) -> p a b", a=4)
            S.op("act", lambda e: e.activation(out=arw3[:, :, 0:L], in_=psr3[:, :, 0:L], func=AF.Copy), reads=[psr], writes=[arw])
            S.op("act", lambda e: e.activation(out=erow3[:, :, 0:L], in_=arw3[:, :, 0:L], func=AF.Exp), reads=[arw], writes=[erow])
            S.op("dve", lambda e: e.tensor_tensor(out=sm4[0:L, 0:4], in0=arw3[0:L, :, L - 1], in1=acsc[0:L, cc, g * 4:g * 4 + 4], op=ALU.subtract),
                 reads=[arw, acsc], writes=[sm4])
            S.op("act", lambda e: e.activation(out=sm4[0:L, 4:8], in_=sm4[0:L, 0:4], func=AF.Exp), reads=[sm4], writes=[sm4])
            S.op("dve", lambda e: e.tensor_tensor(out=sm4[0:L, 8:12], in0=sm4[0:L, 4:8], in1=dtt[0:L, cc, g * 4:g * 4 + 4], op=ALU.mult),
                 reads=[sm4, dtt], writes=[sm4])
            Cd3 = Cdec[:].rearrange("p (a b) -> p a b", a=4)
            S.op("dve", lambda e: e.tensor_tensor(out=Cd3[:, :, 0:L], in0=erow3[:, :, 0:L],
                                                  in1=cv[:, 3, cs].unsqueeze(1).to_broadcast([128, 4, L]), op=ALU.mult),
                 reads=[erow, ("cv", 3)], writes=[Cdec])
            psy = [P(), P()]
            tf3 = tmpf[:].rearrange("p (a b) -> p a b", a=4)
            Ld3 = Es[0][:].rearrange("p (a b) -> p a b", a=4)
            MT3 = Es[1][:].rearrange("p (a b) -> p a b", a=4)
            S.op("dve", lambda e: e.tensor_tensor(out=tf3[0:L, :, 0:L], in0=arw3[0:L, :, 0:L],
                                                  in1=acsc[0:L, cc, g * 4:g * 4 + 4].unsqueeze(2).to_broadcast([L, 4, L]), op=ALU.subtract),
                 reads=[arw, acsc], writes=[tmpf])
            S.op("dve", lambda e: e.tensor_scalar(out=tf3[0:L, :, 0:L], in0=tf3[0:L, :, 0:L], scalar1=0.0, scalar2=None, op0=ALU.min),
                 reads=[tmpf], writes=[tmpf])
            S.op("act", lambda e: e.activation(out=Ld3[0:L, :, 0:L], in_=tf3[0:L, :, 0:L], func=AF.Exp), reads=[tmpf], writes=[Es[0]])
            S.op("dve", lambda e: e.tensor_tensor(out=MT3[0:L, :, 0:L], in0=Ld3[0:L, :, 0:L],
                                                  in1=cbm[0:L, 0:L].unsqueeze(1).to_broadcast([L, 4, L]), op=ALU.mult),
                 reads=[Es[0], cbm], writes=[Es[1]])
            xs3 = xs_tok[:].rearrange("p (a b) -> p a b", a=4)
            S.op("dve", lambda e: e.tensor_tensor(out=xdt[:].rearrange("p (a b) -> p a b", a=4)[0:L], in0=xs3[0:L],
                                                  in1=dtt[0:L, cc, g * 4:g * 4 + 4].unsqueeze(2).to_broadcast([L, 4, 64]), op=ALU.mult),
                 reads=[xs_tok, dtt], writes=[xdt])
            S.op("dve", lambda e: e.tensor_tensor(out=xdtw[:].rearrange("p (a b) -> p a b", a=4)[0:L], in0=xs3[0:L],
                                                  in1=sm4[0:L, 8:12].unsqueeze(2).to_broadcast([L, 4, 64]), op=ALU.mult),
                 reads=[xs_tok, sm4], writes=[xdtw])
            for hh in range(4):
                m, half = hh // 2, hh % 2
                py = psy[m]
                S.op("pe", lambda e, hh=hh, half=half, py=py: e.matmul(py[half * 64:(half + 1) * 64, 0:L], xdt[0:L, hh * 64:(hh + 1) * 64],
                                                                       Es[1][0:L, hh * 128:hh * 128 + L], start=True, stop=False), reads=[xdt, Es[1]], writes=[py])
                S.op("pe", lambda e, hh=hh, half=half, py=py: e.matmul(py[half * 64:(half + 1) * 64, 0:L], sbf(hh),
                                                                       Cdec[:, hh * 128:hh * 128 + L], start=False, stop=True),
                     reads=[skeys[1], Cdec], writes=[py])
            for m in range(2):
                S.op("dve", lambda e, m=m: e.scalar_tensor_tensor(out=Yg[:, m, cs], in0=cv[:, m, cs],
                                                                   scalar=pp[:, PP_DS + g * 2 + m:PP_DS + g * 2 + m + 1], in1=psy[m][:, 0:L],
                                                                   op0=ALU.mult, op1=ALU.add), reads=[("cv", m), pp, psy[m]], writes=[Yg])
            psc = P()
            S.op("pe", lambda e: e.matmul(psc[:, 0:256], B_tok[0:L, :], xdtw[0:L, :], start=True, stop=True), reads=[B_tok, xdtw], writes=[psc])
            sfa3 = sfa.rearrange("p (a b) -> p a b", a=4)
            S.op("dve", lambda e: e.tensor_tensor(out=sfa3, in0=sfa3, in1=erow3[:, :, L - 1].unsqueeze(2).to_broadcast([128, 4, 64]), op=ALU.mult),
                 reads=[skeys[0], erow], writes=[skeys[0]])
            S.op("dve", lambda e: e.tensor_tensor(out=sfa3, in0=sfa3, in1=psc[:, 0:256].rearrange("p (a b) -> p a b", a=4), op=ALU.add),
                 reads=[skeys[0], psc], writes=[skeys[0]])
            S.op("act", lambda e: e.activation(out=sba, in_=sfa, func=AF.Copy), reads=[skeys[0]], writes=[skeys[1]])

        def dsa_prompt(st):
            S.alias("acc", ["mergedT"])
            S.op("dve", lambda e: e.memset(maskT[:, 0:4 * st + 4, :], 0.0), writes=["maskT"])
            R, lo, stp, cand, cnt, gg = [bis[:, i:i + 1] for i in range(6)]
            for qb in range(4):
                i = st * 4 + qb
                Nk = (i + 1) * 128
                qs = slice(qb * 128, (qb + 1) * 128)
                for h in range(8):
                    pair, half = h // 2, h % 2
                    hs = slice(half * 64, (half + 1) * 64)
                    for kt in range((Nk + 511) // 512):
                        n = min(512, Nk - kt * 512)
                        ps = P()
                        S.op("pe", lambda e, ps=ps, pair=pair, hs=hs, kt=kt, n=n: e.matmul(
                            ps[:, 0:n], qiT[hs, pair, qs], kiT2[hs, kt * 512:kt * 512 + n], start=True, stop=True),
                            reads=["qiT", kiT2], writes=[ps])
                        rr = rrs[(h + kt) % 2]
                        S.op("act", lambda e, ps=ps, rr=rr, n=n: e.activation(out=rr[:, 0:n], in_=ps[:, 0:n], func=AF.Relu), reads=[ps], writes=[rr])
                        if h == 0:
                            S.op("dve", lambda e, rr=rr, kt=kt, n=n: e.tensor_scalar(out=acc[:, kt * 512:kt * 512 + n], in0=rr[:, 0:n],
                                                                                     scalar1=witok[:, qb, 0:1], scalar2=None, op0=ALU.mult),
                                 reads=[rr, witok], writes=["acc"])
                        else:
                            S.op("dve", lambda e, rr=rr, kt=kt, n=n, h=h: e.scalar_tensor_tensor(
                                out=acc[:, kt * 512:kt * 512 + n], in0=rr[:, 0:n], scalar=witok[:, qb, h:h + 1],
                                in1=acc[:, kt * 512:kt * 512 + n], op0=ALU.mult, op1=ALU.add), reads=[rr, witok, "acc"], writes=["acc"])
                S.op("dve", lambda e: e.tensor_reduce(out=R, in_=acc[:, 0:Nk], axis=AX.X, op=ALU.max, apply_absolute_value=True),
                     reads=["acc"], writes=[bis])
                S.op("dve", lambda e: e.tensor_tensor(out=acc[:, i * 128:(i + 1) * 128], in0=acc[:, i * 128:(i + 1) * 128], in1=negm, op=ALU.add),
                     reads=["acc", cst], writes=["acc"])
                S.op("dve", lambda e: e.tensor_scalar(out=lo, in0=R, scalar1=-1.0, scalar2=None, op0=ALU.mult), reads=[bis], writes=[bis])
                if i >= 2:
                    for it in range(1, NIT + 1):
                        S.op("dve", lambda e, it=it: e.tensor_scalar(out=stp, in0=R, scalar1=2.0 ** (1 - it), scalar2=None, op0=ALU.mult),
                             reads=[bis], writes=[bis])
                        S.op("dve", lambda e: e.tensor_tensor(out=cand, in0=lo, in1=stp, op=ALU.add), reads=[bis], writes=[bis])
                        S.op("dve", lambda e: e.tensor_scalar(out=junk[:, 0:Nk], in0=acc[:, 0:Nk], scalar1=cand, scalar2=None,
                                                              op0=ALU.is_ge, op1=ALU.add, accum_out=cnt), reads=["acc", bis], writes=["mask01", bis])
                        S.op("dve", lambda e: e.tensor_scalar(out=gg, in0=cnt, scalar1=255.5, scalar2=None, op0=ALU.is_ge),
                             reads=[bis], writes=[bis])
                        S.op("dve", lambda e: e.scalar_tensor_tensor(out=lo, in0=gg, scalar=stp, in1=lo, op0=ALU.mult, op1=ALU.add),
                             reads=[bis], writes=[bis])
                S.op("dve", lambda e: e.tensor_scalar(out=mask01[:, 0:Nk], in0=acc[:, 0:Nk], scalar1=lo, scalar2=None, op0=ALU.is_ge),
                     reads=["acc", bis], writes=["mask01"])
                for sc0 in range(0, i + 1, 8):
                    n8 = min(8, i + 1 - sc0)
                    for j8 in range(n8):
                        S.op("pe", lambda e, j8=j8: e.transpose(PTS(j8), mask01[:, (sc0 + j8) * 128:(sc0 + j8 + 1) * 128], identb),
                             reads=["mask01", cstb], writes=[ptb])
                    S.op("act", lambda e: e.activation(out=maskT[:, sc0:sc0 + n8, qs], in_=ptb[:, 0:n8 * 128].rearrange("p (a b) -> p a b", a=n8),
                                                       func=AF.Copy), reads=[ptb], writes=["maskT"])
            nsc = 4 * st + 4
            for hp in range(8):
                for half in range(2):
                    h = HPERM[2 * hp + half]
                    g = h // 4
                    hs = slice(half * 64, (half + 1) * 64)
                    for sc in range(nsc):
                        ps = P()
                        S.op("pe", lambda e, ps=ps, hs=hs, g=g, sc=sc: e.matmul(ps[:, :ST], kT2[hs, g // 2, sc * 128:(sc + 1) * 128], qT[hs, hp, :],
                                                                                start=True, stop=True), reads=[kT2, "qT"], writes=[ps])
                        c.ei = (c.ei + 1) % 3
                        E = Es[c.ei]
                        S.op("act", lambda e, ps=ps, E=E: e.activation(out=E[:], in_=ps[:, :ST], func=AF.Exp, scale=0.125), reads=[ps], writes=[E])
                        S.op("dve", lambda e, E=E, sc=sc: e.tensor_tensor(out=E[:], in0=E[:], in1=maskT[:, sc, :], op=ALU.mult),
                             reads=[E, "maskT"], writes=[E])
                        S.op("pe", lambda e, E=E, hs=hs, g=g, sc=sc: e.matmul(accA[hs, :ST], vtok[:, sc, g * 64:(g + 1) * 64], E[:],
                                                                              start=(sc == 0), stop=(sc == nsc - 1)), reads=[vtok, E], writes=[accA])
                        S.op("pe", lambda e, E=E, hs=hs, sc=sc: e.matmul(accB[hs, :ST], onesb[:, 0:64], E[:],
                                                                         start=(sc == 0), stop=(sc == nsc - 1)), reads=[cstb, E], writes=[accB])
                S.op("dve", lambda e: e.reciprocal(out=rden[:], in_=accB[:, :ST]), reads=[accB], writes=[rden])
                S.op("dve", lambda e, hp=hp: e.tensor_tensor(out=yattT[:, hp, :], in0=accA[:, :ST], in1=rden[:], op=ALU.mult),
                     reads=[accA, rden], writes=["yattT"])

        def merge(T):
            win = W["w_in"]
            S.alias("mergedT", ["acc"])
            for k in range(KC):
                def gate(c0, sg):
                    buf, view = wload(win, 0, KC, c0 + k * 128, 128)
                    ps = P()
                    for kk in range(KC):
                        S.op("pe", lambda e, kk=kk, ps=ps, view=view: e.matmul(ps[:, :T], view[:, kk, :], uT[:, kk, :T], start=(kk == 0),
                                                                              stop=(kk == KC - 1)), reads=[buf, uT], writes=[ps])
                    S.op("act", lambda e, ps=ps: e.activation(out=sg[:, :T], in_=ps[:, :T], func=AF.Sigmoid), reads=[ps], writes=[sg])
                gate(O_GS, sgs[0])
                buf, view = wload(W["w_br_ssd"], 0, 16, k * 128, 128)
                ps1 = P()
                for kk in range(16):
                    S.op("pe", lambda e, kk=kk, view=view: e.matmul(ps1[:, :T], view[:, kk, :], yssdT[:, kk, :T], start=(kk == 0), stop=(kk == 15)),
                         reads=[buf, "yssdT"], writes=[ps1])
                S.op("dve", lambda e: e.tensor_tensor(out=tmpf[:, :T], in0=ps1[:, :T], in1=sgs[0][:, :T], op=ALU.mult),
                     reads=[ps1, sgs[0]], writes=[tmpf])
                gate(O_GA, sgs[1])
                buf2, view2 = wload(W["w_br_att"], 0, KC, k * 128, 128)
                ps2 = P()
                for kk in range(KC):
                    S.op("pe", lambda e, kk=kk, view2=view2: e.matmul(ps2[:, :T], view2[:, kk, :], yattT[:, kk, :T], start=(kk == 0), stop=(kk == KC - 1)),
                         reads=[buf2, "yattT"], writes=[ps2])
                S.op("dve", lambda e: e.tensor_tensor(out=sqb[:, :T], in0=ps2[:, :T], in1=sgs[1][:, :T], op=ALU.mult),
                     reads=[ps2, sgs[1]], writes=[sqb])
                S.op("dve", lambda e, k=k: e.tensor_tensor(out=mergedT[:, k, :T], in0=tmpf[:, :T], in1=sqb[:, :T], op=ALU.add),
                     reads=[tmpf, sqb], writes=["mergedT"])
            S.alias("ybuf", SCRK)
            proj_fm(W["w_out"], 0, KC, 0, D, lambda k: mergedT[:, k, :T], ["mergedT"], T,
                    lambda idx, ps, ms: S.op("act", lambda e: e.activation(out=ybuf[:, idx, :T], in_=ps[:, :T], func=AF.Copy), reads=[ps], writes=["ybuf"]))
            postnorm_add("mix_post_g", T, 1.0)

        def mem_attn(T, batches):
            prenorm("mem_pre_g", T)
            proj_fm(W["w_mq"], 0, KC, 0, D, lambda k: uT[:, k, :T], [uT], T,
                    lambda idx, ps, ms: copy_alt(idx, qT[:, idx, :T], ps[:, :T], [ps], ["qT"]))
            for cs, bsel in batches:
                nT = cs.stop - cs.start
                if bsel is not None:
                    S.dma("mkl", lambda e: e.dma_start(out=mkTb[:], in_=cmkT_d[bsel].rearrange("(k p) m -> p k m", p=128)), writes=[mkTb], queue="pool")
                    S.dma("mvl", lambda e: e.dma_start(out=mvb[:], in_=cmv_d[bsel].rearrange("(k p) c -> p k c", p=128)), writes=[mvb], queue="pool")
                for h in range(4):
                    Em = []
                    for mc in range(2):
                        ps = P()
                        for dc in range(2):
                            S.op("pe", lambda e, ps=ps: e.matmul(ps[:, :nT], mkTb[:, h * 2 + dc, mc * 128:(mc + 1) * 128], qT[:, h * 2 + dc, cs],
                                                                 start=(dc == 0), stop=(dc == 1)), reads=[mkTb, "qT"], writes=[ps])
                        E = Es[mc]
                        S.op("act", lambda e, ps=ps, E=E: e.activation(out=E[:, :nT], in_=ps[:, :nT], func=AF.Exp, scale=1.0 / 16.0), reads=[ps], writes=[E])
                        Em.append(E)
                    for mc in range(2):
                        S.op("pe", lambda e: e.matmul(accB[:, :nT], onesb, Em[mc][:, :nT], start=(mc == 0), stop=(mc == 1)), reads=[cstb, Em[mc]], writes=[accB])
                    S.op("dve", lambda e: e.reciprocal(out=rden[:, :nT], in_=accB[:, :nT]), reads=[accB], writes=[rden])
                    for dc in range(2):
                        for mc in range(2):
                            S.op("pe", lambda e: e.matmul(accA[:, :nT], mvb[:, mc, h * 256 + dc * 128:h * 256 + (dc + 1) * 128], Em[mc][:, :nT],
                                                          start=(mc == 0), stop=(mc == 1)), reads=[mvb, Em[mc]], writes=[accA])
                        S.op("dve", lambda e: e.tensor_tensor(out=yattT[:, h * 2 + dc, cs], in0=accA[:, :nT], in1=rden[:, :nT], op=ALU.mult),
                             reads=[accA, rden], writes=["yattT"])
            proj_fm(W["w_mo"], 0, KC, 0, D, lambda k: yattT[:, k, :T], ["yattT"], T,
                    lambda idx, ps, ms: S.op("act", lambda e: e.activation(out=ybuf[:, idx, :T], in_=ps[:, :T], func=AF.Copy), reads=[ps], writes=["ybuf"]))
            postnorm_add("mem_post_g", T, 1.0)

        def memory_kv_prompt():
            S.dma("xin", lambda e: e.dma_start(out=hT[:, :, 0:256], in_=memT.rearrange("(k p) t -> p k t", p=128)), writes=[hT])
            prenorm("mem_kv_g", 256)

            def cons(idx, ps, ms):
                S.op("dve", lambda e: e.tensor_copy(out=mkTb[:, idx, :], in_=ps[:, 0:256]), reads=[ps], writes=[mkTb])
                stage_out(ps[:, 0:256], 128, 256, mkT_o[idx * 128:(idx + 1) * 128, :], (ps, "mkT_o"))
            proj_fm(W["w_mk"], 0, KC, 0, D, lambda k: uT[:, k, 0:256], [uT], 256, cons)
            for cb in range(2):
                buf, view = wload(W["w_mv"], 0, KC, cb * 512, 512)
                for mc in range(2):
                    ps = P()
                    tm_proj(view, buf, 0, 512, mc, ps)
                    S.op("dve", lambda e, mc=mc, cb=cb, ps=ps: e.tensor_copy(out=mvb[:, mc, cb * 512:(cb + 1) * 512], in_=ps[:, :]), reads=[ps], writes=[mvb])
                    stage_out(ps[:, :], 128, 512, mv_o[mc * 128:(mc + 1) * 128, cb * 512:(cb + 1) * 512], (ps, "mv_o"))


        def sample_path():
            T = 16
            cst2 = S.sb([128, C2N], F32, "cst2")
            S.dma("cst2", lambda e: e.dma_start(out=cst2[:], in_=cst2_d), writes=[cst2])
            Gblk = cst2[:, 0:128]
            Gsel = cst2[:, 128:132]
            negnew = cst2[:, 132:136]
            pmod64 = cst2[:, 136:137]
            Rep = cst2[0:4, 137:265]
            Dsel = cst2[0:32, 265:269]
            hsel = cst2[0:8, 269:301]
            onesf = cst[:, 384:512]
            kibs = [kT2[:].rearrange("p a b -> p (a b)")[:, i * 2048:(i + 1) * 2048].bitcast(F32) for i in range(2)]
            vt_f = vtok[:].rearrange("p a b -> p (a b)").bitcast(F32)
            Ksel = vt_f[:, 0:1024].rearrange("p (c d) -> p c d", c=4)
            Vsel = vt_f[:, 1024:2048].rearrange("p (c d) -> p c d", c=4)
            qrow = kiT2[:].bitcast(F32)
            prodb = mask01t[:].bitcast(F32)
            kv4 = stT[:].rearrange("p g c -> p (g c)")[0:4, :].rearrange("p (b c) -> p b c", b=4)
            for nk_, ok_ in (("kib0", kT2), ("kib1", kT2), ("KV", vtok), ("qrow", kiT2), ("prodb", mask01t), ("kv4", stT)):
                S.alias(nk_, [ok_])
            hists = S.sb([128, 32, 4, 3], F32, "hists")
            S.dma("hists", lambda e: e.dma_start(out=hists[:].rearrange("p a b c -> p (a b c)"), in_=hists_d), writes=[hists])
            pres = S.sb([128, 4, 4, 7], F32, "pres")
            stS = [S.sb([128, 256], F32, f"stS{i}") for i in range(2)]
            stSb = [S.sb([128, 256], BF16, f"stSb{i}") for i in range(2)]
            sc = S.sb([128, 516], F32, "sc")
            work = S.sb([128, 512], F32, "work")
            qiU = S.sb([128, 4, 32], F32, "qiU")
            qiE = S.sb([128, 4, 32], F32, "qiE")
            qiO = S.sb([128, 4, 32], F32, "qiO")
            kiTs = S.sb([64, 16], F32, "kiTs")
            wiT = S.sb([8, 16], F32, "wiT")
            WW = S.sb([32, 252], F32, "WW")
            ptrow = S.sb([128, 128], I32, "ptrow")
            ptf = S.sb([128, 128], F32, "ptf")
            kix = S.sb([128, 128], I32, "kix")
            ptji = S.sb([128, 4], I32, "ptji")
            smf = S.sb([128, 256], F32, "smf")
            smi = S.sb([128, 96], I32, "smi")
            rowi = S.sb([128, 32], I32, "rowi")
            opart = S.sb([128, 1024], F32, "opart")
            q4 = opart[0:4, :]
            sE = S.sb([128, 64], F32, "sE")
            vals = smf[:, 0:32]
            valid = smf[:, 32:64]
            validn = smf[:, 64:68]
            wicol = smf[0:32, 68:69]
            rmax = smf[:, 69:70]
            Rr, lo, stp, cand, cnt, gg = [smf[:, 70 + i:71 + i] for i in range(6)]
            ptjf = smf[:, 76:80]
            pglf = smf[:, 80:112]
            offf = smf[:, 112:144]
            physf = smf[:, 144:176]
            tmp32 = smf[:, 176:208]
            denp = smf[:, 208:224]
            tmp16 = smf[:, 224:240]
            rd4 = smf[:, 240:244]
            rn4 = smf[0:32, 244:248]
            S.op("dve", lambda e: e.memset(WW[:], 0.0), writes=[WW])

            S.dma("xin", lambda e: e.dma_start(out=hT[:, :, 0:T], in_=xsT.rearrange("(k p) t -> p k t", p=128)), writes=[hT])
            ffn("ffn1", T)
            for k_ in MIXK:
                S.alias(k_, ["hid"])
            for k_ in SCRK:
                S.alias(k_, ["ybuf"])
            S.alias("ssdtmp", ["maskT"])
            prenorm("mix_pre_g", T)
            win = W["w_in"]
            buf, view = wload(win, 0, KC, O_K, 512)
            for b in range(4):
                ps = P()
                for k in range(KC):
                    S.op("pe", lambda e, k=k, ps=ps, view=view: e.matmul(ps[0:4, 0:512], uT[:, k, b * 4:(b + 1) * 4], view[:, k, :],
                                                                        start=(k == 0), stop=(k == KC - 1)), reads=[buf, uT], writes=[ps])
                S.op("act", lambda e, ps=ps: e.activation(out=kv4[:, b, :], in_=ps[0:4, 0:512], func=AF.Copy), reads=[ps], writes=["kv4"])
                S.dma("kso", lambda e: e.dma_start(out=ks_o[b * 4:(b + 1) * 4, :], in_=kv4[:, b, 0:256]), reads=["kv4"], writes=["ks_o"])
                S.dma("vso", lambda e: e.dma_start(out=vs_o[b * 4:(b + 1) * 4, :], in_=kv4[:, b, 256:512]), reads=["kv4"], writes=["vs_o"])
            buf, view = wload(win, 0, KC, O_KI - 32, 128)
            ps = P()
            for k in range(KC):
                S.op("pe", lambda e, k=k, ps=ps, view=view: e.matmul(ps[0:64, 0:T], view[:, k, 32:96], uT[:, k, :T], start=(k == 0), stop=(k == KC - 1)),
                     reads=[buf, uT], writes=[ps])
            S.op("act", lambda e, ps=ps: e.activation(out=kiTs[:], in_=ps[0:64, 0:T], func=AF.Copy), reads=[ps], writes=[kiTs])
            S.dma("kiso", lambda e: e.dma_start(out=kisT_o, in_=kiTs[:]), reads=[kiTs], writes=["kisT_o"])
            ps = P()
            for k in range(KC):
                S.op("pe", lambda e, k=k, ps=ps, view=view: e.matmul(ps[0:8, 0:T], view[:, k, 96:104], uT[:, k, :T], start=(k == 0), stop=(k == KC - 1)),
                     reads=[buf, uT], writes=[ps])
            S.op("act", lambda e, ps=ps: e.activation(out=wiT[:], in_=ps[0:8, 0:T], func=AF.Copy), reads=[ps], writes=[wiT])
            buf, view = wload(win, 0, KC, O_DT, 128)
            for b in range(4):
                ps = P()
                for k in range(KC):
                    S.op("pe", lambda e, k=k, ps=ps, view=view: e.matmul(ps[0:4, 0:32], uT[:, k, b * 4:(b + 1) * 4], view[:, k, 0:32],
                                                                        start=(k == 0), stop=(k == KC - 1)), reads=[buf, uT], writes=[ps])
                S.op("dve", lambda e, ps=ps: e.tensor_tensor(out=dtt[0:4, b, :], in0=ps[0:4, 0:32], in1=rowp[0:4, 0:32], op=ALU.add),
                     reads=[ps, rowp], writes=[dtt])
            S.op("act", lambda e: e.activation(out=dtt[0:4], in_=dtt[0:4], func=AF.Exp), reads=[dtt], writes=[dtt])
            S.op("act", lambda e: e.activation(out=dtt[0:4], in_=dtt[0:4], func=AF.Ln, bias=1.0), reads=[dtt], writes=[dtt])
            for b in range(4):
                S.op("dve", lambda e: e.tensor_tensor(out=dta[0:4, b, :], in0=dtt[0:4, b, :], in1=arow[0:4, :], op=ALU.mult),
                     reads=[dtt, arow], writes=[dta])
                ps = P()
                S.op("pe", lambda e, ps=ps: e.matmul(ps[0:4, 0:32], Uf[0:4, 0:4], dta[0:4, b, :], start=True, stop=True), reads=[cst, dta], writes=[ps])
                S.op("act", lambda e, ps=ps: e.activation(out=acsc[0:4, b, :], in_=ps[0:4, 0:32], func=AF.Copy), reads=[ps], writes=[acsc])
            buf, view = wload(win, 0, KC, O_QI, 512)
            ps = P()
            for h in range(8):
                for half in range(2):
                    for k in range(KC):
                        S.op("pe", lambda e, k=k, ps=ps, view=view: e.matmul(ps[half * 64:(half + 1) * 64, h * 16:(h + 1) * 16], view[:, k, h * 64:(h + 1) * 64],
                                                                            uT[:, k, :T], start=(k == 0), stop=(k == KC - 1)), reads=[buf, uT], writes=[ps])
            S.op("act", lambda e, ps=ps: e.activation(out=qiU[:].rearrange("p b (h t) -> p h b t", h=8), in_=ps[:, 0:128].rearrange("p (h b t) -> p h b t", h=8, b=4),
                                                    func=AF.Copy), reads=[ps], writes=[qiU])
            S.op("dve", lambda e: e.memset(qiE[:], 0.0), writes=[qiE])
            S.op("dve", lambda e: e.memset(qiO[:], 0.0), writes=[qiO])
            S.op("dve", lambda e: e.tensor_copy(out=qiE[0:64], in_=qiU[0:64]), reads=[qiU], writes=[qiE])
            S.op("dve", lambda e: e.tensor_copy(out=qiO[64:128], in_=qiU[64:128]), reads=[qiU], writes=[qiO])
            for g in range(8):
                buf, view = wload(win, 0, KC, O_Z + g * 256, 256)
                for m in range(2):
                    ps = P()
                    for k in range(KC):
                        S.op("pe", lambda e, k=k, ps=ps, view=view: e.matmul(ps[:, :T], view[:, k, m * 128:(m + 1) * 128], uT[:, k, :T],
                                                                            start=(k == 0), stop=(k == KC - 1)), reads=[buf, uT], writes=[ps])
                    S.op("act", lambda e, ps=ps: e.activation(out=zs[:, m, :T], in_=ps[:, :T], func=AF.Silu), reads=[ps], writes=["zs"])
                chs = [g * 2, g * 2 + 1, 16 + g, 24 + g]
                for j in range(4):
                    if j == 0:
                        bufx, viewx = wload(win, 0, KC, O_X + g * 256, 256)
                    if j == 2:
                        bufx, viewx = wload(win, 0, KC, O_B + g * 128, 128)
                    if j == 3:
                        bufx, viewx = wload(win, 0, KC, O_C + g * 128, 128)
                    c0 = 128 if j == 1 else 0
                    ps = P()
                    for k in range(KC):
                        S.op("pe", lambda e, k=k, ps=ps, viewx=viewx: e.matmul(ps[:, :T], viewx[:, k, c0:c0 + 128], uT[:, k, :T],
                                                                              start=(k == 0), stop=(k == KC - 1)), reads=[bufx, uT], writes=[ps])
                    ch = chs[j]
                    S.op("dve", lambda e: e.tensor_copy(out=pres[:, j, :, 0:3], in_=hists[:, ch, :, :]), reads=[hists], writes=[pres])
                    S.op("act", lambda e, ps=ps: e.activation(out=pres[:, j, :, 3:7], in_=ps[:, 0:T].rearrange("p (b t) -> p b t", b=4), func=AF.Copy),
                         reads=[ps], writes=[pres])
                    S.op("dve", lambda e: e.tensor_copy(out=hists[:, ch, :, :], in_=pres[:, j, :, 4:7]), reads=[pres], writes=[hists])
                    cwf = lambda tap: pp[:, PP_CW + ch * 4 + tap:PP_CW + ch * 4 + tap + 1]
                    tv = tmpf[:, 0:T].rearrange("p (b t) -> p b t", b=4)
                    S.op("dve", lambda e: e.tensor_scalar(out=tv, in0=pres[:, j, :, 0:4], scalar1=cwf(0), scalar2=pp[:, PP_CB + ch:PP_CB + ch + 1],
                                                          op0=ALU.mult, op1=ALU.add), reads=[pres, pp], writes=[tmpf])
                    for tap in range(1, 4):
                        S.op("dve", lambda e: e.scalar_tensor_tensor(out=tv, in0=pres[:, j, :, tap:tap + 4], scalar=cwf(tap), in1=tv,
                                                                     op0=ALU.mult, op1=ALU.add), reads=[pres, pp, tmpf], writes=[tmpf])
                    S.op("act", lambda e: e.activation(out=cv[:, j, :T], in_=tmpf[:, :T], func=AF.Silu), reads=[tmpf], writes=[("cv", j)])
                for b in range(4):
                    si = (g * 4 + b) % 2
                    sF, sB = stS[si], stSb[si]
                    S.dma(("sts", si), lambda e: e.dma_start(out=sF[:], in_=ssmsT_d[b, :, g * 256:(g + 1) * 256]), writes=[sF])
                    S.op("act", lambda e: e.activation(out=sB[:], in_=sF[:], func=AF.Copy), reads=[sF], writes=[sB])
                    ssd_chunk(g, b, 4, b * 4, sf=lambda hh, sF=sF: sF[:, hh * 64:(hh + 1) * 64], sbf=lambda hh, sB=sB: sB[:, hh * 64:(hh + 1) * 64],
                              skeys=(sF, sB), sfa=sF[:], sba=sB[:])
                    S.dma(("sto", si), lambda e: e.dma_start(out=ssms_o[b, :, g * 256:(g + 1) * 256], in_=sF[:]), reads=[sF], writes=["ssms_o"])
                for m in range(2):
                    S.op("dve", lambda e: e.tensor_tensor(out=Yg[:, m, :T], in0=Yg[:, m, :T], in1=zs[:, m, :T], op=ALU.mult),
                         reads=[Yg, "zs"], writes=[Yg])
                norm_stats([Yg[:, 0, :T], Yg[:, 1, :T]], Yg, T, 256.0)
                for m in range(2):
                    S.op("dve", lambda e: e.scalar_tensor_tensor(out=yssdT[:, g * 2 + m, :T], in0=Yg[:, m, :T],
                                                                 scalar=pp[:, PP_NG + g * 2 + m:PP_NG + g * 2 + m + 1], in1=rstd[:, :T],
                                                                 op0=ALU.mult, op1=ALU.mult), reads=[Yg, rstd, pp], writes=["yssdT"])
            S.dma("convso", lambda e: e.dma_start(out=convs_o, in_=hists[:].rearrange("p a b c -> p (a b c)")), reads=[hists], writes=["convs_o"])
            for b in range(4):
                cs = slice(b * 4, b * 4 + 4)
                S.dma("ptrow", lambda e: e.dma_start(out=ptrow[:], in_=ptab_d[b:b + 1, :].partition_broadcast(128)), writes=[ptrow])
                S.op("dve", lambda e: e.tensor_copy(out=ptf[:], in_=ptrow[:]), reads=[ptrow], writes=[ptf])
                t64 = work[:, 0:64]
                S.op("dve", lambda e: e.tensor_tensor(out=t64, in0=ptf[:, 1:128:2], in1=ptf[:, 0:127:2], op=ALU.subtract), reads=[ptf], writes=[work])
                S.op("dve", lambda e: e.scalar_tensor_tensor(out=t64, in0=t64, scalar=cst2[:, 941:942], in1=ptf[:, 0:127:2], op0=ALU.mult, op1=ALU.add),
                     reads=[work, ptf, cst2], writes=[work])
                S.op("dve", lambda e: e.tensor_scalar(out=t64, in0=t64, scalar1=64.0, scalar2=pmod64, op0=ALU.mult, op1=ALU.add),
                     reads=[work, cst2], writes=[work])
                S.op("dve", lambda e: e.tensor_copy(out=kix[:, 0:64], in_=t64), reads=[work], writes=[kix])
                for par in range(2):
                    S.dma("ptji", lambda e: e.dma_start(out=ptji[par * 16:(par + 1) * 16, :], in_=ptab_d[b, :].rearrange("(i e) -> i e", e=8)[:, par:par + 7:2],
                                                        allow_slow_non_contiguous=True), writes=[ptji])
                S.op("dve", lambda e: e.tensor_copy(out=tmp16[0:32, 0:4], in_=ptji[0:32, :]), reads=[ptji], writes=[smf])
                ps = P()
                S.op("pe", lambda e, ps=ps: e.matmul(ps[:, 0:4], cst2[0:32, 813:941], tmp16[0:32, 0:4], start=True, stop=True), reads=[cst2, smf], writes=[ps])
                S.op("dve", lambda e, ps=ps: e.tensor_copy(out=ptjf, in_=ps[:, 0:4]), reads=[ps], writes=[smf])
                ps = P()
                S.op("pe", lambda e, ps=ps: e.matmul(ps[0:32, 0:4], hsel, wiT[0:8, cs], start=True, stop=True), reads=[cst2, wiT], writes=[ps])
                S.op("dve", lambda e, ps=ps: e.tensor_tensor(out=rn4, in0=ps[0:32, 0:4], in1=Dsel, op=ALU.mult), reads=[ps, cst2], writes=[smf])
                S.op("dve", lambda e: e.tensor_reduce(out=wicol, in_=rn4, axis=AX.X, op=ALU.add), reads=[smf], writes=[smf])
                S.op("dve", lambda e: e.tensor_scalar(out=WW[:, 124:128], in0=Dsel, scalar1=wicol, scalar2=None, op0=ALU.mult), reads=[smf, cst2], writes=[WW])
                qil = qiU[0:64, b, :]
                nmm = 0
                for i8 in range(16):
                    kb = kibs[i8 % 2]
                    kkey = f"kib{i8 % 2}"
                    for c4 in range(4):
                        pq = i8 * 4 + c4
                        S.dma(("kibd", i8 % 2), lambda e: e.indirect_dma_start(
                            out=kb[:, c4 * 128:(c4 + 1) * 128], out_offset=None, in_=kidxT_d,
                            in_offset=bass.IndirectOffsetOnAxis(ap=kix[:, pq:pq + 1], axis=0)), reads=[kix], writes=[kkey], queue="pool")
                    for par in range(2):
                        jv = par * 16 + i8
                        ps = P()
                        S.op("pe", lambda e, ps=ps: e.matmul(ps[0:32, 0:512], (qiE if par == 0 else qiO)[:, b, :], kb[:, 0:512], start=True, stop=True),
                             reads=[qiE, qiO, kkey], writes=[ps])
                        rr = rrs[nmm % 2]
                        S.op("act", lambda e, ps=ps: e.activation(out=rr[0:32, :], in_=ps[0:32, 0:512], func=AF.Relu), reads=[ps], writes=[rr])
                        S.op("pe", lambda e: e.matmul(accA[:, 0:512], WW[:, 124 - 4 * jv:252 - 4 * jv], rr[0:32, :], start=(nmm == 0), stop=(nmm == 31)),
                             reads=[WW, rr], writes=[accA])
                        nmm += 1
                ps = P()
                S.op("pe", lambda e, ps=ps: e.matmul(ps[0:32, 0:4], qil, kiTs[:, cs], start=True, stop=True), reads=[qiU, kiTs], writes=[ps])
                S.op("act", lambda e, ps=ps: e.activation(out=rn4, in_=ps[0:32, 0:4], func=AF.Relu), reads=[ps], writes=[smf])
                S.op("pe", lambda e: e.matmul(accB[:, 0:4], WW[:, 124:252], rn4, start=True, stop=True), reads=[WW, smf], writes=[accB])
                S.op("act", lambda e: e.activation(out=sc[:, 0:512], in_=accA[:, 0:512], func=AF.Copy), reads=[accA], writes=[sc])
                S.op("dve", lambda e: e.tensor_tensor(out=sc[:, 512:516], in0=accB[:, 0:4], in1=negnew, op=ALU.add), reads=[accB, cst2], writes=[sc])
                S.op("dve", lambda e: e.tensor_reduce(out=rmax, in_=sc[:, 0:512], axis=AX.X, op=ALU.max, apply_absolute_value=True), reads=[sc], writes=[smf])
                ps = P()
                S.op("pe", lambda e, ps=ps: e.matmul(ps[:, 0:1], onesf, rmax, start=True, stop=True), reads=[cst, smf], writes=[ps])
                S.op("dve", lambda e, ps=ps: e.tensor_copy(out=Rr, in_=ps[:, 0:1]), reads=[ps], writes=[smf])
                S.op("dve", lambda e: e.tensor_scalar(out=lo, in0=Rr, scalar1=-1.0, scalar2=None, op0=ALU.mult), reads=[smf], writes=[smf])
                for it in range(1, NITS + 1):
                    S.op("dve", lambda e: e.tensor_scalar(out=stp, in0=Rr, scalar1=2.0 ** (1 - it), scalar2=None, op0=ALU.mult), reads=[smf], writes=[smf])
                    S.op("dve", lambda e: e.tensor_tensor(out=cand, in0=lo, in1=stp, op=ALU.add), reads=[smf], writes=[smf])
                    S.op("dve", lambda e: e.tensor_scalar(out=work[:, 0:516 - 4], in0=sc[:, 0:512], scalar1=cand, scalar2=None, op0=ALU.is_ge,
                                                          op1=ALU.add, accum_out=cnt), reads=[sc, smf], writes=[work, smf])
                    S.op("dve", lambda e: e.tensor_scalar(out=tmp16[:, 0:4], in0=sc[:, 512:516], scalar1=cand, scalar2=None, op0=ALU.is_ge,
                                                          op1=ALU.add, accum_out=gg), reads=[sc, smf], writes=[smf])
                    S.op("dve", lambda e: e.tensor_tensor(out=cnt, in0=cnt, in1=gg, op=ALU.add), reads=[smf], writes=[smf])
                    ps = P()
                    S.op("pe", lambda e, ps=ps: e.matmul(ps[:, 0:1], Gblk, cnt, start=True, stop=True), reads=[cst2, smf], writes=[ps])
                    S.op("dve", lambda e, ps=ps: e.tensor_scalar(out=gg, in0=ps[:, 0:1], scalar1=255.5, scalar2=None, op0=ALU.is_ge), reads=[ps], writes=[smf])
                    S.op("dve", lambda e: e.scalar_tensor_tensor(out=lo, in0=gg, scalar=stp, in1=lo, op0=ALU.mult, op1=ALU.add), reads=[smf], writes=[smf])
                S.op("dve", lambda e: e.tensor_copy(out=work[:], in_=sc[:, 0:512]), reads=[sc], writes=[work])
                cidx = smi[:, 0:32]
                for r in range(4):
                    S.op("dve", lambda e: e.max(out=vals[:, r * 8:(r + 1) * 8], in_=work[:]), reads=[work], writes=[smf])
                    S.op("dve", lambda e: e.max_index(out=cidx[:, r * 8:(r + 1) * 8].bitcast(U32), in_max=vals[:, r * 8:(r + 1) * 8], in_values=work[:]),
                         reads=[work, smf], writes=[smi])
                    S.op("dve", lambda e: e.match_replace(out=work[:], in_to_replace=vals[:, r * 8:(r + 1) * 8], in_values=work[:], imm_value=-3e38),
                         reads=[work, smf], writes=[work])
                S.op("dve", lambda e: e.tensor_scalar(out=valid, in0=vals, scalar1=lo, scalar2=None, op0=ALU.is_ge), reads=[smf], writes=[smf])
                S.op("dve", lambda e: e.tensor_scalar(out=validn, in0=sc[:, 512:516], scalar1=lo, scalar2=None, op0=ALU.is_ge), reads=[smf, sc], writes=[smf])
                S.op("dve", lambda e: e.tensor_scalar(out=smi[:, 32:64], in0=cidx, scalar1=7, scalar2=None, op0=ALU.arith_shift_right), reads=[smi], writes=[smi])
                S.op("dve", lambda e: e.tensor_scalar(out=smi[:, 64:96], in0=cidx, scalar1=127, scalar2=None, op0=ALU.bitwise_and), reads=[smi], writes=[smi])
                S.op("dve", lambda e: e.tensor_copy(out=pglf, in_=smi[:, 32:64]), reads=[smi], writes=[smf])
                S.op("dve", lambda e: e.tensor_copy(out=offf, in_=smi[:, 64:96]), reads=[smi], writes=[smf])
                for kq in range(4):
                    dst = physf if kq == 0 else tmp32
                    S.op("dve", lambda e: e.tensor_scalar(out=dst, in0=pglf, scalar1=float(kq), scalar2=ptjf[:, PGP[kq]:PGP[kq] + 1], op0=ALU.is_equal, op1=ALU.mult),
                         reads=[smf], writes=[smf])
                    if kq > 0:
                        S.op("dve", lambda e: e.tensor_tensor(out=physf, in0=physf, in1=tmp32, op=ALU.add), reads=[smf], writes=[smf])
                S.op("dve", lambda e: e.scalar_tensor_tensor(out=physf, in0=physf, scalar=128.0, in1=offf, op0=ALU.mult, op1=ALU.add), reads=[smf], writes=[smf])
                S.op("dve", lambda e: e.tensor_copy(out=rowi[:], in_=physf), reads=[smf], writes=[rowi])
                for half in range(2):
                    buf, view = wload(win, 0, KC, O_Q + half * 512, 512)
                    ps = P()
                    for k in range(KC):
                        S.op("pe", lambda e, k=k, ps=ps, view=view: e.matmul(ps[0:4, 0:512], uT[:, k, cs], view[:, k, :], start=(k == 0), stop=(k == KC - 1)),
                             reads=[buf, uT], writes=[ps])
                    S.op("act", lambda e, ps=ps: e.activation(out=q4[:, half * 512:(half + 1) * 512], in_=ps[0:4, 0:512], func=AF.Copy), reads=[ps], writes=[opart])
                    ps = P()
                    S.op("pe", lambda e, ps=ps: e.matmul(ps[:, 0:512], Rep, q4[:, half * 512:(half + 1) * 512], start=True, stop=True), reads=[cst2, opart], writes=[ps])
                    S.op("act", lambda e, ps=ps: e.activation(out=qrow[:, half * 512:(half + 1) * 512], in_=ps[:, 0:512], func=AF.Copy, scale=0.125),
                         reads=[ps], writes=["qrow"])
                qrow3 = qrow.rearrange("p (i d) -> p i d", i=16)
                opart3 = opart[:].rearrange("p (i d) -> p i d", i=16)
                S.op("dve", lambda e: e.memset(opart[:], 0.0), writes=[opart])
                S.op("dve", lambda e: e.memset(denp, 0.0), writes=[smf])
                s_all = sE[:]
                s_all3 = sE[:].rearrange("p (c i) -> p c i", c=4)
                sT3 = sE[:].rearrange("p (c i) -> p i c", c=4)
                q4acc = work[:, 0:256].rearrange("p (r d) -> p r d", r=4)
                prodK = prodb[:, 0:1024].rearrange("p (c r d) -> p c r d", c=4, r=4)
                prodV = prodb[:, 0:1024].rearrange("p (r d c) -> p r d c", r=4, d=64)
                for rnd in range(9):
                    if rnd < 8:
                        for c4 in range(4):
                            cc_ = rnd * 4 + c4
                            S.dma("ksel", lambda e: e.indirect_dma_start(out=Ksel[:, c4, :], out_offset=None, in_=poolk_d,
                                                                         in_offset=bass.IndirectOffsetOnAxis(ap=rowi[:, cc_:cc_ + 1], axis=0)),
                                  reads=[rowi], writes=["KV"], queue="pool")
                            S.dma("vsel", lambda e: e.indirect_dma_start(out=Vsel[:, c4, :], out_offset=None, in_=poolv_d,
                                                                         in_offset=bass.IndirectOffsetOnAxis(ap=rowi[:, cc_:cc_ + 1], axis=0)),
                                  reads=[rowi], writes=["KV"], queue="pool")
                        vmask = valid[:, rnd * 4:(rnd + 1) * 4]
                    else:
                        for tq in range(4):
                            ps = P()
                            S.op("pe", lambda e, ps=ps: e.matmul(ps[:, 0:512], cst2[0:4, 301 + tq * 128:301 + (tq + 1) * 128], kv4[:, b, :], start=True, stop=True),
                                 reads=[cst2, "kv4"], writes=[ps])
                            S.op("act", lambda e, ps=ps: e.activation(out=Ksel[:, tq, :], in_=ps[:, 0:256], func=AF.Copy), reads=[ps], writes=["KV"])
                            S.op("act", lambda e, ps=ps: e.activation(out=Vsel[:, tq, :], in_=ps[:, 256:512], func=AF.Copy), reads=[ps], writes=["KV"])
                        vmask = validn
                    for g in range(4):
                        base = (g // 2) * 8 + (g % 2)
                        S.op("dve", lambda e: e.tensor_tensor(out=prodK, in0=Ksel[:, :, g * 64:(g + 1) * 64].unsqueeze(2).to_broadcast([128, 4, 4, 64]),
                                                              in1=qrow3[:, base:base + 7:2, :].unsqueeze(1).to_broadcast([128, 4, 4, 64]), op=ALU.mult),
                             reads=["KV", "qrow"], writes=["prodb"])
                        S.op("dve", lambda e: e.tensor_reduce(out=s_all3[:, :, base:base + 7:2], in_=prodK, axis=AX.X, op=ALU.add), reads=["prodb"], writes=[sE])
                    S.op("act", lambda e: e.activation(out=s_all, in_=s_all, func=AF.Exp), reads=[sE], writes=[sE])
                    S.op("dve", lambda e: e.tensor_tensor(out=s_all3, in0=s_all3, in1=vmask.unsqueeze(2).to_broadcast([128, 4, 16]), op=ALU.mult),
                         reads=[sE, smf], writes=[sE])
                    S.op("dve", lambda e: e.tensor_reduce(out=tmp16, in_=sT3, axis=AX.X, op=ALU.add), reads=[sE], writes=[smf])
                    S.op("dve", lambda e: e.tensor_tensor(out=denp, in0=denp, in1=tmp16, op=ALU.add), reads=[smf], writes=[smf])
                    for g in range(4):
                        base = (g // 2) * 8 + (g % 2)
                        S.op("dve", lambda e: e.tensor_tensor(out=prodV, in0=sT3[:, base:base + 7:2, :].unsqueeze(2).to_broadcast([128, 4, 64, 4]),
                                                              in1=Vsel[:, :, g * 64:(g + 1) * 64].rearrange("p c d -> p d c").unsqueeze(1).to_broadcast([128, 4, 64, 4]),
                                                              op=ALU.mult), reads=[sE, "KV"], writes=["prodb"])
                        S.op("dve", lambda e: e.tensor_reduce(out=q4acc, in_=prodV, axis=AX.X, op=ALU.add), reads=["prodb"], writes=[work])
                        S.op("dve", lambda e: e.tensor_tensor(out=opart3[:, base:base + 7:2, :], in0=opart3[:, base:base + 7:2, :], in1=q4acc, op=ALU.add),
                             reads=[work, opart], writes=[opart])
                ps = P()
                S.op("pe", lambda e, ps=ps: e.matmul(ps[:, 0:16], Gblk, denp, start=True, stop=True), reads=[cst2, smf], writes=[ps])
                S.op("dve", lambda e, ps=ps: e.reciprocal(out=tmp16, in_=ps[:, 0:16]), reads=[ps], writes=[smf])
                S.op("dve", lambda e: e.tensor_tensor(out=opart3, in0=opart3, in1=tmp16.unsqueeze(2).to_broadcast([128, 16, 64]), op=ALU.mult),
                     reads=[opart, smf], writes=[opart])
                for hp in range(8):
                    ps = P()
                    S.op("pe", lambda e, ps=ps: e.matmul(ps[:, 0:4], opart[:, hp * 128:(hp + 1) * 128], Gsel, start=True, stop=True), reads=[opart, cst2], writes=[ps])
                    S.op("dve", lambda e, ps=ps: e.tensor_copy(out=yattT[:, hp, cs], in_=ps[:, 0:4]), reads=[ps], writes=["yattT"])
            merge(T)
            mem_attn(T, [(slice(b * 4, b * 4 + 4), b) for b in range(4)])
            ffn("ffn2", T)
            S.dma("yout", lambda e: e.dma_start(out=ysT_o.rearrange("(k p) t -> p k t", p=128), in_=hT[:, :, 0:T]), reads=[hT], writes=["ysT_o"])

        SKIP = os.environ.get("KSKIP", "").split(",")
        if "kv" not in SKIP:
            memory_kv_prompt()
        for st in range(int(os.environ.get("KNST", NST))):
            t0 = st * ST
            S.dma("xin", lambda e, t0=t0: e.dma_start(out=hT[:], in_=xT[:, t0:t0 + ST].rearrange("(k p) t -> p k t", p=128)),
                  writes=[hT])
            ffn("ffn1", ST)
            if os.environ.get("KSTAGE", "all") != "ffn":
                mix_prompt(st)
            if os.environ.get("KSTAGE", "all") == "all":
                mem_attn(ST, [(slice(0, ST), None)])
            ffn("ffn2", ST)
            S.dma("yout", lambda e, t0=t0: e.dma_start(out=yT_o[:, t0:t0 + ST].rearrange("(k p) t -> p k t", p=128), in_=hT[:]),
                  reads=[hT], writes=["yT_o"])
        if "ssmo" not in SKIP:
          S.dma("ssmo", lambda e: e.dma_start(out=ssmT_o, in_=stT[:].rearrange("p g c -> p (g c)")), reads=[stT], writes=["ssmT_o"])
        if "convo" not in SKIP:
          S.dma("convo", lambda e: e.dma_start(out=convT_o.rearrange("(k p) j -> p k j", p=128), in_=hist[:]), reads=[hist], writes=["convT_o"])
        if "sample" not in SKIP:
            sample_path()
        S.emit()
    return nc


def _consts():
    s = np.arange(128)
    ident = np.eye(128, dtype=np.float32)
    U = (s[:, None] <= s[None, :]).astype(np.float32)
    negm = np.where(s[None, :] <= s[:, None], 0.0, -1e30).astype(np.float32)
    ones = np.ones((128, 128), np.float32)
    tri = U.copy()
    return np.concatenate([ident, U, negm, ones, tri], axis=1)


def _fm(v, n):
    return np.ascontiguousarray(np.asarray(v, np.float32).reshape(n, 128).T)


def kernel(**inp):
    inp = {k: np.asarray(v) for k, v in inp.items()}
    nc = build_nc()
    pp = np.zeros((128, PP_N), np.float32)
    for n, o in PP_G.items():
        pp[:, o:o + 8] = _fm(inp[n][0], 8)
    pp[:, PP_NG:PP_NG + 16] = _fm(inp["ssd_norm_g"][0], 16)
    pp[:, PP_CB:PP_CB + 32] = _fm(inp["conv_b"][0], 32)
    cw = inp["conv_w"][0]
    pp[:, PP_CW:PP_CW + 128] = cw.T.reshape(32, 128, 4).transpose(1, 0, 2).reshape(128, 128)
    pp[:, PP_DS:PP_DS + 16] = _fm(np.repeat(inp["d_skip"][0], 64), 16)
    rowp = np.zeros((128, 64), np.float32)
    rowp[:, 0:32] = inp["dt_bias"][0][None, :]
    rowp[:, 32:64] = inp["a_log"][0][None, :]
    cst = _consts()
    wnames = ["ffn1_wg", "ffn1_wu", "ffn1_wd", "w_in", "w_br_ssd", "w_br_att", "w_out", "w_mq", "w_mk", "w_mv", "w_mo",
              "ffn2_wg", "ffn2_wu", "ffn2_wd"]
    shared = {n: np.ascontiguousarray(inp[n][0]) for n in wnames}
    qcols = np.concatenate([np.arange(O_Q + h * 64, O_Q + (h + 1) * 64) for h in HPERM])
    w_in_l = shared["w_in"].copy()
    w_in_l[:, O_Q:O_Q + D] = shared["w_in"][:, qcols]
    shared["w_in"] = w_in_l
    shared["w_br_att"] = np.ascontiguousarray(shared["w_br_att"][qcols - O_Q, :])
    shared.update(pp=pp, rowp=rowp, cst=cst)
    p_ = np.arange(128)
    cst2 = np.zeros((128, C2N), np.float32)
    cst2[:, 0:128] = (p_[:, None] % 4 == p_[None, :] % 4)
    cst2[:, 128:132] = (p_[:, None] % 4 == np.arange(4)[None, :])
    cst2[:, 132:136] = np.where((p_[:, None] // 4 == 0) & (np.arange(4)[None, :] <= p_[:, None] % 4), 0.0, -1e30)
    cst2[:, 136] = p_ % 64
    cst2[0:4, 137:265] = (np.arange(4)[:, None] == p_[None, :] % 4)
    ht = np.arange(32)
    cst2[0:32, 265:269] = (ht[:, None] % 4 == np.arange(4)[None, :])
    cst2[0:8, 269:301] = (np.arange(8)[:, None] == ht[None, :] // 4)
    for tq in range(4):
        cst2[tq, 301 + tq * 128:301 + (tq + 1) * 128] = 1.0
    cst2[0:32, 813:941] = (ht[:, None] == p_[None, :] // 4)
    cst2[:, 941] = (p_ >= 64)
    shared["cst2"] = cst2
    have_pool = "cache_k" in inp
    if have_pool:
        shared["kidxT"] = np.ascontiguousarray(inp["cache_kidx"][0].transpose(0, 2, 1)).reshape(5120 * 64, 128)
        shared["poolk"] = inp["cache_k"][0].reshape(5120 * 128, 256)
        shared["poolv"] = inp["cache_v"][0].reshape(5120 * 128, 256)
    in_maps = []
    NCR = int(os.environ.get("KCORES", 8))
    for c in range(NCR):
        m = dict(shared)
        m["xT"] = np.ascontiguousarray(inp["x_prompt"][c].T)
        m["xsT"] = np.ascontiguousarray(inp["x_sample"][4 * c:4 * c + 4].reshape(16, D).T)
        m["memT"] = np.ascontiguousarray(inp["mem_prompt"][c].T)
        sc_ = inp["state_conv"][0, 4 * c:4 * c + 4]
        m["hists"] = np.ascontiguousarray(sc_.transpose(2, 0, 1).reshape(32, 128, 4, 3).transpose(1, 0, 2, 3)).reshape(128, 384)
        m["ssmsT"] = np.ascontiguousarray(inp["state_ssm"][0, 4 * c:4 * c + 4].reshape(4, 2048, 128).transpose(0, 2, 1))
        m["cmkT"] = np.ascontiguousarray(inp["cache_mem_k"][0, 4 * c:4 * c + 4].reshape(4, 256, D).transpose(0, 2, 1))
        m["cmv"] = np.ascontiguousarray(inp["cache_mem_v"][0, 4 * c:4 * c + 4].reshape(4, 256, D))
        m["ptab"] = np.ascontiguousarray(inp["page_table"][4 * c:4 * c + 4].astype(np.int32))
        in_maps.append(m)
    res = run_bass_kernel_spmd(nc, in_maps, core_ids=list(range(NCR)))
    if os.environ.get("KTIME"):
        print("EXEC_TIME_NS", res.exec_time_ns)
    R = res.results
    y_p = np.stack([R[c]["yT"].T for c in range(NCR)])
    y_s = np.concatenate([R[c]["ysT"].T.reshape(4, 4, D) for c in range(NCR)])
    nk_p = np.stack([R[c]["kT"].T.reshape(SEQ, 4, 64) for c in range(NCR)])[None]
    nv_p = np.stack([R[c]["v_o"].reshape(SEQ, 4, 64) for c in range(NCR)])[None]
    nki_p = np.stack([R[c]["kiT"].T for c in range(NCR)])[None]
    nssm_p = np.stack([R[c]["ssmT"].T.reshape(32, 64, 128) for c in range(NCR)])[None]
    nconv_p = np.stack([R[c]["convT"].T for c in range(NCR)])[None]
    nmk_p = np.stack([R[c]["mkT"].T.reshape(256, 4, 256) for c in range(NCR)])[None]
    nmv_p = np.stack([R[c]["mv_o"].reshape(256, 4, 256) for c in range(NCR)])[None]
    nk_s = np.concatenate([R[c]["ks_o"].reshape(4, 4, 4, 64) for c in range(NCR)])[None]
    nv_s = np.concatenate([R[c]["vs_o"].reshape(4, 4, 4, 64) for c in range(NCR)])[None]
    nki_s = np.concatenate([R[c]["kisT"].T.reshape(4, 4, 64) for c in range(NCR)])[None]
    nssm_s = np.concatenate([R[c]["ssms"].transpose(0, 2, 1).reshape(4, 32, 64, 128) for c in range(NCR)])[None]
    nconv_s = np.concatenate([R[c]["convs"].reshape(128, 32, 4, 3).transpose(2, 3, 1, 0).reshape(4, 3, 4096) for c in range(NCR)])[None]
    return (y_p, y_s, nk_p, nv_p, nki_p, nssm_p, nconv_p, nmk_p, nmv_p, nk_s, nv_s, nki_s, nssm_s, nconv_s)
```
